# Optimizing a Trainium2 kernel written in Bass

```python
import jax, jax.numpy as jnp
from jax import lax
import numpy as np

D_MODEL = 1024
BATCH = 8
SEQ = 4096
DEPTH = 2
DEC_BATCH = 2
DEC_SEQ = 16384
PAST_LEN = 128

ROPE_THETA = 10000.0
EPS = 1e-6
NEG_INF = -1e30
MLA_HEADS = 8
MLA_NOPE = 64
MLA_ROPE = 32
MLA_V = 64
Q_LORA = 384
KV_LORA = 256
MLA_WIDTH = MLA_HEADS * MLA_V
MLA_QBLOCK = 128
DIL_GROUPS = ((128, 1), (512, 4), (2048, 16))
DIL_HEADS = 8
DIL_HEAD_DIM = 64
DIL_WIDTH = DIL_HEADS * DIL_HEAD_DIM
N_BRANCH = 2
IN_SPLITS = (Q_LORA, KV_LORA, MLA_ROPE, MLA_WIDTH) + (DIL_WIDTH,) * (3 * len(DIL_GROUPS)) + (DIL_WIDTH, N_BRANCH * D_MODEL)
IN_WIDTH = sum(IN_SPLITS)

kernel_name = "hybrid_mla_dilated_encoder"


def _split_in(u):
    offs, acc = [], 0
    for w in IN_SPLITS[:-1]:
        acc += w
        offs.append(acc)
    return jnp.split(u, offs, axis=-1)


def _rmsnorm(x, g):
    xf = x.astype(jnp.float32)
    y = xf * lax.rsqrt(jnp.mean(xf * xf, axis=-1, keepdims=True) + EPS)
    return (y * g.astype(jnp.float32)).astype(x.dtype)


def _rope(x, pos):
    d = x.shape[-1]
    inv = jnp.power(jnp.float32(ROPE_THETA), -jnp.arange(0, d, 2, dtype=jnp.float32) / d)
    ang = pos[:, None] * inv[None, :]
    cos = jnp.cos(ang)[None, :, None, :]
    sin = jnp.sin(ang)[None, :, None, :]
    xf = x.astype(jnp.float32)
    x1, x2 = xf[..., : d // 2], xf[..., d // 2:]
    return jnp.concatenate([x1 * cos - x2 * sin, x1 * sin + x2 * cos], axis=-1).astype(x.dtype)


def _mla_attention(q, k, v):
    B, S, H, dq = q.shape
    scale = dq ** -0.5
    nq = S // MLA_QBLOCK
    qb = q.reshape(B, nq, MLA_QBLOCK, H, dq).transpose(1, 0, 2, 3, 4)

    def one(qblk):
        s = jnp.einsum('bqhe,bkhe->bhqk', qblk, k, preferred_element_type=jnp.float32) * scale
        p = jax.nn.softmax(s, axis=-1)
        return jnp.einsum('bhqk,bkhe->bqhe', p, v.astype(jnp.float32)).astype(q.dtype)

    o = lax.map(one, qb)
    return o.transpose(1, 0, 2, 3, 4).reshape(B, S, H, v.shape[-1])


def _dilated_group_attention(q, k, v, window, dil):
    B, S, H, dh = q.shape
    half = window // (2 * dil)
    blk = half
    span = dil * blk
    S_pad = -(-S // span) * span
    pad = S_pad - S
    L = S_pad // dil
    nb = L // blk

    def strided(t):
        t = jnp.pad(t, ((0, 0), (0, pad), (0, 0), (0, 0)))
        t = t.reshape(B, L, dil, H, dh).transpose(0, 2, 1, 3, 4)
        return t.reshape(B, dil, nb, blk, H, dh)

    def neighbours(t):
        tp = jnp.pad(t, ((0, 0), (0, 0), (1, 1), (0, 0), (0, 0), (0, 0)))
        return jnp.concatenate([tp[:, :, :-2], tp[:, :, 1:-1], tp[:, :, 2:]], axis=3)

    qs = strided(q)
    kw = neighbours(strided(k))
    vw = neighbours(strided(v))

    valid = (jnp.arange(S_pad) < S).reshape(L, dil).T.reshape(dil, nb, blk)
    vp = jnp.pad(valid, ((0, 0), (1, 1), (0, 0)))
    kvalid = jnp.concatenate([vp[:, :-2], vp[:, 1:-1], vp[:, 2:]], axis=2)
    rel = (jnp.arange(3 * blk)[None, :] - blk) - jnp.arange(blk)[:, None]
    mask = (jnp.abs(rel) <= half)[None, None] & kvalid[:, :, None, :]

    s = jnp.einsum('bdnqhe,bdnkhe->bdnhqk', qs, kw, preferred_element_type=jnp.float32) * (dh ** -0.5)
    s = jnp.where(mask[None, :, :, None], s, NEG_INF)
    m = jnp.max(s, axis=-1, keepdims=True)
    p = jnp.exp(s - m)
    l = jnp.sum(p, axis=-1, keepdims=True)
    o = jnp.einsum('bdnhqk,bdnkhe->bdnqhe', p / l, vw.astype(jnp.float32))
    lse = (m + jnp.log(l))[..., 0]

    o = o.reshape(B, dil, L, H, dh).transpose(0, 2, 1, 3, 4).reshape(B, S_pad, H, dh)[:, :S]
    lse = lse.transpose(0, 1, 2, 4, 3).reshape(B, dil, L, H).transpose(0, 2, 1, 3).reshape(B, S_pad, H)[:, :S]
    return o, lse


def _layer(x, c, pos, w_ada, b_ada, g_norm, w_in, b_gate, g_cq, w_uq, g_ckv, w_ukv, w_pa, w_pb, w_out):
    B, S, _ = x.shape
    mod = jax.nn.silu(c) @ w_ada + b_ada
    shift, scale, gate = jnp.split(mod, 3, axis=-1)
    h = _rmsnorm(x, g_norm) * (1.0 + scale[:, None, :]) + shift[:, None, :]

    parts = _split_in(h @ w_in)
    cq, ckv, kr, z_mla = parts[0], parts[1], parts[2], parts[3]
    z_dil, merge = parts[4 + 3 * len(DIL_GROUPS)], parts[5 + 3 * len(DIL_GROUPS)]

    q = (_rmsnorm(cq, g_cq) @ w_uq).reshape(B, S, MLA_HEADS, MLA_NOPE + MLA_ROPE)
    q = jnp.concatenate([q[..., :MLA_NOPE], _rope(q[..., MLA_NOPE:], pos)], axis=-1)
    kv = (_rmsnorm(ckv, g_ckv) @ w_ukv).reshape(B, S, MLA_HEADS, MLA_NOPE + MLA_V)
    k_pe = jnp.broadcast_to(_rope(kr.reshape(B, S, 1, MLA_ROPE), pos), (B, S, MLA_HEADS, MLA_ROPE))
    k = jnp.concatenate([kv[..., :MLA_NOPE], k_pe], axis=-1)
    v = kv[..., MLA_NOPE:]
    o_mla = _mla_attention(q, k, v).reshape(B, S, MLA_WIDTH) * jax.nn.silu(z_mla)

    outs, lses = [], []
    for gi, (win, dil) in enumerate(DIL_GROUPS):
        qg, kg, vg = [t.reshape(B, S, DIL_HEADS, DIL_HEAD_DIM) for t in parts[4 + 3 * gi: 7 + 3 * gi]]
        og, lg = _dilated_group_attention(_rope(qg, pos), _rope(kg, pos), vg, win, dil)
        outs.append(og)
        lses.append(lg)
    wts = jax.nn.softmax(jnp.stack(lses, axis=0), axis=0)
    o_dil = jnp.sum(wts[..., None] * jnp.stack(outs, axis=0), axis=0).astype(x.dtype)
    o_dil = o_dil.reshape(B, S, DIL_WIDTH) * jax.nn.silu(z_dil)

    ga, gb = jnp.split(jax.nn.sigmoid(merge + b_gate), 2, axis=-1)
    y = (ga * (o_mla @ w_pa) + gb * (o_dil @ w_pb)) @ w_out
    return x + gate[:, None, :] * y


def _trunk(x, c, w_ada, b_ada, g_norm, w_in, b_gate, g_cq, w_uq, g_ckv, w_ukv, w_pa, w_pb, w_out, g_final):
    pos = jnp.arange(x.shape[1], dtype=jnp.float32)
    for l in range(DEPTH):
        x = _layer(x, c, pos, w_ada[l], b_ada[l], g_norm[l], w_in[l], b_gate[l], g_cq[l], w_uq[l],
                   g_ckv[l], w_ukv[l], w_pa[l], w_pb[l], w_out[l])
    return _rmsnorm(x, g_final)


def setup_inputs(seed: int = 0) -> dict:
    key = jax.random.key(seed)
    ks = jax.random.split(key, 20)
    f32 = jnp.float32
    nrm = lambda k, shape, s: jax.random.normal(k, shape, f32) * s
    D = D_MODEL
    return {
        "x_prompt": nrm(ks[0], (BATCH, SEQ, D), 1.0),
        "x_sample": nrm(ks[1], (DEC_BATCH, DEC_SEQ, D), 1.0),
        "c_prompt": nrm(ks[2], (BATCH, D), 1.0),
        "c_sample": nrm(ks[3], (DEC_BATCH, D), 1.0),
        "w_ada": nrm(ks[4], (DEPTH, D, 3 * D), D ** -0.5),
        "b_ada": nrm(ks[5], (DEPTH, 3 * D), 0.02),
        "g_norm": 1.0 + nrm(ks[6], (DEPTH, D), 0.02),
        "w_in": nrm(ks[7], (DEPTH, D, IN_WIDTH), D ** -0.5),
        "b_gate": nrm(ks[8], (DEPTH, N_BRANCH * D), 0.02),
        "g_cq": 1.0 + nrm(ks[9], (DEPTH, Q_LORA), 0.02),
        "w_uq": nrm(ks[10], (DEPTH, Q_LORA, MLA_HEADS * (MLA_NOPE + MLA_ROPE)), Q_LORA ** -0.5),
        "g_ckv": 1.0 + nrm(ks[11], (DEPTH, KV_LORA), 0.02),
        "w_ukv": nrm(ks[12], (DEPTH, KV_LORA, MLA_HEADS * (MLA_NOPE + MLA_V)), KV_LORA ** -0.5),
        "w_pa": nrm(ks[13], (DEPTH, MLA_WIDTH, D), MLA_WIDTH ** -0.5),
        "w_pb": nrm(ks[14], (DEPTH, DIL_WIDTH, D), DIL_WIDTH ** -0.5),
        "w_out": nrm(ks[15], (DEPTH, D, D), D ** -0.5),
        "g_final": 1.0 + nrm(ks[16], (D,), 0.02),
    }


def reference(x_prompt, x_sample, c_prompt, c_sample, w_ada, b_ada, g_norm, w_in, b_gate, g_cq, w_uq,
              g_ckv, w_ukv, w_pa, w_pb, w_out, g_final):
    y_prompt = _trunk(x_prompt, c_prompt, w_ada, b_ada, g_norm, w_in, b_gate, g_cq, w_uq, g_ckv, w_ukv,
                      w_pa, w_pb, w_out, g_final)
    y_sample = _trunk(x_sample, c_sample, w_ada, b_ada, g_norm, w_in, b_gate, g_cq, w_uq, g_ckv, w_ukv,
                      w_pa, w_pb, w_out, g_final)
    return (y_prompt, y_sample)
```

```python
import numpy as np
import ml_dtypes
from contextlib import ExitStack
import concourse.bass as bass
import concourse.mybir as mybir
from concourse.bass_utils import run_bass_kernel_spmd

F32 = mybir.dt.float32
BF16 = mybir.dt.bfloat16
ALU = mybir.AluOpType
AF = mybir.ActivationFunctionType

NCORES = 8
D = 1024
SEG = 4096
T = 8192
TT = 512
NT = T // TT
H = 8
DEPTH = 2
EPS = 1e-6
INW = 8352
C_CQ, C_CKV, C_KR, C_ZM, C_DIL, C_ZD, C_MG = 0, 384, 640, 672, 1184, 5792, 6304
DILS = (1, 4, 16)
HW = (64, 256, 1024)
KPAD = 1024
SC_MLA = 96 ** -0.5
SC_DIL = 64 ** -0.5


class _Stop(Exception):
    pass


class Buf:
    def __init__(self, name):
        self.name = name
        self.w = {}
        self.r = {}
        self.dsem = None
        self.dcnt = 0
        self.keep = False


class Sched:
    def __init__(self, nc, es):
        self.nc = nc
        self.es = es
        self.E = {'pe': nc.tensor, 'act': nc.scalar, 'dve': nc.vector, 'pool': nc.gpsimd, 'sp': nc.sync}
        self.sem = {k: es.enter_context(nc.semaphore("sem_" + k)) for k in self.E}
        self.cnt = {k: 0 for k in self.E}
        self.waited = {k: {} for k in self.E}
        self.dsems = []
        self.dpool = []
        self.bsem = es.enter_context(nc.semaphore("sem_bar"))
        self.bcnt = 0
        self.ccsem = es.enter_context(nc.semaphore("sem_cc"))
        self.cccnt = 0
        self.nsem = 8

    def _wait(self, eng, tok):
        sem, c, owner = tok
        if eng == 'pe' and owner == 'pe':
            return
        w = self.waited[eng]
        key = id(sem)
        if w.get(key, 0) >= c:
            return
        w[key] = c
        self.E[eng].wait_ge(sem, c)

    def pre(self, eng, reads, writes):
        for b in reads:
            for t in b.w.values():
                self._wait(eng, t)
        for b in writes:
            for t in b.w.values():
                self._wait(eng, t)
            for t in b.r.values():
                self._wait(eng, t)

    def _mark(self, tok, reads, writes, accum=False):
        key = id(tok[0])
        for b in reads:
            b.r[key] = tok
        for b in writes:
            if accum:
                b.w[key] = tok
            else:
                b.w = {key: tok}
                b.r = {}

    def post(self, eng, ins, reads, writes):
        self.cnt[eng] += 1
        ins.then_inc(self.sem[eng], 1)
        self._mark((self.sem[eng], self.cnt[eng], eng), reads, writes)

    def run(self, eng, reads, writes, fn):
        self.pre(eng, reads, writes)
        ins = fn(self.E[eng])
        self.post(eng, ins, reads, writes)

    def dma(self, q, out, in_, reads, writes, sb, accum=False):
        self.pre(q, reads, writes)
        if sb.dsem is None:
            if self.dpool and not sb.keep:
                sb.dsem, sb.dcnt = self.dpool.pop()
            else:
                sb.dsem = self.es.enter_context(self.nc.semaphore("d%d" % self.nsem))
                sb.dcnt = 0
                self.nsem += 1
            self.dsems.append(sb)
        ins = self.E[q].dma_start(out=out, in_=in_)
        sb.dcnt += 16
        ins.then_inc(sb.dsem, 16)
        self._mark((sb.dsem, sb.dcnt, 'dma'), reads, writes, accum=accum)

    def collective(self, in_ap, out_ap, reads, writes):
        self.pre('pool', reads, writes)
        ins = self.nc.gpsimd.collective_compute("AllGather", ALU.bypass,
                                                replica_groups=[[0, 1, 2, 3], [4, 5, 6, 7]],
                                                ins=[in_ap], outs=[out_ap], dma_qos="P2")
        self.cccnt += 1
        ins.then_inc(self.ccsem)
        self._mark((self.ccsem, self.cccnt, 'cc'), reads, writes)

    def barrier(self, end_phase=True):
        sp = self.E['sp']
        for k in self.E:
            if k != 'sp' and self.cnt[k] > 0:
                self._wait('sp', (self.sem[k], self.cnt[k], k))
        for sb in self.dsems:
            if sb.dcnt > 0:
                self._wait('sp', (sb.dsem, sb.dcnt, 'dma'))
        if self.cccnt > 0:
            self._wait('sp', (self.ccsem, self.cccnt, 'cc'))
        self.bcnt += 1
        sp.sem_inc(self.bsem, 1)
        for k in self.E:
            if k != 'sp':
                self.E[k].wait_ge(self.bsem, self.bcnt)
        if end_phase:
            keep = []
            for sb in self.dsems:
                if sb.keep:
                    keep.append(sb)
                else:
                    self.dpool.append((sb.dsem, sb.dcnt))
                    sb.dsem = None
            self.dsems = keep


def build_program(dbg=(), stop=None):
    nc = bass.Bass("TRN2", target_bir_lowering=False)
    es = ExitStack()
    st = {'ph': None}
    build_program.last = st
    try:
        return _build(nc, es, st, dbg, stop)
    except _Stop:
        build_program.last = st
        if st['ph'] is not None:
            st['ph'].close()
        es.close()
        return nc


def _build(nc, es, st, dbg, stop):

    def din(name, shape, dt=F32):
        return nc.dram_tensor(name, list(shape), dt, kind="ExternalInput").ap()

    def dscr(name, shape, dt=BF16, out=False):
        if name in dbg:
            return nc.dram_tensor(name, list(shape), dt, kind="ExternalOutput").ap()
        return nc.dram_tensor(name, list(shape), dt).ap()

    x_in = din("x_own", [T, D])
    cT_in = din("cT", [128, 8, 2])
    w_ada = din("w_ada", [DEPTH, D, 3 * D])
    b_ada = din("b_ada", [DEPTH, 3 * D])
    g_norm = din("g_norm", [DEPTH, D])
    w_in = din("w_in", [DEPTH, D, INW])
    b_gate = din("b_gate", [DEPTH, 128, 16])
    g_cq = din("g_cq", [DEPTH, 128, 3])
    w_uq = din("w_uq", [DEPTH, 384, 768])
    g_ckv = din("g_ckv", [DEPTH, 128, 2])
    w_ukv = din("w_ukv", [DEPTH, 256, 1024])
    w_pa = din("w_pa", [DEPTH, 512, D])
    w_pb = din("w_pb", [DEPTH, 512, D])
    w_out = din("w_out", [DEPTH, D, D])
    g_final = din("g_final", [D])
    tabs = din("tabs", [NT, 4, 128, TT])
    masks_in = din("maskb", [128, 512], BF16)
    ident_in = din("ident", [128, 128], BF16)
    sel_in = din("sel", [2, 2, 128])
    flags_in = din("flags", [128, 2])
    y_out = nc.dram_tensor("y", [T, D], F32, kind="ExternalOutput").ap()

    xres = dscr("xres", [T, D], F32)
    modbc = dscr("modbc", [DEPTH, 2, 3, 128, D], F32)
    hTs = dscr("hTs", [D, T])
    qm = dscr("qm", [H * 96, T])
    km = dscr("km", [H * 96, SEG])
    vm = dscr("vm", [512, SEG])
    km_sh = dscr("km_sh", [H * 96, SEG])
    vm_sh = dscr("vm_sh", [512, SEG])
    kmg = dscr("kmg", [H, 4 * 96, SEG])
    vmg = dscr("vmg", [4, 4 * 128, SEG])
    qd = dscr("qd", [3, 512, T])
    kd = dscr("kd", [3, 512, T])
    vd = dscr("vd", [3, 512, T])
    kd_sh = [[dscr(f"kd_sh{g}_{s}", [512, HW[g]]) for s in range(2)] for g in range(3)]
    vd_sh = [[dscr(f"vd_sh{g}_{s}", [512, HW[g]]) for s in range(2)] for g in range(3)]
    kdg = [[dscr(f"kdg{g}_{s}", [4 * 512, HW[g]]) for s in range(2)] for g in range(3)]
    vdg = [[dscr(f"vdg{g}_{s}", [4 * 512, HW[g]]) for s in range(2)] for g in range(3)]
    halo_k = [[dscr(f"halo_k{g}_{s}", [512, HW[g]]) for s in range(2)] for g in range(3)]
    halo_v = [[dscr(f"halo_v{g}_{s}", [512, HW[g]]) for s in range(2)] for g in range(3)]
    zm = dscr("zm", [512, T])
    zd = dscr("zd", [512, T])
    gab = dscr("gab", [2048, T])
    om = dscr("om", [512, T])
    od = dscr("od", [512, T])

    S = Sched(nc, es)
    st['S'] = S

    def chk(tag):
        if stop == tag:
            S.barrier()
            raise _Stop()
    pid = nc.sync.partition_id()
    rank = pid % 4
    ppid = nc.gpsimd.partition_id()
    rank_l = nc.gpsimd.snap((ppid + 3) % 4, min_val=0, max_val=3)
    rank_r = nc.gpsimd.snap((ppid + 1) % 4, min_val=0, max_val=3)

    uniq = [0]

    def sb(name, shape, dt):
        uniq[0] += 1
        return es_ph.enter_context(nc.sbuf_tensor("s%d_%s" % (uniq[0], name), list(shape), dt))

    def ps(name, shape, dt=F32):
        uniq[0] += 1
        return es_ph.enter_context(nc.psum_tensor("p%d_%s" % (uniq[0], name), list(shape), dt))

    B_modbc = Buf("modbc")
    B_scr = Buf("scr")
    B_share = Buf("share")
    B_gath = Buf("gath")
    B_halo = Buf("halo")
    dummy = Buf("dummy")
    dummy.keep = True
    dummy2 = Buf("dummy2")
    dummy2.keep = True

    es_ph = es
    ident = sb("ident", [128, 128], BF16); Bc = Buf("consts"); Bc.keep = True
    onesb = sb("onesb", [128, 128], BF16)
    onesf = sb("onesf", [128, 128], F32)
    maskb = sb("maskb", [128, 512], BF16)
    sel = sb("sel", [2, 2, 128], F32)
    flags = sb("flags", [128, 2], F32)
    S.dma('sp', ident[:], ident_in[:], [], [Bc], Bc)
    S.dma('sp', maskb[:], masks_in[:], [], [Bc], Bc, accum=True)
    S.dma('sp', sel[:], sel_in[:], [], [Bc], Bc, accum=True)
    S.dma('sp', flags[:], flags_in[:], [], [Bc], Bc, accum=True)
    Bc2 = Buf("consts2")
    S.run('dve', [], [Bc2], lambda e: (e.memset(onesb[:], 1.0), e.memset(onesf[:], 1.0))[1])

    es_ph = ExitStack()
    st['ph'] = es_ph
    cT = sb("cT", [128, 8, 2], F32); B_cT = Buf("cT")
    scT = sb("scT", [128, 8, 2], F32); B_scT = Buf("scT")
    wst = [sb(f"wst{i}", [128, 8, 512], F32) for i in range(2)]; B_wst = [Buf(f"wst{i}") for i in range(2)]
    bada = sb("bada", [1, 3 * D], F32); B_bada = Buf("bada")
    gn = sb("gn", [1, D], F32); B_gn = Buf("gn")
    modsb = sb("modsb", [2, 3 * D], F32); B_modsb = Buf("modsb")
    gnbc = sb("gnbc", [128, D], F32); B_gnbc = Buf("gnbc")
    bct = [sb(f"bct{i}", [128, 512], F32) for i in range(2)]; B_bct = [Buf(f"bct{i}") for i in range(2)]
    pmod = ps("pmod", [128, 512]); B_pmod = Buf("pmod")
    pbc = ps("pbc", [128, 512]); B_pbc = Buf("pbc")

    S.dma('sp', cT[:], cT_in[:], [], [B_cT], B_cT)
    S.run('act', [B_cT], [B_scT], lambda e: e.activation(out=scT[:], in_=cT[:], func=AF.Silu))
    it = 0
    for l in range(DEPTH):
        S.dma('sp', bada[:], b_ada[l:l + 1, :], [], [B_bada], B_bada)
        S.dma('sp', gn[:], g_norm[l:l + 1, :], [], [B_gn], B_gn)
        for j in range(6):
            w = it % 2
            it += 1
            S.dma('sp', wst[w][:], w_ada[l, :, j * 512:(j + 1) * 512].rearrange("(k p) c -> p k c", p=128),
                  [], [B_wst[w]], B_wst[w])

            def f(e, w=w, j=j):
                for k in range(8):
                    e.matmul(pmod[0:2, :], lhsT=scT[:, k, :], rhs=wst[w][:, k, :], start=(k == 0), stop=False)
                return e.matmul(pmod[0:2, :], lhsT=onesf[0:1, 0:2], rhs=bada[0:1, j * 512:(j + 1) * 512],
                                start=False, stop=True)
            S.run('pe', [B_scT, B_wst[w], B_bada, Bc2], [B_pmod], f)
            S.run('act', [B_pmod], [B_modsb],
                  lambda e, j=j: e.activation(out=modsb[0:2, j * 512:(j + 1) * 512], in_=pmod[0:2, :], func=AF.Copy))
        for hf in range(2):
            S.run('pe', [B_gn, Bc2], [B_pbc],
                  lambda e, hf=hf: e.matmul(pbc[:, :], lhsT=onesf[0:1, :], rhs=gn[0:1, hf * 512:(hf + 1) * 512],
                                            start=True, stop=True))
            S.run('act', [B_pbc], [B_gnbc],
                  lambda e, hf=hf: e.activation(out=gnbc[:, hf * 512:(hf + 1) * 512], in_=pbc[:, :], func=AF.Copy))
        for b in range(2):
            for kind in range(3):
                dst = {0: 1, 1: 0, 2: 2}[kind]
                for hf in range(2):
                    w = it % 2
                    it += 1
                    S.run('pe', [B_modsb, Bc], [B_pbc],
                          lambda e, b=b, kind=kind, hf=hf: e.matmul(
                              pbc[:, :], lhsT=sel[0:2, b, :],
                              rhs=modsb[0:2, kind * D + hf * 512: kind * D + (hf + 1) * 512], start=True, stop=True))
                    if kind == 1:
                        S.run('dve', [B_pbc, B_gnbc], [B_bct[w]],
                              lambda e, w=w, hf=hf: e.scalar_tensor_tensor(
                                  out=bct[w][:], in0=pbc[:, :], scalar=1.0, in1=gnbc[:, hf * 512:(hf + 1) * 512],
                                  op0=ALU.add, op1=ALU.mult))
                    else:
                        S.run('act', [B_pbc], [B_bct[w]],
                              lambda e, w=w: e.activation(out=bct[w][:], in_=pbc[:, :], func=AF.Copy))
                    S.dma('sp', modbc[l, b, dst, :, hf * 512:(hf + 1) * 512], bct[w][:], [B_bct[w]], [B_modbc],
                          B_bct[w], accum=True)
    S.barrier()
    es_ph.close()
    if stop == "P":
        es.close()
        return nc

    for l in range(DEPTH):
        x_src = x_in if l == 0 else xres
        tile_order = list(range(8, 16)) + list(range(0, 8))

        for pa in range(2):
            es_ph = ExitStack()
            st['ph'] = es_ph
            if pa == 0:
                segs = [(0, 672), (C_KR, 32), (C_ZM, 512), (C_DIL, 1536)]
                L_CQ, L_CKV, L_KR, L_ZM, L_G = 0, 384, 640, 704, 1216
                NWC = 2752
            else:
                segs = [(C_DIL + 1536, 3072), (C_ZD, 512), (C_MG, 2048)]
                L_G, L_ZD, L_MG = 0, 3072, 3584
                NWC = 5632
            wbf = sb("wbf", [128, 8, NWC], BF16); B_wbf = Buf("wbf")
            wstg = [sb(f"wstg{i}", [128, 8, 256], F32) for i in range(2)]; B_wstg = [Buf(f"wstg{i}") for i in range(2)]
            it = 0
            lo = 0
            for (c0, ncol) in segs:
                for cc in range(0, ncol, 256):
                    n = min(256, ncol - cc)
                    w = it % 2
                    S.dma('sp', wstg[w][:, :, 0:n],
                          w_in[l, :, c0 + cc:c0 + cc + n].rearrange("(k p) c -> p k c", p=128),
                          [], [B_wstg[w]], B_wstg[w])
                    dst_lo = lo + cc
                    if pa == 0 and c0 == C_KR and ncol == 32:
                        def f(e, w=w, dst_lo=dst_lo):
                            e.tensor_copy(out=wbf[:, :, dst_lo:dst_lo + 16], in_=wstg[w][:, :, 16:32])
                            return e.tensor_copy(out=wbf[:, :, dst_lo + 16:dst_lo + 32], in_=wstg[w][:, :, 0:16])
                        S.run('dve', [B_wstg[w]], [B_wbf], f)
                    else:
                        eng = 'dve' if it % 2 == 0 else 'act'
                        if eng == 'dve':
                            S.run('dve', [B_wstg[w]], [B_wbf], lambda e, w=w, dst_lo=dst_lo, n=n: e.tensor_copy(
                                out=wbf[:, :, dst_lo:dst_lo + n], in_=wstg[w][:, :, 0:n]))
                        else:
                            S.run('act', [B_wstg[w]], [B_wbf], lambda e, w=w, dst_lo=dst_lo, n=n: e.activation(
                                out=wbf[:, :, dst_lo:dst_lo + n], in_=wstg[w][:, :, 0:n], func=AF.Copy))
                    it += 1
                lo += ncol
            assert lo == NWC
            chk("A%dw" % pa)

            tbs = [sb(f"tb{i}", [128, 4, TT], F32) for i in range(2)]; B_tbs = [Buf(f"tb{i}") for i in range(2)]
            tb = tbs[0]; B_tb = B_tbs[0]
            hT = [sb(f"hT{i}", [128, 8, TT], BF16) for i in range(2)]; B_hT = [Buf(f"hT{i}") for i in range(2)]
            pmm = [ps(f"pmm{i}", [128, 512]) for i in range(4)]; B_pmm = [Buf(f"pmm{i}") for i in range(4)]
            stg = [sb(f"stg{i}", [128, TT], BF16) for i in range(6)]; B_stg = [Buf(f"stg{i}") for i in range(6)]
            ra = [sb(f"ra{i}", [128, TT], F32) for i in range(2)]; B_ra = [Buf(f"ra{i}") for i in range(2)]
            rt = [sb(f"rt{i}", [128, TT], F32) for i in range(2)]; B_rt = [Buf(f"rt{i}") for i in range(2)]
            cnt = {'pmm': 0, 'stg': 0, 'r': 0}

            if pa == 0:
                xt = [sb(f"xt{i}", [128, D], F32) for i in range(3)]; B_xt = [Buf(f"xt{i}") for i in range(3)]
                junk = sb("junk", [128, D], F32); B_junk = Buf("junk")
                hb = [sb(f"hb{i}", [128, D], BF16) for i in range(4)]; B_hb = [Buf(f"hb{i}") for i in range(4)]
                ssq = sb("ssq", [128, 8], F32); B_ssq = Buf("ssq")
                rstd = sb("rstd", [128, 8], F32); B_rstd = Buf("rstd")
                gsb = sb("gsb", [128, D], F32); shb = sb("shb", [128, D], F32); B_mod = Buf("modt")
                ptr = ps("ptr", [128, 1024], BF16); B_ptr = Buf("ptr")
                pss = ps("pss", [128, 512]); B_pss = Buf("pss")
                cqf = sb("cqf", [128, 5, TT], F32); B_cqf = Buf("cqf")
                sqb = sb("sqb", [128, 5, TT], BF16); B_sqb = Buf("sqb")
                rsq = sb("rsq", [128, 2, TT], F32); B_rsq = Buf("rsq")
                cn = sb("cn", [128, 5, TT], BF16); B_cn = Buf("cn")
                wuq = sb("wuq", [128, 3, H * 128], BF16); B_wuq = Buf("wuq")
                wuk = sb("wuk", [128, 2, 512], BF16); wuv = sb("wuv", [128, 2, 512], BF16); B_wukv = Buf("wukv")
                gq = sb("gq", [128, 3], F32); gkv = sb("gkv", [128, 2], F32); B_g = Buf("gqkv")
                kst = sb("kst", [32, TT], BF16); B_kst = Buf("kst")
                S.dma('sp', gq[:], g_cq[l], [], [B_g], B_g)
                S.dma('sp', gkv[:], g_ckv[l], [], [B_g], B_g, accum=True)
                for k in range(3):
                    w = it % 2
                    it += 1
                    for part in range(3):
                        S.dma('sp', wstg[w][:, part, 0:256], w_uq[l, k * 128:(k + 1) * 128, part * 256:(part + 1) * 256],
                              [], [B_wstg[w]], B_wstg[w], accum=(part > 0))

                    def f(e, w=w, k=k):
                        src = wstg[w][:, 0:3, 0:256].rearrange("p a c -> p (a c)").rearrange("p (h c) -> p h c", c=96)
                        dst = wuq[:, k, :].rearrange("p (h c) -> p h c", c=128)
                        e.tensor_scalar(out=dst[:, :, 0:96], in0=src, scalar1=gq[:, k:k + 1], scalar2=None, op0=ALU.mult)
                        e.tensor_scalar(out=dst[:, :, 96:112], in0=src[:, :, 80:96], scalar1=gq[:, k:k + 1], scalar2=None,
                                        op0=ALU.mult)
                        return e.tensor_scalar(out=dst[:, :, 112:128], in0=src[:, :, 64:80], scalar1=gq[:, k:k + 1],
                                               scalar2=None, op0=ALU.mult)
                    S.run('dve', [B_wstg[w], B_g], [B_wuq], f)
                for k in range(2):
                    w = it % 2
                    it += 1
                    for part in range(4):
                        S.dma('sp', wstg[w][:, part, 0:256], w_ukv[l, k * 128:(k + 1) * 128, part * 256:(part + 1) * 256],
                              [], [B_wstg[w]], B_wstg[w], accum=(part > 0))

                    def f(e, w=w, k=k):
                        src = wstg[w][:, 0:4, 0:256].rearrange("p a (h2 c) -> p (a h2) c", c=128)
                        e.tensor_scalar(out=wuk[:, k, :].rearrange("p (h c) -> p h c", c=64), in0=src[:, :, 0:64],
                                        scalar1=gkv[:, k:k + 1], scalar2=None, op0=ALU.mult)
                        return e.tensor_scalar(out=wuv[:, k, :].rearrange("p (h c) -> p h c", c=64), in0=src[:, :, 64:128],
                                               scalar1=gkv[:, k:k + 1], scalar2=None, op0=ALU.mult)
                    S.run('dve', [B_wstg[w], B_g], [B_wukv], f)
                chk("A0u")
            else:
                bg = sb("bg", [128, 16], F32); B_bg = Buf("bg")
                S.dma('sp', bg[:], b_gate[l], [], [B_bg], B_bg)

            def next_pmm():
                i = cnt['pmm'] % 4
                cnt['pmm'] += 1
                return i

            def next_stg():
                i = cnt['stg'] % 6
                cnt['stg'] += 1
                return i

            def inproj(pi, hi, lcol, m=128):
                def f(e):
                    for k in range(8):
                        ins = e.matmul(pmm[pi][0:m, :], lhsT=wbf[:, k, lcol:lcol + m], rhs=hT[hi][:, k, :],
                                       start=(k == 0), stop=(k == 7))
                    return ins
                S.run('pe', [B_wbf, B_hT[hi]], [B_pmm[pi]], f)

            def store(si, dst_ap, rows=128, q='sp'):
                S.dma(q, dst_ap, stg[si][0:rows, :], [B_stg[si]], [B_scr], B_stg[si], accum=True)

            def rope2(pi, dst_ap):
                ri = cnt['r'] % 2
                cnt['r'] += 1
                si = next_stg()
                S.run('dve', [B_pmm[pi], B_tb], [B_ra[ri]],
                      lambda e: e.tensor_tensor(out=ra[ri][:], in0=pmm[pi][:, :], in1=tb[:, 0, :], op=ALU.mult))

                def f(e):
                    for blk in range(4):
                        src = (blk ^ 1) * 32
                        ins = e.tensor_tensor(out=rt[ri][blk * 32:(blk + 1) * 32, :], in0=pmm[pi][src:src + 32, :],
                                              in1=tb[src:src + 32, 1, :], op=ALU.mult)
                    return ins
                S.run('dve', [B_pmm[pi], B_tb], [B_rt[ri]], f)
                S.run('pool', [B_ra[ri], B_rt[ri]], [B_stg[si]],
                      lambda e: e.tensor_tensor(out=stg[si][:], in0=ra[ri][:], in1=rt[ri][:], op=ALU.add))
                store(si, dst_ap)

            def actout(pi, dst_ap, func, bias=None, q='sp'):
                si = next_stg()
                if bias is None:
                    S.run('act', [B_pmm[pi]], [B_stg[si]],
                          lambda e: e.activation(out=stg[si][:], in_=pmm[pi][:, :], func=func))
                else:
                    S.run('act', [B_pmm[pi], B_bg], [B_stg[si]],
                          lambda e: e.activation(out=stg[si][:], in_=pmm[pi][:, :], func=func, bias=bias))
                store(si, dst_ap, q=q)

            segstate = {'cur': -1}

            def prepA(ti, s4):
                tile = tile_order[ti]
                seg = tile // 8
                t0 = tile * TT
                if s4 == 0 and seg != segstate['cur']:
                    segstate['cur'] = seg
                    S.dma('sp', gsb[:], modbc[l, seg, 0], [B_modbc], [B_mod], B_mod)
                    S.dma('sp', shb[:], modbc[l, seg, 1], [B_modbc], [B_mod], B_mod, accum=True)
                xi = (ti * 4 + s4) % 3
                hbi = s4
                S.dma('sp', xt[xi][:], x_src[t0 + s4 * 128:t0 + (s4 + 1) * 128, :], [], [B_xt[xi]], B_xt[xi])
                col = (ti % 2) * 4 + s4
                S.run('act', [B_xt[xi]], [B_junk, B_ssq],
                      lambda e: e.activation(out=junk[:], in_=xt[xi][:], func=AF.Square, accum_out=ssq[:, col:col + 1]))
                S.run('dve', [B_ssq], [B_rstd], lambda e: e.tensor_scalar(
                    out=rstd[:, col:col + 1], in0=ssq[:, col:col + 1], scalar1=1.0 / D, scalar2=EPS,
                    op0=ALU.mult, op1=ALU.add))
                S.run('act', [B_rstd], [B_rstd], lambda e: e.activation(
                    out=rstd[:, col:col + 1], in_=rstd[:, col:col + 1], func=AF.Sqrt))
                S.run('dve', [B_rstd], [B_rstd], lambda e: e.reciprocal(
                    out=rstd[:, col:col + 1], in_=rstd[:, col:col + 1]))
                S.run('dve', [B_xt[xi], B_rstd, B_mod], [B_junk],
                      lambda e: e.scalar_tensor_tensor(
                          out=junk[:], in0=xt[xi][:], scalar=rstd[:, col:col + 1], in1=gsb[:],
                          op0=ALU.mult, op1=ALU.mult))
                S.run('pool', [B_junk, B_mod], [B_hb[hbi]],
                      lambda e: e.tensor_tensor(out=hb[hbi][:], in0=junk[:], in1=shb[:], op=ALU.add))

            def prepB(ti, s4):
                hi = ti % 2
                hbi = s4

                def f(e):
                    for k in range(8):
                        ins = e.transpose(ptr[:, k * 128:(k + 1) * 128], hb[hbi][:, k * 128:(k + 1) * 128], ident[:])
                    return ins
                S.run('pe', [B_hb[hbi], Bc], [B_ptr], f)
                S.run('act', [B_ptr], [B_hT[hi]],
                      lambda e: e.activation(out=hT[hi][:, :, s4 * 128:(s4 + 1) * 128],
                                             in_=ptr[:, :].rearrange("p (k t) -> p k t", t=128), func=AF.Copy))
                if s4 == 3:
                    t0 = tile_order[ti] * TT
                    S.dma('sp', hTs[:, t0:t0 + TT].rearrange("(k p) t -> p k t", p=128), hT[hi][:], [B_hT[hi]], [B_scr],
                          B_hT[hi], accum=True)

            def hook(ti, where):
                if pa != 0 or ti + 1 >= NT:
                    return
                n = ti + 1
                if where == 'start':
                    prepA(n, 0); prepA(n, 1)
                elif where == 'lat':
                    prepA(n, 2); prepA(n, 3)
                elif where == 'zm':
                    prepB(n, 0)
                elif where == 'dq':
                    prepB(n, 1)
                elif where == 'dk':
                    prepB(n, 2)
                elif where == 'dv':
                    prepB(n, 3)

            if pa == 0:
                for s4 in range(4):
                    prepA(0, s4)
                for s4 in range(4):
                    prepB(0, s4)
            else:
                S.dma('sp', hT[0][:], hTs[:, tile_order[0] * TT:tile_order[0] * TT + TT].rearrange("(k p) t -> p k t", p=128), [B_scr], [B_hT[0]], B_hT[0])
            for ti, tile in enumerate(tile_order):
                if ti > 0:
                    chk("A%dt%d" % (pa, ti - 1))
                seg = tile // 8
                t0 = tile * TT
                hi = ti % 2
                tb = tbs[ti % 2]; B_tb = B_tbs[ti % 2]
                if ti == 0:
                    S.dma('sp', tb[:], tabs[tile].rearrange("a p t -> p a t"), [], [B_tb], B_tb)
                if ti + 1 < NT:
                    S.dma('sp', tbs[(ti + 1) % 2][:], tabs[tile_order[ti + 1]].rearrange("a p t -> p a t"), [],
                          [B_tbs[(ti + 1) % 2]], B_tbs[(ti + 1) % 2])
                if pa == 0:
                    hook(ti, 'start')
                    chk("A0h")
                    for j in range(5):
                        pi = next_pmm()
                        inproj(pi, hi, L_CQ + j * 128)
                        S.run('act', [B_pmm[pi]], [B_cqf],
                              lambda e, j=j, pi=pi: e.activation(out=cqf[:, j, :], in_=pmm[pi][:, :], func=AF.Copy))
                        S.run('act', [B_pmm[pi]], [B_sqb],
                              lambda e, j=j, pi=pi: e.activation(out=sqb[:, j, :], in_=pmm[pi][:, :], func=AF.Square))
                    for which, (j0, nj, nfeat) in enumerate(((0, 3, 384), (3, 2, 256))):
                        def f(e, j0=j0, nj=nj):
                            for j in range(nj):
                                ins = e.matmul(pss[:, :], lhsT=onesb[:], rhs=sqb[:, j0 + j, :], start=(j == 0),
                                               stop=(j == nj - 1))
                            return ins
                        S.run('pe', [B_sqb, Bc2], [B_pss], f)

                        S.run('dve', [B_pss], [B_rsq], lambda e, which=which, nfeat=nfeat: e.tensor_scalar(
                            out=rsq[:, which, :], in0=pss[:, :], scalar1=1.0 / nfeat, scalar2=EPS, op0=ALU.mult, op1=ALU.add))
                        S.run('act', [B_rsq], [B_rsq], lambda e, which=which: e.activation(
                            out=rsq[:, which, :], in_=rsq[:, which, :], func=AF.Sqrt))
                        S.run('dve', [B_rsq], [B_rsq], lambda e, which=which: e.reciprocal(
                            out=rsq[:, which, :], in_=rsq[:, which, :]))
                        for j in range(nj):
                            S.run('pool' if j % 2 else 'dve', [B_cqf, B_rsq], [B_cn],
                                  lambda e, j=j, j0=j0, which=which: e.tensor_tensor(
                                      out=cn[:, j0 + j, :], in0=cqf[:, j0 + j, :], in1=rsq[:, which, :], op=ALU.mult))
                    hook(ti, 'lat')
                    chk("A0c")
                    pi = next_pmm()
                    inproj(pi, hi, L_KR, m=64)
                    ri = cnt['r'] % 2
                    cnt['r'] += 1
                    S.run('dve', [B_pmm[pi], B_tb], [B_ra[ri]],
                          lambda e: e.tensor_tensor(out=ra[ri][0:32, :], in0=pmm[pi][0:32, :], in1=tb[0:32, 2, :], op=ALU.mult))
                    S.run('dve', [B_pmm[pi], B_tb], [B_rt[ri]],
                          lambda e: e.tensor_tensor(out=rt[ri][0:32, :], in0=pmm[pi][32:64, :], in1=tb[32:64, 3, :], op=ALU.mult))

                    S.run('pool', [B_ra[ri], B_rt[ri]], [B_kst],
                          lambda e: e.tensor_tensor(out=kst[:, :], in0=ra[ri][0:32, :], in1=rt[ri][0:32, :], op=ALU.add))
                    kdst = km if seg == 0 else km_sh
                    tl = t0 - seg * SEG
                    for h in range(H):
                        S.dma('sp', kdst[h * 96 + 64:h * 96 + 96, tl:tl + TT], kst[:, :], [B_kst],
                              [B_scr if seg == 0 else B_share], B_kst, accum=True)
                    chk("A0k")
                    for j in range(4):
                        pi = next_pmm()
                        inproj(pi, hi, L_ZM + j * 128)
                        actout(pi, zm[j * 128:(j + 1) * 128, t0:t0 + TT], AF.Silu, q='act')
                    glist = (0,)
                else:
                    if ti + 1 < NT:
                        tn = tile_order[ti + 1] * TT
                        S.dma('sp', hT[1 - hi][:], hTs[:, tn:tn + TT].rearrange("(k p) t -> p k t", p=128), [B_scr],
                              [B_hT[1 - hi]], B_hT[1 - hi])
                    glist = (1, 2)
                chk("A%dkv" % pa)
                hook(ti, 'zm')
                for gi, g in enumerate(glist):
                    base = L_G + gi * 1536
                    for j in range(4):
                        pi = next_pmm()
                        inproj(pi, hi, base + j * 128)
                        rope2(pi, qd[g, j * 128:(j + 1) * 128, t0:t0 + TT])
                    if gi == 0:
                        hook(ti, 'dq')
                    for j in range(4):
                        pi = next_pmm()
                        inproj(pi, hi, base + 512 + j * 128)
                        rope2(pi, kd[g, j * 128:(j + 1) * 128, t0:t0 + TT])
                    if gi == 0:
                        hook(ti, 'dk')
                    for j in range(4):
                        pi = next_pmm()
                        inproj(pi, hi, base + 1024 + j * 128)
                        actout(pi, vd[g, j * 128:(j + 1) * 128, t0:t0 + TT], AF.Copy, q='act')
                    if gi == 0:
                        hook(ti, 'dv')
                if pa == 0:
                    chk("A0z")
                    for h in range(H):
                        pi = next_pmm()

                        def f(e, h=h, pi=pi):
                            for k in range(3):
                                ins = e.matmul(pmm[pi][:, :], lhsT=wuq[:, k, h * 128:(h + 1) * 128], rhs=cn[:, k, :],
                                               start=(k == 0), stop=(k == 2))
                            return ins
                        S.run('pe', [B_wuq, B_cn], [B_pmm[pi]], f)
                        si = next_stg()
                        ri = cnt['r'] % 2
                        cnt['r'] += 1
                        S.run('act', [B_pmm[pi]], [B_stg[si]],
                              lambda e, si=si, pi=pi: e.activation(out=stg[si][0:64, :], in_=pmm[pi][0:64, :], func=AF.Copy))
                        S.run('dve', [B_pmm[pi], B_tb], [B_ra[ri]],
                              lambda e, ri=ri, pi=pi: e.tensor_tensor(out=ra[ri][64:96, :], in0=pmm[pi][64:96, :],
                                                                      in1=tb[64:96, 2, :], op=ALU.mult))
                        S.run('dve', [B_pmm[pi], B_tb], [B_rt[ri]],
                              lambda e, ri=ri, pi=pi: e.tensor_tensor(out=rt[ri][64:96, :], in0=pmm[pi][96:128, :],
                                                                      in1=tb[96:128, 3, :], op=ALU.mult))
                        S.run('pool', [B_ra[ri], B_rt[ri]], [B_stg[si]],
                              lambda e, ri=ri, si=si: e.tensor_tensor(out=stg[si][64:96, :], in0=ra[ri][64:96, :],
                                                                      in1=rt[ri][64:96, :], op=ALU.add))
                        store(si, qm[h * 96:(h + 1) * 96, t0:t0 + TT], rows=96)
                    chk("A0q")
                    for j in range(4):
                        pi = next_pmm()

                        def f(e, j=j, pi=pi):
                            for k in range(2):
                                ins = e.matmul(pmm[pi][:, :], lhsT=wuk[:, k, j * 128:(j + 1) * 128], rhs=cn[:, 3 + k, :],
                                               start=(k == 0), stop=(k == 1))
                            return ins
                        S.run('pe', [B_wukv, B_cn], [B_pmm[pi]], f)
                        si = next_stg()
                        S.run('act', [B_pmm[pi]], [B_stg[si]],
                              lambda e, si=si, pi=pi: e.activation(out=stg[si][:], in_=pmm[pi][:, :], func=AF.Copy))
                        for hh in range(2):
                            h = 2 * j + hh
                            S.dma('sp', kdst[h * 96:h * 96 + 64, tl:tl + TT], stg[si][hh * 64:(hh + 1) * 64, :], [B_stg[si]],
                                  [B_scr if seg == 0 else B_share], B_stg[si], accum=True)
                    vdst = vm if seg == 0 else vm_sh
                    for j in range(4):
                        pi = next_pmm()

                        def f(e, j=j, pi=pi):
                            for k in range(2):
                                ins = e.matmul(pmm[pi][:, :], lhsT=wuv[:, k, j * 128:(j + 1) * 128], rhs=cn[:, 3 + k, :],
                                               start=(k == 0), stop=(k == 1))
                            return ins
                        S.run('pe', [B_wukv, B_cn], [B_pmm[pi]], f)
                        si = next_stg()
                        S.run('act', [B_pmm[pi]], [B_stg[si]],
                              lambda e, si=si, pi=pi: e.activation(out=stg[si][:], in_=pmm[pi][:, :], func=AF.Copy))
                        S.dma('act', vdst[j * 128:(j + 1) * 128, tl:tl + TT], stg[si][:], [B_stg[si]],
                              [B_scr if seg == 0 else B_share], B_stg[si], accum=True)
                chk("A%dd" % pa)
                if pa == 0 and ti == 7:
                    for h in range(H):
                        S.collective(km_sh[h * 96:(h + 1) * 96, :], kmg[h], [B_share], [B_gath])
                    for j in range(4):
                        S.collective(vm_sh[j * 128:(j + 1) * 128, :], vmg[j], [B_share], [B_gath])
                if pa == 1:
                    for j in range(4):
                        pi = next_pmm()
                        inproj(pi, hi, L_ZD + j * 128)
                        actout(pi, zd[j * 128:(j + 1) * 128, t0:t0 + TT], AF.Silu, q='act')
                    for j in range(16):
                        pi = next_pmm()
                        inproj(pi, hi, L_MG + j * 128)
                        actout(pi, gab[j * 128:(j + 1) * 128, t0:t0 + TT], AF.Sigmoid, bias=bg[:, j:j + 1])
                    if ti == 7:
                        chk("A1pre")
                        for g in range(3):
                            for s in range(2):
                                c0 = SEG if s == 0 else T - HW[g]
                                S.dma('sp', kd_sh[g][s][:, :], kd[g, :, c0:c0 + HW[g]], [B_scr], [B_share], dummy2, accum=True)
                                S.dma('sp', vd_sh[g][s][:, :], vd[g, :, c0:c0 + HW[g]], [B_scr], [B_share], dummy2, accum=True)
                        for g in range(3):
                            for s in range(2):
                                S.collective(kd_sh[g][s], kdg[g][s], [B_share], [B_gath])
                                S.collective(vd_sh[g][s], vdg[g][s], [B_share], [B_gath])
            if pa == 1:
                for g in range(3):
                    for (srcs, dsts) in ((kdg, halo_k), (vdg, halo_v)):
                        S.dma('pool', dsts[g][0][:, :], srcs[g][1].rearrange("(a p) f -> a p f", a=4)[
                            bass.ds(rank_l, 1), :, :].rearrange("a p f -> (a p) f"), [B_gath], [B_halo], dummy, accum=True)
                        S.dma('pool', dsts[g][1][:, :], srcs[g][0].rearrange("(a p) f -> a p f", a=4)[
                            bass.ds(rank_r, 1), :, :].rearrange("a p f -> (a p) f"), [B_gath], [B_halo], dummy, accum=True)
            S.barrier()
            es_ph.close()
            if l == 0 and stop == "A%d" % pa:
                es.close()
                return nc

        es_ph = ExitStack()
        st['ph'] = es_ph
        kt = [sb(f"kt{i}", [96, 4 * SEG], BF16) for i in range(2)]; B_kt = [Buf(f"kt{i}") for i in range(2)]
        vT1 = sb("vT0", [64, 4 * SEG], BF16); B_vT1 = Buf("vT0")
        vt = [sb(f"vt{i}", [128, 128, 65], BF16) for i in range(2)]; B_vt = [Buf(f"vt{i}") for i in range(2)]
        qt = [sb(f"qt{i}", [96, SEG], BF16) for i in range(2)]; B_qt = [Buf(f"qt{i}") for i in range(2)]
        zt = [sb(f"zt{i}", [64, SEG], BF16) for i in range(2)]; B_zt = [Buf(f"zt{i}") for i in range(2)]
        ot = [sb(f"ot{i}", [64, SEG], BF16) for i in range(2)]; B_ot = [Buf(f"ot{i}") for i in range(2)]
        pt = [sb(f"pt{i}", [128, 1024], BF16) for i in range(2)]; B_pt = [Buf(f"pt{i}") for i in range(2)]
        rd = sb("rd", [128, 512], F32); B_rd = Buf("rd")
        tmpo = sb("tmpo", [64, 512], F32); B_tmpo = Buf("tmpo")
        pS = [ps(f"pS{i}", [128, 1024]) for i in range(2)]; B_pS = [Buf(f"pS{i}") for i in range(2)]
        pOs = [ps(f"pO{i}", [128, 512]) for i in range(2)]; B_pOs = [Buf(f"pO{i}") for i in range(2)]
        pB = ps("pB", [128, 512]); B_pB = Buf("pB")
        pT = ps("pT", [128, 1024], BF16); B_pT = Buf("pT")
        for i in range(2):
            S.run('dve', [], [B_vt[i]], lambda e, i=i: e.memset(vt[i][:, :, 64:65], 1.0))

        def b_loads(sh):
            seg, h = sh // H, sh % H
            i2 = sh % 2
            S.dma('sp', qt[i2][:], qm[h * 96:(h + 1) * 96, seg * SEG:(seg + 1) * SEG], [B_scr], [B_qt[i2]], B_qt[i2])
            S.dma('sp', zt[i2][:], zm[h * 64:(h + 1) * 64, seg * SEG:(seg + 1) * SEG], [B_scr], [B_zt[i2]], B_zt[i2])
            if seg == 0:
                S.dma('sp', vT1[:, 0:SEG], vm[h * 64:(h + 1) * 64, :], [B_scr], [B_vT1], B_vT1)
                S.dma('sp', kt[i2][:, 0:SEG], km[h * 96:(h + 1) * 96, :], [B_scr], [B_kt[i2]], B_kt[i2])
            else:
                jj, hh = h // 2, h % 2
                for r in range(4):
                    S.dma('sp', vT1[:, r * SEG:(r + 1) * SEG], vmg[jj, r * 128 + hh * 64:r * 128 + (hh + 1) * 64, :],
                          [B_gath], [B_vT1], B_vT1, accum=(r > 0))
                for r in range(4):
                    S.dma('sp', kt[i2][:, r * SEG:(r + 1) * SEG], kmg[h, r * 96:(r + 1) * 96, :], [B_gath], [B_kt[i2]],
                          B_kt[i2], accum=(r > 0))

        def b_prep(sh, c8lo, c8hi):
            seg = sh // H
            i2 = sh % 2
            nch_ = (SEG if seg == 0 else 4 * SEG) // 128
            for c8 in range(c8lo, min(c8hi, nch_ // 8)):
                def f(e, c8=c8):
                    for c in range(8):
                        ch = c8 * 8 + c
                        ins = e.transpose(pT[:, c * 128:c * 128 + 64], vT1[:, ch * 128:(ch + 1) * 128], ident[0:64, 0:64])
                    return ins
                S.run('pe', [B_vT1, Bc], [B_pT], f)
                S.run('dve', [B_pT], [B_vt[i2]],
                      lambda e, c8=c8: e.tensor_copy(out=vt[i2][:, c8 * 8:(c8 + 1) * 8, 0:64],
                                                     in_=pT[:, :].rearrange("p (c x) -> p c x", x=128)[:, :, 0:64]))

        pending = []

        def flush():
            while pending:
                pending.pop(0)()

        gidx = 0
        b_loads(0)
        b_prep(0, 0, 16)
        for sh in range(2 * H):
            seg, h = sh // H, sh % H
            i2 = sh % 2
            NK = SEG if seg == 0 else 4 * SEG
            nch = NK // 128
            G = nch // 2
            for qi in range(8):
                qs = slice(qi * 512, (qi + 1) * 512)
                pO = pOs[qi % 2]
                B_pO = B_pOs[qi % 2]

                def QK(g, gi):
                    si = gi % 2

                    def f(e):
                        for j in range(2):
                            ch = 2 * g + j
                            ins = e.matmul(pS[si][:, j * 512:(j + 1) * 512], lhsT=kt[i2][:, ch * 128:(ch + 1) * 128],
                                           rhs=qt[i2][:, qs], start=True, stop=True)
                        return ins
                    S.run('pe', [B_kt[i2], B_qt[i2]], [B_pS[si]], f)

                def EXP(g, gi):
                    si, pi_ = gi % 2, gi % 2
                    S.run('act', [B_pS[si]], [B_pt[pi_]],
                          lambda e: e.activation(out=pt[pi_][:], in_=pS[si][:, :], func=AF.Exp, scale=SC_MLA))

                def PV(g, gi):
                    pi_ = gi % 2

                    def f(e):
                        for j in range(2):
                            ch = 2 * g + j
                            ins = e.matmul(pO[0:65, :], lhsT=vt[i2][:, ch, :], rhs=pt[pi_][:, j * 512:(j + 1) * 512],
                                           start=(ch == 0), stop=(ch == nch - 1))
                        return ins
                    S.run('pe', [B_vt[i2], B_pt[pi_]], [B_pO], f)
                QK(0, gidx)
                QK(1, gidx + 1)
                for g in range(G):
                    EXP(g, gidx + g)
                    PV(g, gidx + g)
                    if g + 2 < G:
                        QK(g + 2, gidx + g + 2)
                    if g == 3:
                        flush()
                        if qi == 0 and sh + 1 < 2 * H:
                            b_loads(sh + 1)
                    if sh + 1 < 2 * H and qi >= 2 and g == 5:
                        n8 = (SEG if (sh + 1) // H == 0 else 4 * SEG) // 128 // 8
                        per = (n8 + 5) // 6
                        b_prep(sh + 1, (qi - 2) * per, (qi - 1) * per)
                gidx += G
                S.run('dve', [B_pO], [B_rd], lambda e, pO=pO: e.reciprocal(out=rd[64:65, :], in_=pO[64:65, :]))

                def tail(pO=pO, B_pO=B_pO, qs=qs, i2=i2, last=(qi == 7), h=h, seg=seg):
                    S.run('pe', [B_rd, Bc2], [B_pB],
                          lambda e: e.matmul(pB[0:64, :], lhsT=onesf[64:65, 0:64], rhs=rd[64:65, :], start=True, stop=True))
                    S.run('dve', [B_pO, B_zt[i2]], [B_tmpo],
                          lambda e: e.tensor_tensor(out=tmpo[:, :], in0=pO[0:64, :], in1=zt[i2][:, qs], op=ALU.mult))
                    S.run('dve', [B_tmpo, B_pB], [B_ot[i2]],
                          lambda e: e.tensor_tensor(out=ot[i2][:, qs], in0=tmpo[:, :], in1=pB[0:64, :], op=ALU.mult))
                    if last:
                        S.dma('sp', om[h * 64:(h + 1) * 64, seg * SEG:(seg + 1) * SEG], ot[i2][:], [B_ot[i2]], [B_scr],
                              B_ot[i2], accum=True)
                pending.append(tail)
        flush()
        S.barrier()
        es_ph.close()
        if l == 0 and stop == "B":
            es.close()
            return nc

        es_ph = ExitStack()
        st['ph'] = es_ph
        WK = SEG + 2 * KPAD
        qdt = sb("qdt", [128, 3, SEG], BF16)
        kdt = sb("kdt", [128, 3, WK], BF16)
        vdT = sb("vdT", [128, 3, WK], BF16)
        B_qd2 = [Buf(f"qd2_{i}") for i in range(2)]
        B_kd2 = [Buf(f"kd2_{i}") for i in range(2)]
        B_vd2 = [Buf(f"vd2_{i}") for i in range(2)]
        NCH = [d * (SEG // (128 * d) + 1) for d in DILS]
        CB = [0, NCH[0], NCH[0] + NCH[1]]
        NCHT = sum(NCH)
        vdt = [sb(f"vdt{i}", [128, NCHT, 65], BF16) for i in range(2)]; B_vdt = [Buf(f"vdt{i}") for i in range(2)]
        zdt = [sb(f"zdt{i}", [64, SEG], BF16) for i in range(2)]; B_zdt = [Buf(f"zdt{i}") for i in range(2)]
        odt = [sb(f"odt{i}", [64, SEG], BF16) for i in range(2)]; B_odt = [Buf(f"odt{i}") for i in range(2)]
        osum = sb("osum", [65, SEG], F32); B_osum = Buf("osum")
        rdd = sb("rdd", [65, SEG], F32); B_rdd = Buf("rdd")
        pdt = [sb(f"pdt{i}", [128, 1024], BF16) for i in range(2)]; B_pdt = [Buf(f"pdt{i}") for i in range(2)]
        pS = [ps(f"pSd{i}", [128, 1024]) for i in range(2)]; B_pS = [Buf(f"pSd{i}") for i in range(2)]
        pO = [ps(f"pOd{i}", [128, 512]) for i in range(2)]; B_pO = [Buf(f"pOd{i}") for i in range(2)]
        pB = ps("pBd", [128, 512]); B_pB = Buf("pBd")
        pT = ps("pTd", [128, 1024], BF16); B_pT = Buf("pTd")
        for i in range(2):
            S.run('dve', [], [B_vdt[i]], lambda e, i=i: e.memset(vdt[i][:, :, 64:65], 1.0))

        def slab(bf, g):
            s_ = bf * 3 + g
            return 64 * (s_ % 2), s_ // 2

        def c_loads(sh):
            seg, h = sh // H, sh % H
            bf = sh % 2
            s0 = seg * SEG
            S.dma('sp', zdt[bf][:], zd[h * 64:(h + 1) * 64, s0:s0 + SEG], [B_scr], [B_zdt[bf]], B_zdt[bf])
            for g in range(3):
                p0, sl = slab(bf, g)
                hw = HW[g]
                S.dma('sp', qdt[p0:p0 + 64, sl, :], qd[g, h * 64:(h + 1) * 64, s0:s0 + SEG], [B_scr], [B_qd2[bf]], B_qd2[bf],
                      accum=(g > 0))
                S.dma('sp', kdt[p0:p0 + 64, sl, KPAD:KPAD + SEG], kd[g, h * 64:(h + 1) * 64, s0:s0 + SEG], [B_scr],
                      [B_kd2[bf]], B_kd2[bf], accum=(g > 0))
                S.dma('sp', vdT[p0:p0 + 64, sl, KPAD:KPAD + SEG], vd[g, h * 64:(h + 1) * 64, s0:s0 + SEG], [B_scr],
                      [B_vd2[bf]], B_vd2[bf], accum=(g > 0))
                if seg == 1:
                    S.dma('sp', kdt[p0:p0 + 64, sl, KPAD - hw:KPAD], halo_k[g][0][h * 64:(h + 1) * 64, :], [B_halo],
                          [B_kd2[bf]], B_kd2[bf], accum=True)
                    S.dma('sp', kdt[p0:p0 + 64, sl, KPAD + SEG:KPAD + SEG + hw], halo_k[g][1][h * 64:(h + 1) * 64, :],
                          [B_halo], [B_kd2[bf]], B_kd2[bf], accum=True)
                    S.dma('sp', vdT[p0:p0 + 64, sl, KPAD - hw:KPAD], halo_v[g][0][h * 64:(h + 1) * 64, :], [B_halo],
                          [B_vd2[bf]], B_vd2[bf], accum=True)
                    S.dma('sp', vdT[p0:p0 + 64, sl, KPAD + SEG:KPAD + SEG + hw], halo_v[g][1][h * 64:(h + 1) * 64, :],
                          [B_halo], [B_vd2[bf]], B_vd2[bf], accum=True)

        def c_prep(sh):
            seg, h = sh // H, sh % H
            bf = sh % 2
            if seg == 0:
                def f(e):
                    for g in range(3):
                        p0, sl = slab(bf, g)
                        hw = HW[g]
                        e.memset(kdt[p0:p0 + 64, sl, KPAD - hw:KPAD], 0.0)
                        e.memset(kdt[p0:p0 + 64, sl, KPAD + SEG:KPAD + SEG + hw], 0.0)
                        e.memset(vdT[p0:p0 + 64, sl, KPAD - hw:KPAD], 0.0)
                        ins = e.memset(vdT[p0:p0 + 64, sl, KPAD + SEG:KPAD + SEG + hw], 0.0)
                    return ins
                S.run('pool', [], [B_kd2[bf], B_vd2[bf]], f)
            else:
                def f(e):
                    for g in range(3):
                        p0, sl = slab(bf, g)
                        hw = HW[g]
                        e.tensor_scalar(out=vdT[p0:p0 + 64, sl, KPAD - hw:KPAD], in0=vdT[p0:p0 + 64, sl, KPAD - hw:KPAD],
                                        scalar1=flags[p0:p0 + 64, 0:1], scalar2=None, op0=ALU.mult)
                        ins = e.tensor_scalar(out=vdT[p0:p0 + 64, sl, KPAD + SEG:KPAD + SEG + hw],
                                              in0=vdT[p0:p0 + 64, sl, KPAD + SEG:KPAD + SEG + hw],
                                              scalar1=flags[p0:p0 + 64, 1:2], scalar2=None, op0=ALU.mult)
                    return ins
                S.run('pool', [Bc], [B_vd2[bf]], f)
            chunks = []
            for g, d in enumerate(DILS):
                ntg = SEG // (128 * d)
                for r in range(d):
                    for t in range(ntg + 1):
                        start = KPAD + r + d * (128 * t - 64)
                        chunks.append((g, CB[g] + r * (ntg + 1) + t, start, d))
            groups8 = []
            for g in range(3):
                cg = [c for c in chunks if c[0] == g]
                for c8 in range(0, len(cg), 8):
                    groups8.append(cg[c8:c8 + 8])
            for grp in groups8:

                def f(e, grp=grp):
                    for c, (g, ci, start, d) in enumerate(grp):
                        p0, sl = slab(bf, g)
                        ins = e.transpose(pT[:, c * 128:c * 128 + 64], vdT[p0:p0 + 64, sl, start:start + 127 * d + 1:d],
                                          ident[p0:p0 + 64, p0:p0 + 64])
                    return ins
                S.run('pe', [B_vd2[bf], Bc], [B_pT], f)
                n = len(grp)
                ci0 = grp[0][1]
                S.run('dve', [B_pT], [B_vdt[bf]],
                      lambda e, n=n, ci0=ci0: e.tensor_copy(out=vdt[bf][:, ci0:ci0 + n, 0:64],
                                                            in_=pT[:, 0:n * 128].rearrange("p (c x) -> p c x", x=128)[:, :, 0:64]))

            def f(e):
                ins = None
                for g, d in enumerate(DILS):
                    ntg = SEG // (128 * d)
                    v = vdt[bf][:, CB[g]:CB[g] + NCH[g], :].rearrange("p (r t) x -> p r t x", t=ntg + 1)
                    if seg == 0:
                        e.memset(v[0:64, :, 0, 64:65], 0.0)
                        ins = e.memset(v[64:128, :, ntg, 64:65], 0.0)
                    else:
                        e.tensor_copy(out=v[0:64, :, 0, 64:65], in_=flags[0:64, 0:1].to_broadcast([64, d, 1]))
                        ins = e.tensor_copy(out=v[64:128, :, ntg, 64:65], in_=flags[64:128, 1:2].to_broadcast([64, d, 1]))
                return ins
            S.run('dve', [Bc], [B_vdt[bf]], f)

        bi = 0
        c_loads(0)
        chk("C0l")
        c_prep(0)
        chk("C0p")
        for sh in range(2 * H):
            if sh > 0:
                chk("C%de" % (sh - 1))
            seg, h = sh // H, sh % H
            bf = sh % 2
            s0 = seg * SEG
            if sh + 1 < 2 * H:
                c_loads(sh + 1)
            batches = []
            for g, d in enumerate(DILS):
                ntg = SEG // (128 * d)
                tiles = [(r, t) for r in range(d) for t in range(ntg)]
                for b4 in range(0, len(tiles), 4):
                    batches.append((g, d, ntg, tiles[b4:b4 + 4]))

            def QK(bidx, si):
                g, d, ntg, batch = batches[bidx]
                p0, sl = slab(bf, g)

                def f(e):
                    for bk in range(2):
                        e.matmul(pS[si][:, bk * 512:(bk + 1) * 512], lhsT=ident[:, :], rhs=maskb[:, :], start=True, stop=False,
                                 skip_group_check=True)
                    for n, (r, t) in enumerate(batch):
                        qstart = r + d * 128 * t
                        for ab in range(2):
                            kstart = KPAD + r + d * (128 * (t + ab) - 64)
                            o_ = pS[si][:, n * 256 + ab * 128:n * 256 + (ab + 1) * 128]
                            ins = e.matmul(o_, lhsT=kdt[p0:p0 + 64, sl, kstart:kstart + 127 * d + 1:d],
                                           rhs=qdt[p0:p0 + 64, sl, qstart:qstart + 127 * d + 1:d], start=False,
                                           stop=(n % 2 == 1 and ab == 1), skip_group_check=True)
                    return ins
                S.run('pe', [B_kd2[bf], B_qd2[bf], Bc], [B_pS[si]], f)

            def EXP(bidx, si):
                S.run('act', [B_pS[si]], [B_pdt[si]],
                      lambda e: e.activation(out=pdt[si][:], in_=pS[si][:, :], func=AF.Exp, scale=SC_DIL))

            def PV(bidx, si):
                g, d, ntg, batch = batches[bidx]

                def f(e):
                    for n, (r, t) in enumerate(batch):
                        for ab in range(2):
                            ci = CB[g] + r * (ntg + 1) + t + ab
                            ins = e.matmul(pO[si][0:65, n * 128:(n + 1) * 128], lhsT=vdt[bf][:, ci, :],
                                           rhs=pdt[si][:, n * 256 + ab * 128:n * 256 + (ab + 1) * 128],
                                           start=(ab == 0), stop=(ab == 1))
                    return ins
                S.run('pe', [B_vdt[bf], B_pdt[si]], [B_pO[si]], f)

            def OSUM(bidx, si):
                g, d, ntg, batch = batches[bidx]
                r0, t0 = batch[0]
                if g == 0:
                    dst = osum[0:65, 128 * t0:128 * t0 + 512]
                    src = pO[si][0:65, :]
                elif g == 1:
                    st_ = r0 + 512 * t0
                    dst = osum[0:65, st_:st_ + 4 * 511 + 1:4]
                    src = pO[si][0:65, :]
                else:
                    dst = osum[0:65, :].rearrange("p (n d) -> p d n", d=16)[:, r0:r0 + 2, :]
                    src = pO[si][0:65, :].rearrange("p (a b) -> p a b", a=2)
                if g == 0:
                    S.run('dve', [B_pO[si]], [B_osum], lambda e: e.tensor_copy(out=dst, in_=src))
                else:
                    S.run('dve', [B_pO[si], B_osum], [B_osum], lambda e: e.tensor_tensor(out=dst, in0=dst, in1=src, op=ALU.add))

            nb = len(batches)
            QK(0, bi % 2)
            for b_ in range(nb):
                si = (bi + b_) % 2
                EXP(b_, si)
                if b_ + 1 < nb:
                    QK(b_ + 1, (bi + b_ + 1) % 2)
                PV(b_, si)
                OSUM(b_, si)
            bi += nb
            chk("C%db" % sh)
            if sh + 1 < 2 * H:
                c_prep(sh + 1)
            S.run('act', [B_osum], [B_rdd], lambda e: e.activation(out=rdd[64:65, :], in_=osum[64:65, :], func=AF.Ln))
            S.run('act', [B_rdd], [B_rdd], lambda e: e.activation(out=rdd[64:65, :], in_=rdd[64:65, :], func=AF.Exp, scale=-1.0))
            for qi in range(8):
                qs = slice(qi * 512, (qi + 1) * 512)
                S.run('pe', [B_rdd, Bc2], [B_pB],
                      lambda e, qs=qs: e.matmul(pB[0:64, :], lhsT=onesf[64:65, 0:64], rhs=rdd[64:65, qs], start=True, stop=True))
                S.run('pool', [B_osum, B_zdt[bf]], [B_osum],
                      lambda e, qs=qs: e.tensor_tensor(out=osum[0:64, qs], in0=osum[0:64, qs], in1=zdt[bf][:, qs], op=ALU.mult))
                S.run('dve', [B_osum, B_pB], [B_odt[bf]],
                      lambda e, qs=qs: e.tensor_tensor(out=odt[bf][:, qs], in0=osum[0:64, qs], in1=pB[0:64, :], op=ALU.mult))
            S.dma('sp', od[h * 64:(h + 1) * 64, s0:s0 + SEG], odt[bf][:], [B_odt[bf]], [B_scr], B_odt[bf], accum=True)
        S.barrier()
        es_ph.close()
        if l == 0 and stop == "C":
            es.close()
            return nc

        es_ph = ExitStack()
        st['ph'] = es_ph
        wpa = sb("wpa", [128, 4, D], BF16); wpb = sb("wpb", [128, 4, D], BF16); wo = sb("wo", [128, 8, D], BF16)
        B_wd = Buf("wd")
        wstg = [sb(f"wstgd{i}", [128, 1024], F32) for i in range(2)]; B_wstg = [Buf(f"wstgd{i}") for i in range(2)]
        it = 0
        for (src, dstt, nk) in ((w_pa, wpa, 4), (w_pb, wpb, 4), (w_out, wo, 8)):
            for k in range(nk):
                w = it % 2
                it += 1
                S.dma('sp', wstg[w][:], src[l, k * 128:(k + 1) * 128, :], [], [B_wstg[w]], B_wstg[w])
                S.run('dve' if it % 2 else 'act', [B_wstg[w]], [B_wd],
                      (lambda e, w=w, dstt=dstt, k=k: e.tensor_copy(out=dstt[:, k, :], in_=wstg[w][:])) if it % 2 else
                      (lambda e, w=w, dstt=dstt, k=k: e.activation(out=dstt[:, k, :], in_=wstg[w][:], func=AF.Copy)))
        oin = [sb(f"oin{i}", [128, 8, TT], BF16) for i in range(2)]; B_oin = [Buf(f"oin{i}") for i in range(2)]
        gin = [sb(f"gin{i}", [128, 16, TT], BF16) for i in range(2)]; B_gin = [Buf(f"gin{i}") for i in range(2)]
        uT = [sb(f"uT{i}", [128, 8, TT], BF16) for i in range(2)]; B_uT = [Buf(f"uT{i}") for i in range(2)]
        t1 = [sb(f"t1{i}", [128, TT], F32) for i in range(2)]; B_t1 = [Buf(f"t1{i}") for i in range(2)]
        t2 = [sb(f"t2{i}", [128, TT], F32) for i in range(2)]; B_t2 = [Buf(f"t2{i}") for i in range(2)]
        xo = [sb(f"xo{i}", [128, D], F32) for i in range(2)]; B_xo = [Buf(f"xo{i}") for i in range(2)]
        yv = [sb(f"yv{i}", [128, D], F32) for i in range(2)]; B_yv = [Buf(f"yv{i}") for i in range(2)]
        gtb = sb("gtb", [128, D], F32); B_gtb = Buf("gtb")
        gfb = sb("gfb", [128, D], F32); B_gfb = Buf("gfb")
        gf1 = sb("gf1", [1, D], F32); B_gf1 = Buf("gf1")
        junk2 = sb("junk2", [128, D], F32); B_junk2 = Buf("junk2")
        ss2 = sb("ss2", [128, 2], F32); B_ss2 = Buf("ss2")
        pya = [ps(f"pya{i}", [128, 512]) for i in range(2)]; B_pya = [Buf(f"pya{i}") for i in range(2)]
        pyb = [ps(f"pyb{i}", [128, 512]) for i in range(2)]; B_pyb = [Buf(f"pyb{i}") for i in range(2)]
        py = [ps(f"py{i}", [128, 1024]) for i in range(2)]; B_py = [Buf(f"py{i}") for i in range(2)]
        if l == DEPTH - 1:
            S.dma('sp', gf1[:], g_final.rearrange("(a d) -> a d", a=1), [], [B_gf1], B_gf1)
            for hf in range(2):
                S.run('pe', [B_gf1, Bc2], [B_py[0]],
                      lambda e, hf=hf: e.matmul(py[0][:, hf * 512:(hf + 1) * 512], lhsT=onesf[0:1, :],
                                                rhs=gf1[0:1, hf * 512:(hf + 1) * 512], start=True, stop=True))
            S.run('act', [B_py[0]], [B_gfb], lambda e: e.activation(out=gfb[:], in_=py[0][:, :], func=AF.Copy))
        cur_seg = -1
        ci = 0
        si4 = 0
        for tile in range(NT):
            seg = tile // 8
            t0 = tile * TT
            i2 = tile % 2
            if seg != cur_seg:
                cur_seg = seg
                S.dma('sp', gtb[:], modbc[l, seg, 2], [B_modbc], [B_gtb], B_gtb)
            def d_loads(tl_):
                j2 = tl_ % 2
                tt0 = tl_ * TT
                S.dma('sp', oin[j2][:, 0:4, :], om[:, tt0:tt0 + TT].rearrange("(k p) t -> p k t", p=128), [B_scr], [B_oin[j2]],
                      B_oin[j2])
                S.dma('sp', oin[j2][:, 4:8, :], od[:, tt0:tt0 + TT].rearrange("(k p) t -> p k t", p=128), [B_scr], [B_oin[j2]],
                      B_oin[j2], accum=True)
                S.dma('sp', gin[j2][:], gab[:, tt0:tt0 + TT].rearrange("(k p) t -> p k t", p=128), [B_scr], [B_gin[j2]], B_gin[j2])
            if tile == 0:
                d_loads(0)
            if tile + 1 < NT:
                d_loads(tile + 1)
            for j in range(8):
                c2 = ci % 2
                ci += 1

                def f(e, j=j, c2=c2):
                    for k in range(4):
                        ins = e.matmul(pya[c2][:, :], lhsT=wpa[:, k, j * 128:(j + 1) * 128], rhs=oin[i2][:, k, :],
                                       start=(k == 0), stop=(k == 3))
                    return ins
                S.run('pe', [B_wd, B_oin[i2]], [B_pya[c2]], f)

                def f(e, j=j, c2=c2):
                    for k in range(4):
                        ins = e.matmul(pyb[c2][:, :], lhsT=wpb[:, k, j * 128:(j + 1) * 128], rhs=oin[i2][:, 4 + k, :],
                                       start=(k == 0), stop=(k == 3))
                    return ins
                S.run('pe', [B_wd, B_oin[i2]], [B_pyb[c2]], f)
                S.run('dve', [B_pya[c2], B_gin[i2]], [B_t1[c2]],
                      lambda e, j=j, c2=c2: e.tensor_tensor(out=t1[c2][:], in0=pya[c2][:, :], in1=gin[i2][:, j, :], op=ALU.mult))
                S.run('dve', [B_pyb[c2], B_gin[i2]], [B_t2[c2]],
                      lambda e, j=j, c2=c2: e.tensor_tensor(out=t2[c2][:], in0=pyb[c2][:, :], in1=gin[i2][:, 8 + j, :],
                                                            op=ALU.mult))
                S.run('pool', [B_t1[c2], B_t2[c2]], [B_uT[i2]],
                      lambda e, j=j, c2=c2: e.tensor_tensor(out=uT[i2][:, j, :], in0=t1[c2][:], in1=t2[c2][:], op=ALU.add))
            for s4 in range(4):
                x2 = si4 % 2
                si4 += 1
                r0 = t0 + s4 * 128
                S.dma('sp', xo[x2][:], x_src[r0:r0 + 128, :], [], [B_xo[x2]], B_xo[x2])

                def f(e, s4=s4, x2=x2):
                    for hf in range(2):
                        for k in range(8):
                            ins = e.matmul(py[x2][:, hf * 512:(hf + 1) * 512], lhsT=uT[i2][:, k, s4 * 128:(s4 + 1) * 128],
                                           rhs=wo[:, k, hf * 512:(hf + 1) * 512], start=(k == 0), stop=(k == 7))
                    return ins
                S.run('pe', [B_uT[i2], B_wd], [B_py[x2]], f)
                S.run('dve', [B_py[x2], B_gtb], [B_yv[x2]],
                      lambda e, x2=x2: e.tensor_tensor(out=yv[x2][:], in0=py[x2][:, :], in1=gtb[:], op=ALU.mult))
                S.run('pool', [B_yv[x2], B_xo[x2]], [B_xo[x2]],
                      lambda e, x2=x2: e.tensor_tensor(out=xo[x2][:], in0=xo[x2][:], in1=yv[x2][:], op=ALU.add))
                if l < DEPTH - 1:
                    S.dma('sp', xres[r0:r0 + 128, :], xo[x2][:], [B_xo[x2]], [B_scr], B_xo[x2], accum=True)
                else:
                    S.run('act', [B_xo[x2]], [B_junk2, B_ss2],
                          lambda e, x2=x2: e.activation(out=junk2[:], in_=xo[x2][:], func=AF.Square, accum_out=ss2[:, 0:1]))

                    S.run('dve', [B_ss2], [B_ss2], lambda e: e.tensor_scalar(
                        out=ss2[:, 1:2], in0=ss2[:, 0:1], scalar1=1.0 / D, scalar2=EPS, op0=ALU.mult, op1=ALU.add))
                    S.run('act', [B_ss2], [B_ss2], lambda e: e.activation(out=ss2[:, 1:2], in_=ss2[:, 1:2], func=AF.Sqrt))
                    S.run('dve', [B_ss2], [B_ss2], lambda e: e.reciprocal(out=ss2[:, 1:2], in_=ss2[:, 1:2]))
                    S.run('dve', [B_xo[x2], B_ss2, B_gfb], [B_yv[x2]],
                          lambda e, x2=x2: e.scalar_tensor_tensor(out=yv[x2][:], in0=xo[x2][:], scalar=ss2[:, 1:2], in1=gfb[:],
                                                                  op0=ALU.mult, op1=ALU.mult))
                    S.dma('sp', y_out[r0:r0 + 128, :], yv[x2][:], [B_yv[x2]], [B_scr], B_yv[x2], accum=True)
        S.barrier()
        es_ph.close()
        if l == 0 and stop == "D":
            es.close()
            return nc

    es.close()
    return nc


_CACHE = {}


def _rope_tables(core):
    s, c = core // 4, core % 4
    tabs = np.zeros((NT, 4, 128, TT), np.float32)
    p = np.arange(128)
    j2 = p % 32
    half2 = (p % 64) // 32
    inv2 = np.power(np.float32(10000.0), -(2.0 * np.arange(32, dtype=np.float32)) / np.float32(64)).astype(np.float32)
    inv1 = np.power(np.float32(10000.0), -(2.0 * np.arange(16, dtype=np.float32)) / np.float32(32)).astype(np.float32)
    for tile in range(NT):
        seg = tile // 8
        tl = (tile % 8) * TT
        pos = (np.arange(TT) + tl + (c * SEG if seg == 1 else 0)).astype(np.float32)
        ang2 = (pos[None, :] * inv2[j2][:, None]).astype(np.float32)
        tabs[tile, 0] = np.cos(ang2)
        tabs[tile, 1] = np.sin(ang2) * np.where(half2 == 0, 1.0, -1.0)[:, None]
        ang1 = (pos[None, :] * inv1[:, None]).astype(np.float32)
        cos1, sin1 = np.cos(ang1), np.sin(ang1)
        cc = np.concatenate([cos1, cos1], 0)
        ss = np.concatenate([-sin1, sin1], 0)
        tabs[tile, 2, 0:32] = cc
        tabs[tile, 2, 64:96] = cc
        tabs[tile, 3, 32:64] = ss
        tabs[tile, 3, 96:128] = ss
    return tabs


def kernel(x_prompt, x_sample, c_prompt, c_sample, w_ada, b_ada, g_norm, w_in, b_gate, g_cq, w_uq,
           g_ckv, w_ukv, w_pa, w_pb, w_out, g_final, _dbg=(), _stop=None):
    f32 = np.float32
    key = ("nc", tuple(_dbg), _stop)
    if key not in _CACHE:
        _CACHE[key] = build_program(dbg=tuple(_dbg), stop=_stop)
    nc = _CACHE[key]
    kk = np.arange(128)[:, None]
    qq = np.arange(128)[None, :]
    mA = np.where(kk >= qq, 0.0, -30000.0).astype(f32)
    mB = np.where(kk <= qq, 0.0, -30000.0).astype(f32)
    maskb = np.concatenate([mA, mB, mA, mB], 1).astype(ml_dtypes.bfloat16)
    ident = np.eye(128, dtype=f32).astype(ml_dtypes.bfloat16)
    sel = np.zeros((2, 2, 128), f32)
    sel[0, 0] = 1.0
    sel[1, 1] = 1.0
    shared = dict(w_ada=np.asarray(w_ada, f32), b_ada=np.asarray(b_ada, f32), g_norm=np.asarray(g_norm, f32),
                  w_in=np.asarray(w_in, f32),
                  b_gate=np.ascontiguousarray(np.asarray(b_gate, f32).reshape(DEPTH, 16, 128).transpose(0, 2, 1)),
                  g_cq=np.ascontiguousarray(np.asarray(g_cq, f32).reshape(DEPTH, 3, 128).transpose(0, 2, 1)),
                  w_uq=np.asarray(w_uq, f32),
                  g_ckv=np.ascontiguousarray(np.asarray(g_ckv, f32).reshape(DEPTH, 2, 128).transpose(0, 2, 1)), w_ukv=np.asarray(w_ukv, f32),
                  w_pa=np.asarray(w_pa, f32), w_pb=np.asarray(w_pb, f32), w_out=np.asarray(w_out, f32),
                  g_final=np.asarray(g_final, f32), maskb=maskb, ident=ident, sel=sel)
    x_prompt = np.asarray(x_prompt, f32)
    x_sample = np.asarray(x_sample, f32)
    c_prompt = np.asarray(c_prompt, f32)
    c_sample = np.asarray(c_sample, f32)
    in_maps = []
    for core in range(NCORES):
        s, c = core // 4, core % 4
        x_own = np.concatenate([x_prompt[core], x_sample[s, c * SEG:(c + 1) * SEG]], 0)
        cc = np.stack([c_prompt[core], c_sample[s]], 0)
        cT = np.ascontiguousarray(cc.reshape(2, 8, 128).transpose(2, 1, 0))
        flags = np.zeros((128, 2), f32)
        flags[:, 0] = 1.0 if c > 0 else 0.0
        flags[:, 1] = 1.0 if c < 3 else 0.0
        m = dict(shared)
        m.update(x_own=np.ascontiguousarray(x_own), cT=cT, tabs=_rope_tables(core), flags=flags)
        in_maps.append(m)
    res = run_bass_kernel_spmd(nc, in_maps, core_ids=list(range(NCORES)))
    if _dbg or _stop:
        return res
    y_prompt = np.stack([res.results[i]["y"][0:SEG] for i in range(NCORES)], 0)
    y_sample = np.stack([np.concatenate([res.results[4 * s + c]["y"][SEG:T] for c in range(4)], 0) for s in range(2)], 0)
    return (y_prompt.astype(f32), y_sample.astype(f32))
```

```python
import numpy as np
import ml_dtypes
from contextlib import ExitStack
import concourse.bass as bass
import concourse.mybir as mybir
from concourse.bass_utils import run_bass_kernel_spmd

F32 = mybir.dt.float32
BF16 = mybir.dt.bfloat16
ALU = mybir.AluOpType
AF = mybir.ActivationFunctionType

NCORES = 8
D = 1024
SEG = 4096
T = 8192
TT = 512
NT = T // TT
H = 8
DEPTH = 2
EPS = 1e-6
INW = 8352
C_CQ, C_CKV, C_KR, C_ZM, C_DIL, C_ZD, C_MG = 0, 384, 640, 672, 1184, 5792, 6304
DILS = (1, 4, 16)
HW = (64, 256, 1024)
KPAD = 1024
SC_MLA = 96 ** -0.5
SC_DIL = 64 ** -0.5


class _Stop(Exception):
    pass


class Buf:
    def __init__(self, name):
        self.name = name
        self.w = {}
        self.r = {}
        self.dsem = None
        self.dcnt = 0
        self.keep = False


class Sched:
    def __init__(self, nc, es):
        self.nc = nc
        self.es = es
        self.E = {'pe': nc.tensor, 'act': nc.scalar, 'dve': nc.vector, 'pool': nc.gpsimd, 'sp': nc.sync}
        self.sem = {k: es.enter_context(nc.semaphore("sem_" + k)) for k in self.E}
        self.cnt = {k: 0 for k in self.E}
        self.waited = {k: {} for k in self.E}
        self.dsems = []
        self.dpool = []
        self.bsem = es.enter_context(nc.semaphore("sem_bar"))
        self.bcnt = 0
        self.ccsem = es.enter_context(nc.semaphore("sem_cc"))
        self.cccnt = 0
        self.nsem = 8

    def _wait(self, eng, tok):
        sem, c, owner = tok
        if eng == 'pe' and owner == 'pe':
            return
        w = self.waited[eng]
        key = id(sem)
        if w.get(key, 0) >= c:
            return
        w[key] = c
        self.E[eng].wait_ge(sem, c)

    def pre(self, eng, reads, writes):
        for b in reads:
            for t in b.w.values():
                self._wait(eng, t)
        for b in writes:
            for t in b.w.values():
                self._wait(eng, t)
            for t in b.r.values():
                self._wait(eng, t)

    def _mark(self, tok, reads, writes, accum=False):
        key = id(tok[0])
        for b in reads:
            b.r[key] = tok
        for b in writes:
            if accum:
                b.w[key] = tok
            else:
                b.w = {key: tok}
                b.r = {}

    def post(self, eng, ins, reads, writes):
        self.cnt[eng] += 1
        ins.then_inc(self.sem[eng], 1)
        self._mark((self.sem[eng], self.cnt[eng], eng), reads, writes)

    def run(self, eng, reads, writes, fn):
        self.pre(eng, reads, writes)
        ins = fn(self.E[eng])
        self.post(eng, ins, reads, writes)

    def dma(self, q, out, in_, reads, writes, sb, accum=False):
        self.pre(q, reads, writes)
        if sb.dsem is None:
            if self.dpool and not sb.keep:
                sb.dsem, sb.dcnt = self.dpool.pop()
            else:
                sb.dsem = self.es.enter_context(self.nc.semaphore("d%d" % self.nsem))
                sb.dcnt = 0
                self.nsem += 1
            self.dsems.append(sb)
        ins = self.E[q].dma_start(out=out, in_=in_)
        sb.dcnt += 16
        ins.then_inc(sb.dsem, 16)
        self._mark((sb.dsem, sb.dcnt, 'dma'), reads, writes, accum=accum)

    def collective(self, in_ap, out_ap, reads, writes):
        self.pre('pool', reads, writes)
        ins = self.nc.gpsimd.collective_compute("AllGather", ALU.bypass,
                                                replica_groups=[[0, 1, 2, 3], [4, 5, 6, 7]],
                                                ins=[in_ap], outs=[out_ap], dma_qos="P2")
        self.cccnt += 1
        ins.then_inc(self.ccsem)
        self._mark((self.ccsem, self.cccnt, 'cc'), reads, writes)

    def barrier(self, end_phase=True):
        sp = self.E['sp']
        for k in self.E:
            if k != 'sp' and self.cnt[k] > 0:
                self._wait('sp', (self.sem[k], self.cnt[k], k))
        for sb in self.dsems:
            if sb.dcnt > 0:
                self._wait('sp', (sb.dsem, sb.dcnt, 'dma'))
        if self.cccnt > 0:
            self._wait('sp', (self.ccsem, self.cccnt, 'cc'))
        self.bcnt += 1
        sp.sem_inc(self.bsem, 1)
        for k in self.E:
            if k != 'sp':
                self.E[k].wait_ge(self.bsem, self.bcnt)
        if end_phase:
            keep = []
            for sb in self.dsems:
                if sb.keep:
                    keep.append(sb)
                else:
                    self.dpool.append((sb.dsem, sb.dcnt))
                    sb.dsem = None
            self.dsems = keep


def build_program(dbg=(), stop=None):
    nc = bass.Bass("TRN2", target_bir_lowering=False)
    es = ExitStack()
    st = {'ph': None}
    build_program.last = st
    try:
        return _build(nc, es, st, dbg, stop)
    except _Stop:
        build_program.last = st
        if st['ph'] is not None:
            st['ph'].close()
        es.close()
        return nc


def _build(nc, es, st, dbg, stop):

    def din(name, shape, dt=F32):
        return nc.dram_tensor(name, list(shape), dt, kind="ExternalInput").ap()

    def dscr(name, shape, dt=BF16, out=False):
        if name in dbg:
            return nc.dram_tensor(name, list(shape), dt, kind="ExternalOutput").ap()
        return nc.dram_tensor(name, list(shape), dt).ap()

    x_in = din("x_own", [T, D])
    cT_in = din("cT", [128, 8, 2])
    w_ada = din("w_ada", [DEPTH, D, 3 * D])
    b_ada = din("b_ada", [DEPTH, 3 * D])
    g_norm = din("g_norm", [DEPTH, D])
    w_in = din("w_in", [DEPTH, D, INW])
    b_gate = din("b_gate", [DEPTH, 128, 16])
    g_cq = din("g_cq", [DEPTH, 128, 3])
    w_uq = din("w_uq", [DEPTH, 384, 768])
    g_ckv = din("g_ckv", [DEPTH, 128, 2])
    w_ukv = din("w_ukv", [DEPTH, 256, 1024])
    w_pa = din("w_pa", [DEPTH, 512, D])
    w_pb = din("w_pb", [DEPTH, 512, D])
    w_out = din("w_out", [DEPTH, D, D])
    g_final = din("g_final", [D])
    tabs = din("tabs", [NT, 4, 128, TT])
    masks_in = din("maskb", [128, 512], BF16)
    ident_in = din("ident", [128, 128], BF16)
    sel_in = din("sel", [2, 2, 128])
    flags_in = din("flags", [128, 2])
    y_out = nc.dram_tensor("y", [T, D], F32, kind="ExternalOutput").ap()

    xres = dscr("xres", [T, D], F32)
    modbc = dscr("modbc", [DEPTH, 2, 3, 128, D], F32)
    hTs = dscr("hTs", [D, T])
    qm = dscr("qm", [H * 96, T])
    km = dscr("km", [H * 96, SEG])
    vm = dscr("vm", [512, SEG])
    km_sh = dscr("km_sh", [H * 96, SEG])
    vm_sh = dscr("vm_sh", [512, SEG])
    kmg = dscr("kmg", [H, 4 * 96, SEG])
    vmg = dscr("vmg", [4, 4 * 128, SEG])
    qd = dscr("qd", [3, 512, T])
    kd = dscr("kd", [3, 512, T])
    vd = dscr("vd", [3, 512, T])
    kd_sh = [[dscr(f"kd_sh{g}_{s}", [512, HW[g]]) for s in range(2)] for g in range(3)]
    vd_sh = [[dscr(f"vd_sh{g}_{s}", [512, HW[g]]) for s in range(2)] for g in range(3)]
    kdg = [[dscr(f"kdg{g}_{s}", [4 * 512, HW[g]]) for s in range(2)] for g in range(3)]
    vdg = [[dscr(f"vdg{g}_{s}", [4 * 512, HW[g]]) for s in range(2)] for g in range(3)]
    halo_k = [[dscr(f"halo_k{g}_{s}", [512, HW[g]]) for s in range(2)] for g in range(3)]
    halo_v = [[dscr(f"halo_v{g}_{s}", [512, HW[g]]) for s in range(2)] for g in range(3)]
    zm = dscr("zm", [512, T])
    zd = dscr("zd", [512, T])
    gab = dscr("gab", [2048, T])
    om = dscr("om", [512, T])
    od = dscr("od", [512, T])

    S = Sched(nc, es)
    st['S'] = S

    def chk(tag):
        if stop == tag:
            S.barrier()
            raise _Stop()
    pid = nc.sync.partition_id()
    rank = pid % 4
    ppid = nc.gpsimd.partition_id()
    rank_l = nc.gpsimd.snap((ppid + 3) % 4, min_val=0, max_val=3)
    rank_r = nc.gpsimd.snap((ppid + 1) % 4, min_val=0, max_val=3)

    uniq = [0]

    def sb(name, shape, dt):
        uniq[0] += 1
        return es_ph.enter_context(nc.sbuf_tensor("s%d_%s" % (uniq[0], name), list(shape), dt))

    def ps(name, shape, dt=F32):
        uniq[0] += 1
        return es_ph.enter_context(nc.psum_tensor("p%d_%s" % (uniq[0], name), list(shape), dt))

    B_modbc = Buf("modbc")
    B_scr = Buf("scr")
    B_share = Buf("share")
    B_gath = Buf("gath")
    B_halo = Buf("halo")
    dummy = Buf("dummy")
    dummy.keep = True
    dummy2 = Buf("dummy2")
    dummy2.keep = True

    es_ph = es
    ident = sb("ident", [128, 128], BF16); Bc = Buf("consts"); Bc.keep = True
    onesb = sb("onesb", [128, 128], BF16)
    onesf = sb("onesf", [128, 128], F32)
    maskb = sb("maskb", [128, 512], BF16)
    sel = sb("sel", [2, 2, 128], F32)
    flags = sb("flags", [128, 2], F32)
    S.dma('sp', ident[:], ident_in[:], [], [Bc], Bc)
    S.dma('sp', maskb[:], masks_in[:], [], [Bc], Bc, accum=True)
    S.dma('sp', sel[:], sel_in[:], [], [Bc], Bc, accum=True)
    S.dma('sp', flags[:], flags_in[:], [], [Bc], Bc, accum=True)
    Bc2 = Buf("consts2")
    S.run('dve', [], [Bc2], lambda e: (e.memset(onesb[:], 1.0), e.memset(onesf[:], 1.0))[1])

    es_ph = ExitStack()
    st['ph'] = es_ph
    cT = sb("cT", [128, 8, 2], F32); B_cT = Buf("cT")
    scT = sb("scT", [128, 8, 2], F32); B_scT = Buf("scT")
    wst = [sb(f"wst{i}", [128, 8, 512], F32) for i in range(2)]; B_wst = [Buf(f"wst{i}") for i in range(2)]
    bada = sb("bada", [1, 3 * D], F32); B_bada = Buf("bada")
    gn = sb("gn", [1, D], F32); B_gn = Buf("gn")
    modsb = sb("modsb", [2, 3 * D], F32); B_modsb = Buf("modsb")
    gnbc = sb("gnbc", [128, D], F32); B_gnbc = Buf("gnbc")
    bct = [sb(f"bct{i}", [128, 512], F32) for i in range(2)]; B_bct = [Buf(f"bct{i}") for i in range(2)]
    pmod = ps("pmod", [128, 512]); B_pmod = Buf("pmod")
    pbc = ps("pbc", [128, 512]); B_pbc = Buf("pbc")

    S.dma('sp', cT[:], cT_in[:], [], [B_cT], B_cT)
    S.run('act', [B_cT], [B_scT], lambda e: e.activation(out=scT[:], in_=cT[:], func=AF.Silu))
    it = 0
    for l in range(DEPTH):
        S.dma('sp', bada[:], b_ada[l:l + 1, :], [], [B_bada], B_bada)
        S.dma('sp', gn[:], g_norm[l:l + 1, :], [], [B_gn], B_gn)
        for j in range(6):
            w = it % 2
            it += 1
            S.dma('sp', wst[w][:], w_ada[l, :, j * 512:(j + 1) * 512].rearrange("(k p) c -> p k c", p=128),
                  [], [B_wst[w]], B_wst[w])

            def f(e, w=w, j=j):
                for k in range(8):
                    e.matmul(pmod[0:2, :], lhsT=scT[:, k, :], rhs=wst[w][:, k, :], start=(k == 0), stop=False)
                return e.matmul(pmod[0:2, :], lhsT=onesf[0:1, 0:2], rhs=bada[0:1, j * 512:(j + 1) * 512],
                                start=False, stop=True)
            S.run('pe', [B_scT, B_wst[w], B_bada, Bc2], [B_pmod], f)
            S.run('act', [B_pmod], [B_modsb],
                  lambda e, j=j: e.activation(out=modsb[0:2, j * 512:(j + 1) * 512], in_=pmod[0:2, :], func=AF.Copy))
        for hf in range(2):
            S.run('pe', [B_gn, Bc2], [B_pbc],
                  lambda e, hf=hf: e.matmul(pbc[:, :], lhsT=onesf[0:1, :], rhs=gn[0:1, hf * 512:(hf + 1) * 512],
                                            start=True, stop=True))
            S.run('act', [B_pbc], [B_gnbc],
                  lambda e, hf=hf: e.activation(out=gnbc[:, hf * 512:(hf + 1) * 512], in_=pbc[:, :], func=AF.Copy))
        for b in range(2):
            for kind in range(3):
                dst = {0: 1, 1: 0, 2: 2}[kind]
                for hf in range(2):
                    w = it % 2
                    it += 1
                    S.run('pe', [B_modsb, Bc], [B_pbc],
                          lambda e, b=b, kind=kind, hf=hf: e.matmul(
                              pbc[:, :], lhsT=sel[0:2, b, :],
                              rhs=modsb[0:2, kind * D + hf * 512: kind * D + (hf + 1) * 512], start=True, stop=True))
                    if kind == 1:
                        S.run('dve', [B_pbc, B_gnbc], [B_bct[w]],
                              lambda e, w=w, hf=hf: e.scalar_tensor_tensor(
                                  out=bct[w][:], in0=pbc[:, :], scalar=1.0, in1=gnbc[:, hf * 512:(hf + 1) * 512],
                                  op0=ALU.add, op1=ALU.mult))
                    else:
                        S.run('act', [B_pbc], [B_bct[w]],
                              lambda e, w=w: e.activation(out=bct[w][:], in_=pbc[:, :], func=AF.Copy))
                    S.dma('sp', modbc[l, b, dst, :, hf * 512:(hf + 1) * 512], bct[w][:], [B_bct[w]], [B_modbc],
                          B_bct[w], accum=True)
    S.barrier()
    es_ph.close()
    if stop == "P":
        es.close()
        return nc

    for l in range(DEPTH):
        x_src = x_in if l == 0 else xres
        tile_order = list(range(8, 16)) + list(range(0, 8))

        for pa in range(2):
            es_ph = ExitStack()
            st['ph'] = es_ph
            if pa == 0:
                segs = [(0, 672), (C_KR, 32), (C_ZM, 512), (C_DIL, 1536)]
                L_CQ, L_CKV, L_KR, L_ZM, L_G = 0, 384, 640, 704, 1216
                NWC = 2752
            else:
                segs = [(C_DIL + 1536, 3072), (C_ZD, 512), (C_MG, 2048)]
                L_G, L_ZD, L_MG = 0, 3072, 3584
                NWC = 5632
            wbf = sb("wbf", [128, 8, NWC], BF16); B_wbf = Buf("wbf")
            wstg = [sb(f"wstg{i}", [128, 8, 256], F32) for i in range(2)]; B_wstg = [Buf(f"wstg{i}") for i in range(2)]
            it = 0
            lo = 0
            for (c0, ncol) in segs:
                for cc in range(0, ncol, 256):
                    n = min(256, ncol - cc)
                    w = it % 2
                    S.dma('sp', wstg[w][:, :, 0:n],
                          w_in[l, :, c0 + cc:c0 + cc + n].rearrange("(k p) c -> p k c", p=128),
                          [], [B_wstg[w]], B_wstg[w])
                    dst_lo = lo + cc
                    if pa == 0 and c0 == C_KR and ncol == 32:
                        def f(e, w=w, dst_lo=dst_lo):
                            e.tensor_copy(out=wbf[:, :, dst_lo:dst_lo + 16], in_=wstg[w][:, :, 16:32])
                            return e.tensor_copy(out=wbf[:, :, dst_lo + 16:dst_lo + 32], in_=wstg[w][:, :, 0:16])
                        S.run('dve', [B_wstg[w]], [B_wbf], f)
                    else:
                        eng = 'dve' if it % 2 == 0 else 'act'
                        if eng == 'dve':
                            S.run('dve', [B_wstg[w]], [B_wbf], lambda e, w=w, dst_lo=dst_lo, n=n: e.tensor_copy(
                                out=wbf[:, :, dst_lo:dst_lo + n], in_=wstg[w][:, :, 0:n]))
                        else:
                            S.run('act', [B_wstg[w]], [B_wbf], lambda e, w=w, dst_lo=dst_lo, n=n: e.activation(
                                out=wbf[:, :, dst_lo:dst_lo + n], in_=wstg[w][:, :, 0:n], func=AF.Copy))
                    it += 1
                lo += ncol
            assert lo == NWC
            chk("A%dw" % pa)

            tbs = [sb(f"tb{i}", [128, 4, TT], F32) for i in range(2)]; B_tbs = [Buf(f"tb{i}") for i in range(2)]
            tb = tbs[0]; B_tb = B_tbs[0]
            hT = [sb(f"hT{i}", [128, 8, TT], BF16) for i in range(2)]; B_hT = [Buf(f"hT{i}") for i in range(2)]
            pmm = [ps(f"pmm{i}", [128, 512]) for i in range(4)]; B_pmm = [Buf(f"pmm{i}") for i in range(4)]
            stg = [sb(f"stg{i}", [128, TT], BF16) for i in range(6)]; B_stg = [Buf(f"stg{i}") for i in range(6)]
            ra = [sb(f"ra{i}", [128, TT], F32) for i in range(2)]; B_ra = [Buf(f"ra{i}") for i in range(2)]
            rt = [sb(f"rt{i}", [128, TT], F32) for i in range(2)]; B_rt = [Buf(f"rt{i}") for i in range(2)]
            cnt = {'pmm': 0, 'stg': 0, 'r': 0}

            if pa == 0:
                xt = [sb(f"xt{i}", [128, D], F32) for i in range(3)]; B_xt = [Buf(f"xt{i}") for i in range(3)]
                junk = sb("junk", [128, D], F32); B_junk = Buf("junk")
                hb = [sb(f"hb{i}", [128, D], BF16) for i in range(4)]; B_hb = [Buf(f"hb{i}") for i in range(4)]
                ssq = sb("ssq", [128, 8], F32); B_ssq = Buf("ssq")
                rstd = sb("rstd", [128, 8], F32); B_rstd = Buf("rstd")
                gsb = sb("gsb", [128, D], F32); shb = sb("shb", [128, D], F32); B_mod = Buf("modt")
                ptr = ps("ptr", [128, 1024], BF16); B_ptr = Buf("ptr")
                pss = ps("pss", [128, 512]); B_pss = Buf("pss")
                cqf = sb("cqf", [128, 5, TT], F32); B_cqf = Buf("cqf")
                sqb = sb("sqb", [128, 5, TT], BF16); B_sqb = Buf("sqb")
                rsq = sb("rsq", [128, 2, TT], F32); B_rsq = Buf("rsq")
                cn = sb("cn", [128, 5, TT], BF16); B_cn = Buf("cn")
                wuq = sb("wuq", [128, 3, H * 128], BF16); B_wuq = Buf("wuq")
                wuk = sb("wuk", [128, 2, 512], BF16); wuv = sb("wuv", [128, 2, 512], BF16); B_wukv = Buf("wukv")
                gq = sb("gq", [128, 3], F32); gkv = sb("gkv", [128, 2], F32); B_g = Buf("gqkv")
                kst = sb("kst", [32, TT], BF16); B_kst = Buf("kst")
                S.dma('sp', gq[:], g_cq[l], [], [B_g], B_g)
                S.dma('sp', gkv[:], g_ckv[l], [], [B_g], B_g, accum=True)
                for k in range(3):
                    w = it % 2
                    it += 1
                    for part in range(3):
                        S.dma('sp', wstg[w][:, part, 0:256], w_uq[l, k * 128:(k + 1) * 128, part * 256:(part + 1) * 256],
                              [], [B_wstg[w]], B_wstg[w], accum=(part > 0))

                    def f(e, w=w, k=k):
                        src = wstg[w][:, 0:3, 0:256].rearrange("p a c -> p (a c)").rearrange("p (h c) -> p h c", c=96)
                        dst = wuq[:, k, :].rearrange("p (h c) -> p h c", c=128)
                        e.tensor_scalar(out=dst[:, :, 0:96], in0=src, scalar1=gq[:, k:k + 1], scalar2=None, op0=ALU.mult)
                        e.tensor_scalar(out=dst[:, :, 96:112], in0=src[:, :, 80:96], scalar1=gq[:, k:k + 1], scalar2=None,
                                        op0=ALU.mult)
                        return e.tensor_scalar(out=dst[:, :, 112:128], in0=src[:, :, 64:80], scalar1=gq[:, k:k + 1],
                                               scalar2=None, op0=ALU.mult)
                    S.run('dve', [B_wstg[w], B_g], [B_wuq], f)
                for k in range(2):
                    w = it % 2
                    it += 1
                    for part in range(4):
                        S.dma('sp', wstg[w][:, part, 0:256], w_ukv[l, k * 128:(k + 1) * 128, part * 256:(part + 1) * 256],
                              [], [B_wstg[w]], B_wstg[w], accum=(part > 0))

                    def f(e, w=w, k=k):
                        src = wstg[w][:, 0:4, 0:256].rearrange("p a (h2 c) -> p (a h2) c", c=128)
                        e.tensor_scalar(out=wuk[:, k, :].rearrange("p (h c) -> p h c", c=64), in0=src[:, :, 0:64],
                                        scalar1=gkv[:, k:k + 1], scalar2=None, op0=ALU.mult)
                        return e.tensor_scalar(out=wuv[:, k, :].rearrange("p (h c) -> p h c", c=64), in0=src[:, :, 64:128],
                                               scalar1=gkv[:, k:k + 1], scalar2=None, op0=ALU.mult)
                    S.run('dve', [B_wstg[w], B_g], [B_wukv], f)
                chk("A0u")
            else:
                bg = sb("bg", [128, 16], F32); B_bg = Buf("bg")
                S.dma('sp', bg[:], b_gate[l], [], [B_bg], B_bg)

            def next_pmm():
                i = cnt['pmm'] % 4
                cnt['pmm'] += 1
                return i

            def next_stg():
                i = cnt['stg'] % 6
                cnt['stg'] += 1
                return i

            def inproj(pi, hi, lcol, m=128):
                def f(e):
                    for k in range(8):
                        ins = e.matmul(pmm[pi][0:m, :], lhsT=wbf[:, k, lcol:lcol + m], rhs=hT[hi][:, k, :],
                                       start=(k == 0), stop=(k == 7))
                    return ins
                S.run('pe', [B_wbf, B_hT[hi]], [B_pmm[pi]], f)

            def store(si, dst_ap, rows=128, q='sp'):
                S.dma(q, dst_ap, stg[si][0:rows, :], [B_stg[si]], [B_scr], B_stg[si], accum=True)

            def rope2(pi, dst_ap):
                ri = cnt['r'] % 2
                cnt['r'] += 1
                si = next_stg()
                S.run('dve', [B_pmm[pi], B_tb], [B_ra[ri]],
                      lambda e: e.tensor_tensor(out=ra[ri][:], in0=pmm[pi][:, :], in1=tb[:, 0, :], op=ALU.mult))

                def f(e):
                    for blk in range(4):
                        src = (blk ^ 1) * 32
                        ins = e.tensor_tensor(out=rt[ri][blk * 32:(blk + 1) * 32, :], in0=pmm[pi][src:src + 32, :],
                                              in1=tb[src:src + 32, 1, :], op=ALU.mult)
                    return ins
                S.run('dve', [B_pmm[pi], B_tb], [B_rt[ri]], f)
                S.run('pool', [B_ra[ri], B_rt[ri]], [B_stg[si]],
                      lambda e: e.tensor_tensor(out=stg[si][:], in0=ra[ri][:], in1=rt[ri][:], op=ALU.add))
                store(si, dst_ap)

            def actout(pi, dst_ap, func, bias=None, q='sp'):
                si = next_stg()
                if bias is None:
                    S.run('act', [B_pmm[pi]], [B_stg[si]],
                          lambda e: e.activation(out=stg[si][:], in_=pmm[pi][:, :], func=func))
                else:
                    S.run('act', [B_pmm[pi], B_bg], [B_stg[si]],
                          lambda e: e.activation(out=stg[si][:], in_=pmm[pi][:, :], func=func, bias=bias))
                store(si, dst_ap, q=q)

            segstate = {'cur': -1}

            def prepA(ti, s4):
                tile = tile_order[ti]
                seg = tile // 8
                t0 = tile * TT
                if s4 == 0 and seg != segstate['cur']:
                    segstate['cur'] = seg
                    S.dma('sp', gsb[:], modbc[l, seg, 0], [B_modbc], [B_mod], B_mod)
                    S.dma('sp', shb[:], modbc[l, seg, 1], [B_modbc], [B_mod], B_mod, accum=True)
                xi = (ti * 4 + s4) % 3
                hbi = s4
                S.dma('sp', xt[xi][:], x_src[t0 + s4 * 128:t0 + (s4 + 1) * 128, :], [], [B_xt[xi]], B_xt[xi])
                col = (ti % 2) * 4 + s4
                S.run('act', [B_xt[xi]], [B_junk, B_ssq],
                      lambda e: e.activation(out=junk[:], in_=xt[xi][:], func=AF.Square, accum_out=ssq[:, col:col + 1]))
                S.run('dve', [B_ssq], [B_rstd], lambda e: e.tensor_scalar(
                    out=rstd[:, col:col + 1], in0=ssq[:, col:col + 1], scalar1=1.0 / D, scalar2=EPS,
                    op0=ALU.mult, op1=ALU.add))
                S.run('act', [B_rstd], [B_rstd], lambda e: e.activation(
                    out=rstd[:, col:col + 1], in_=rstd[:, col:col + 1], func=AF.Sqrt))
                S.run('dve', [B_rstd], [B_rstd], lambda e: e.reciprocal(
                    out=rstd[:, col:col + 1], in_=rstd[:, col:col + 1]))
                S.run('dve', [B_xt[xi], B_rstd, B_mod], [B_junk],
                      lambda e: e.scalar_tensor_tensor(
                          out=junk[:], in0=xt[xi][:], scalar=rstd[:, col:col + 1], in1=gsb[:],
                          op0=ALU.mult, op1=ALU.mult))
                S.run('pool', [B_junk, B_mod], [B_hb[hbi]],
                      lambda e: e.tensor_tensor(out=hb[hbi][:], in0=junk[:], in1=shb[:], op=ALU.add))

            def prepB(ti, s4):
                hi = ti % 2
                hbi = s4

                def f(e):
                    for k in range(8):
                        ins = e.transpose(ptr[:, k * 128:(k + 1) * 128], hb[hbi][:, k * 128:(k + 1) * 128], ident[:])
                    return ins
                S.run('pe', [B_hb[hbi], Bc], [B_ptr], f)
                S.run('act', [B_ptr], [B_hT[hi]],
                      lambda e: e.activation(out=hT[hi][:, :, s4 * 128:(s4 + 1) * 128],
                                             in_=ptr[:, :].rearrange("p (k t) -> p k t", t=128), func=AF.Copy))
                if s4 == 3:
                    t0 = tile_order[ti] * TT
                    S.dma('sp', hTs[:, t0:t0 + TT].rearrange("(k p) t -> p k t", p=128), hT[hi][:], [B_hT[hi]], [B_scr],
                          B_hT[hi], accum=True)

            def hook(ti, where):
                if pa != 0 or ti + 1 >= NT:
                    return
                n = ti + 1
                if where == 'start':
                    prepA(n, 0); prepA(n, 1)
                elif where == 'lat':
                    prepA(n, 2); prepA(n, 3)
                elif where == 'zm':
                    prepB(n, 0)
                elif where == 'dq':
                    prepB(n, 1)
                elif where == 'dk':
                    prepB(n, 2)
                elif where == 'dv':
                    prepB(n, 3)

            if pa == 0:
                for s4 in range(4):
                    prepA(0, s4)
                for s4 in range(4):
                    prepB(0, s4)
            else:
                S.dma('sp', hT[0][:], hTs[:, tile_order[0] * TT:tile_order[0] * TT + TT].rearrange("(k p) t -> p k t", p=128), [B_scr], [B_hT[0]], B_hT[0])
            for ti, tile in enumerate(tile_order):
                if ti > 0:
                    chk("A%dt%d" % (pa, ti - 1))
                seg = tile // 8
                t0 = tile * TT
                hi = ti % 2
                tb = tbs[ti % 2]; B_tb = B_tbs[ti % 2]
                if ti == 0:
                    S.dma('sp', tb[:], tabs[tile].rearrange("a p t -> p a t"), [], [B_tb], B_tb)
                if ti + 1 < NT:
                    S.dma('sp', tbs[(ti + 1) % 2][:], tabs[tile_order[ti + 1]].rearrange("a p t -> p a t"), [],
                          [B_tbs[(ti + 1) % 2]], B_tbs[(ti + 1) % 2])
                if pa == 0:
                    hook(ti, 'start')
                    chk("A0h")
                    for j in range(5):
                        pi = next_pmm()
                        inproj(pi, hi, L_CQ + j * 128)
                        S.run('act', [B_pmm[pi]], [B_cqf],
                              lambda e, j=j, pi=pi: e.activation(out=cqf[:, j, :], in_=pmm[pi][:, :], func=AF.Copy))
                        S.run('act', [B_pmm[pi]], [B_sqb],
                              lambda e, j=j, pi=pi: e.activation(out=sqb[:, j, :], in_=pmm[pi][:, :], func=AF.Square))
                    for which, (j0, nj, nfeat) in enumerate(((0, 3, 384), (3, 2, 256))):
                        def f(e, j0=j0, nj=nj):
                            for j in range(nj):
                                ins = e.matmul(pss[:, :], lhsT=onesb[:], rhs=sqb[:, j0 + j, :], start=(j == 0),
                                               stop=(j == nj - 1))
                            return ins
                        S.run('pe', [B_sqb, Bc2], [B_pss], f)

                        S.run('dve', [B_pss], [B_rsq], lambda e, which=which, nfeat=nfeat: e.tensor_scalar(
                            out=rsq[:, which, :], in0=pss[:, :], scalar1=1.0 / nfeat, scalar2=EPS, op0=ALU.mult, op1=ALU.add))
                        S.run('act', [B_rsq], [B_rsq], lambda e, which=which: e.activation(
                            out=rsq[:, which, :], in_=rsq[:, which, :], func=AF.Sqrt))
                        S.run('dve', [B_rsq], [B_rsq], lambda e, which=which: e.reciprocal(
                            out=rsq[:, which, :], in_=rsq[:, which, :]))
                        for j in range(nj):
                            S.run('pool' if j % 2 else 'dve', [B_cqf, B_rsq], [B_cn],
                                  lambda e, j=j, j0=j0, which=which: e.tensor_tensor(
                                      out=cn[:, j0 + j, :], in0=cqf[:, j0 + j, :], in1=rsq[:, which, :], op=ALU.mult))
                    hook(ti, 'lat')
                    chk("A0c")
                    pi = next_pmm()
                    inproj(pi, hi, L_KR, m=64)
                    ri = cnt['r'] % 2
                    cnt['r'] += 1
                    S.run('dve', [B_pmm[pi], B_tb], [B_ra[ri]],
                          lambda e: e.tensor_tensor(out=ra[ri][0:32, :], in0=pmm[pi][0:32, :], in1=tb[0:32, 2, :], op=ALU.mult))
                    S.run('dve', [B_pmm[pi], B_tb], [B_rt[ri]],
                          lambda e: e.tensor_tensor(out=rt[ri][0:32, :], in0=pmm[pi][32:64, :], in1=tb[32:64, 3, :], op=ALU.mult))

                    S.run('pool', [B_ra[ri], B_rt[ri]], [B_kst],
                          lambda e: e.tensor_tensor(out=kst[:, :], in0=ra[ri][0:32, :], in1=rt[ri][0:32, :], op=ALU.add))
                    kdst = km if seg == 0 else km_sh
                    tl = t0 - seg * SEG
                    for h in range(H):
                        S.dma('sp', kdst[h * 96 + 64:h * 96 + 96, tl:tl + TT], kst[:, :], [B_kst],
                              [B_scr if seg == 0 else B_share], B_kst, accum=True)
                    chk("A0k")
                    for j in range(4):
                        pi = next_pmm()
                        inproj(pi, hi, L_ZM + j * 128)
                        actout(pi, zm[j * 128:(j + 1) * 128, t0:t0 + TT], AF.Silu, q='act')
                    glist = (0,)
                else:
                    if ti + 1 < NT:
                        tn = tile_order[ti + 1] * TT
                        S.dma('sp', hT[1 - hi][:], hTs[:, tn:tn + TT].rearrange("(k p) t -> p k t", p=128), [B_scr],
                              [B_hT[1 - hi]], B_hT[1 - hi])
                    glist = (1, 2)
                chk("A%dkv" % pa)
                hook(ti, 'zm')
                for gi, g in enumerate(glist):
                    base = L_G + gi * 1536
                    for j in range(4):
                        pi = next_pmm()
                        inproj(pi, hi, base + j * 128)
                        rope2(pi, qd[g, j * 128:(j + 1) * 128, t0:t0 + TT])
                    if gi == 0:
                        hook(ti, 'dq')
                    for j in range(4):
                        pi = next_pmm()
                        inproj(pi, hi, base + 512 + j * 128)
                        rope2(pi, kd[g, j * 128:(j + 1) * 128, t0:t0 + TT])
                    if gi == 0:
                        hook(ti, 'dk')
                    for j in range(4):
                        pi = next_pmm()
                        inproj(pi, hi, base + 1024 + j * 128)
                        actout(pi, vd[g, j * 128:(j + 1) * 128, t0:t0 + TT], AF.Copy, q='act')
                    if gi == 0:
                        hook(ti, 'dv')
                if pa == 0:
                    chk("A0z")
                    for h in range(H):
                        pi = next_pmm()

                        def f(e, h=h, pi=pi):
                            for k in range(3):
                                ins = e.matmul(pmm[pi][:, :], lhsT=wuq[:, k, h * 128:(h + 1) * 128], rhs=cn[:, k, :],
                                               start=(k == 0), stop=(k == 2))
                            return ins
                        S.run('pe', [B_wuq, B_cn], [B_pmm[pi]], f)
                        si = next_stg()
                        ri = cnt['r'] % 2
                        cnt['r'] += 1
                        S.run('act', [B_pmm[pi]], [B_stg[si]],
                              lambda e, si=si, pi=pi: e.activation(out=stg[si][0:64, :], in_=pmm[pi][0:64, :], func=AF.Copy))
                        S.run('dve', [B_pmm[pi], B_tb], [B_ra[ri]],
                              lambda e, ri=ri, pi=pi: e.tensor_tensor(out=ra[ri][64:96, :], in0=pmm[pi][64:96, :],
                                                                      in1=tb[64:96, 2, :], op=ALU.mult))
                        S.run('dve', [B_pmm[pi], B_tb], [B_rt[ri]],
                              lambda e, ri=ri, pi=pi: e.tensor_tensor(out=rt[ri][64:96, :], in0=pmm[pi][96:128, :],
                                                                      in1=tb[96:128, 3, :], op=ALU.mult))
                        S.run('pool', [B_ra[ri], B_rt[ri]], [B_stg[si]],
                              lambda e, ri=ri, si=si: e.tensor_tensor(out=stg[si][64:96, :], in0=ra[ri][64:96, :],
                                                                      in1=rt[ri][64:96, :], op=ALU.add))
                        store(si, qm[h * 96:(h + 1) * 96, t0:t0 + TT], rows=96)
                    chk("A0q")
                    for j in range(4):
                        pi = next_pmm()

                        def f(e, j=j, pi=pi):
                            for k in range(2):
                                ins = e.matmul(pmm[pi][:, :], lhsT=wuk[:, k, j * 128:(j + 1) * 128], rhs=cn[:, 3 + k, :],
                                               start=(k == 0), stop=(k == 1))
                            return ins
                        S.run('pe', [B_wukv, B_cn], [B_pmm[pi]], f)
                        si = next_stg()
                        S.run('act', [B_pmm[pi]], [B_stg[si]],
                              lambda e, si=si, pi=pi: e.activation(out=stg[si][:], in_=pmm[pi][:, :], func=AF.Copy))
                        for hh in range(2):
                            h = 2 * j + hh
                            S.dma('sp', kdst[h * 96:h * 96 + 64, tl:tl + TT], stg[si][hh * 64:(hh + 1) * 64, :], [B_stg[si]],
                                  [B_scr if seg == 0 else B_share], B_stg[si], accum=True)
                    vdst = vm if seg == 0 else vm_sh
                    for j in range(4):
                        pi = next_pmm()

                        def f(e, j=j, pi=pi):
                            for k in range(2):
                                ins = e.matmul(pmm[pi][:, :], lhsT=wuv[:, k, j * 128:(j + 1) * 128], rhs=cn[:, 3 + k, :],
                                               start=(k == 0), stop=(k == 1))
                            return ins
                        S.run('pe', [B_wukv, B_cn], [B_pmm[pi]], f)
                        si = next_stg()
                        S.run('act', [B_pmm[pi]], [B_stg[si]],
                              lambda e, si=si, pi=pi: e.activation(out=stg[si][:], in_=pmm[pi][:, :], func=AF.Copy))
                        S.dma('act', vdst[j * 128:(j + 1) * 128, tl:tl + TT], stg[si][:], [B_stg[si]],
                              [B_scr if seg == 0 else B_share], B_stg[si], accum=True)
                chk("A%dd" % pa)
                if pa == 0 and 7 <= ti <= 12:
                    cl = [(km_sh[h * 96:(h + 1) * 96, :], kmg[h]) for h in range(H)] + \
                         [(vm_sh[j * 128:(j + 1) * 128, :], vmg[j]) for j in range(4)]
                    for (a_, b_) in cl[2 * (ti - 7):2 * (ti - 7) + 2]:
                        S.collective(a_, b_, [B_share], [B_gath])
                if pa == 1:
                    for j in range(4):
                        pi = next_pmm()
                        inproj(pi, hi, L_ZD + j * 128)
                        actout(pi, zd[j * 128:(j + 1) * 128, t0:t0 + TT], AF.Silu, q='act')
                    for j in range(16):
                        pi = next_pmm()
                        inproj(pi, hi, L_MG + j * 128)
                        actout(pi, gab[j * 128:(j + 1) * 128, t0:t0 + TT], AF.Sigmoid, bias=bg[:, j:j + 1])
                    if ti == 7:
                        chk("A1pre")
                        for g in range(3):
                            for s in range(2):
                                c0 = SEG if s == 0 else T - HW[g]
                                S.dma('sp', kd_sh[g][s][:, :], kd[g, :, c0:c0 + HW[g]], [B_scr], [B_share], dummy2, accum=True)
                                S.dma('sp', vd_sh[g][s][:, :], vd[g, :, c0:c0 + HW[g]], [B_scr], [B_share], dummy2, accum=True)
                    if 7 <= ti <= 12:
                        cl = []
                        for g in (2, 1, 0):
                            for s in range(2):
                                cl.append((kd_sh[g][s], kdg[g][s]))
                                cl.append((vd_sh[g][s], vdg[g][s]))
                        for (a_, b_) in cl[2 * (ti - 7):2 * (ti - 7) + 2]:
                            S.collective(a_, b_, [B_share], [B_gath])
            if pa == 1:
                for g in range(3):
                    for (srcs, dsts) in ((kdg, halo_k), (vdg, halo_v)):
                        S.dma('pool', dsts[g][0][:, :], srcs[g][1].rearrange("(a p) f -> a p f", a=4)[
                            bass.ds(rank_l, 1), :, :].rearrange("a p f -> (a p) f"), [B_gath], [B_halo], dummy, accum=True)
                        S.dma('pool', dsts[g][1][:, :], srcs[g][0].rearrange("(a p) f -> a p f", a=4)[
                            bass.ds(rank_r, 1), :, :].rearrange("a p f -> (a p) f"), [B_gath], [B_halo], dummy, accum=True)
            S.barrier()
            es_ph.close()
            if l == 0 and stop == "A%d" % pa:
                es.close()
                return nc

        es_ph = ExitStack()
        st['ph'] = es_ph
        kt = [sb(f"kt{i}", [96, 4 * SEG], BF16) for i in range(2)]; B_kt = [Buf(f"kt{i}") for i in range(2)]
        vT1 = sb("vT0", [64, 4 * SEG], BF16); B_vT1 = Buf("vT0")
        vt = [sb(f"vt{i}", [128, 128, 65], BF16) for i in range(2)]; B_vt = [Buf(f"vt{i}") for i in range(2)]
        qt = [sb(f"qt{i}", [96, SEG], BF16) for i in range(2)]; B_qt = [Buf(f"qt{i}") for i in range(2)]
        zt = [sb(f"zt{i}", [64, SEG], BF16) for i in range(2)]; B_zt = [Buf(f"zt{i}") for i in range(2)]
        ot = [sb(f"ot{i}", [64, SEG], BF16) for i in range(2)]; B_ot = [Buf(f"ot{i}") for i in range(2)]
        pt = [sb(f"pt{i}", [128, 1024], BF16) for i in range(2)]; B_pt = [Buf(f"pt{i}") for i in range(2)]
        rd = sb("rd", [128, 512], F32); B_rd = Buf("rd")
        tmpo = sb("tmpo", [64, 512], F32); B_tmpo = Buf("tmpo")
        pS = [ps(f"pS{i}", [128, 1024]) for i in range(2)]; B_pS = [Buf(f"pS{i}") for i in range(2)]
        pOs = [ps(f"pO{i}", [128, 512]) for i in range(2)]; B_pOs = [Buf(f"pO{i}") for i in range(2)]
        pB = ps("pB", [128, 512]); B_pB = Buf("pB")
        pT = ps("pT", [128, 1024], BF16); B_pT = Buf("pT")
        for i in range(2):
            S.run('dve', [], [B_vt[i]], lambda e, i=i: e.memset(vt[i][:, :, 64:65], 1.0))

        def b_loads(sh):
            seg, h = sh // H, sh % H
            i2 = sh % 2
            S.dma('sp', qt[i2][:], qm[h * 96:(h + 1) * 96, seg * SEG:(seg + 1) * SEG], [B_scr], [B_qt[i2]], B_qt[i2])
            S.dma('sp', zt[i2][:], zm[h * 64:(h + 1) * 64, seg * SEG:(seg + 1) * SEG], [B_scr], [B_zt[i2]], B_zt[i2])
            if seg == 0:
                S.dma('sp', vT1[:, 0:SEG], vm[h * 64:(h + 1) * 64, :], [B_scr], [B_vT1], B_vT1)
                S.dma('sp', kt[i2][:, 0:SEG], km[h * 96:(h + 1) * 96, :], [B_scr], [B_kt[i2]], B_kt[i2])
            else:
                jj, hh = h // 2, h % 2
                for r in range(4):
                    S.dma('sp', vT1[:, r * SEG:(r + 1) * SEG], vmg[jj, r * 128 + hh * 64:r * 128 + (hh + 1) * 64, :],
                          [B_gath], [B_vT1], B_vT1, accum=(r > 0))
                for r in range(4):
                    S.dma('sp', kt[i2][:, r * SEG:(r + 1) * SEG], kmg[h, r * 96:(r + 1) * 96, :], [B_gath], [B_kt[i2]],
                          B_kt[i2], accum=(r > 0))

        def b_prep(sh, c8lo, c8hi):
            seg = sh // H
            i2 = sh % 2
            nch_ = (SEG if seg == 0 else 4 * SEG) // 128
            for c8 in range(c8lo, min(c8hi, nch_ // 8)):
                def f(e, c8=c8):
                    for c in range(8):
                        ch = c8 * 8 + c
                        ins = e.transpose(pT[:, c * 128:c * 128 + 64], vT1[:, ch * 128:(ch + 1) * 128], ident[0:64, 0:64])
                    return ins
                S.run('pe', [B_vT1, Bc], [B_pT], f)
                S.run('dve', [B_pT], [B_vt[i2]],
                      lambda e, c8=c8: e.tensor_copy(out=vt[i2][:, c8 * 8:(c8 + 1) * 8, 0:64],
                                                     in_=pT[:, :].rearrange("p (c x) -> p c x", x=128)[:, :, 0:64]))

        pending = []

        def flush():
            while pending:
                pending.pop(0)()

        gidx = 0
        b_loads(0)
        b_prep(0, 0, 16)
        for sh in range(2 * H):
            seg, h = sh // H, sh % H
            i2 = sh % 2
            NK = SEG if seg == 0 else 4 * SEG
            nch = NK // 128
            G = nch // 2
            for qi in range(8):
                qs = slice(qi * 512, (qi + 1) * 512)
                pO = pOs[qi % 2]
                B_pO = B_pOs[qi % 2]

                def QK(g, gi):
                    si = gi % 2

                    def f(e):
                        for j in range(2):
                            ch = 2 * g + j
                            ins = e.matmul(pS[si][:, j * 512:(j + 1) * 512], lhsT=kt[i2][:, ch * 128:(ch + 1) * 128],
                                           rhs=qt[i2][:, qs], start=True, stop=True)
                        return ins
                    S.run('pe', [B_kt[i2], B_qt[i2]], [B_pS[si]], f)

                def EXP(g, gi):
                    si, pi_ = gi % 2, gi % 2
                    S.run('act', [B_pS[si]], [B_pt[pi_]],
                          lambda e: e.activation(out=pt[pi_][:], in_=pS[si][:, :], func=AF.Exp, scale=SC_MLA))

                def PV(g, gi):
                    pi_ = gi % 2

                    def f(e):
                        for j in range(2):
                            ch = 2 * g + j
                            ins = e.matmul(pO[0:65, :], lhsT=vt[i2][:, ch, :], rhs=pt[pi_][:, j * 512:(j + 1) * 512],
                                           start=(ch == 0), stop=(ch == nch - 1))
                        return ins
                    S.run('pe', [B_vt[i2], B_pt[pi_]], [B_pO], f)
                QK(0, gidx)
                QK(1, gidx + 1)
                for g in range(G):
                    EXP(g, gidx + g)
                    PV(g, gidx + g)
                    if g + 2 < G:
                        QK(g + 2, gidx + g + 2)
                    if g == 3:
                        flush()
                        if qi == 0 and sh + 1 < 2 * H:
                            b_loads(sh + 1)
                    if sh + 1 < 2 * H and qi >= 2 and g == 5:
                        n8 = (SEG if (sh + 1) // H == 0 else 4 * SEG) // 128 // 8
                        per = (n8 + 5) // 6
                        b_prep(sh + 1, (qi - 2) * per, (qi - 1) * per)
                gidx += G
                S.run('dve', [B_pO], [B_rd], lambda e, pO=pO: e.reciprocal(out=rd[64:65, :], in_=pO[64:65, :]))

                def tail(pO=pO, B_pO=B_pO, qs=qs, i2=i2, last=(qi == 7), h=h, seg=seg):
                    S.run('pe', [B_rd, Bc2], [B_pB],
                          lambda e: e.matmul(pB[0:64, :], lhsT=onesf[64:65, 0:64], rhs=rd[64:65, :], start=True, stop=True))
                    S.run('dve', [B_pO, B_zt[i2]], [B_tmpo],
                          lambda e: e.tensor_tensor(out=tmpo[:, :], in0=pO[0:64, :], in1=zt[i2][:, qs], op=ALU.mult))
                    S.run('dve', [B_tmpo, B_pB], [B_ot[i2]],
                          lambda e: e.tensor_tensor(out=ot[i2][:, qs], in0=tmpo[:, :], in1=pB[0:64, :], op=ALU.mult))
                    if last:
                        S.dma('sp', om[h * 64:(h + 1) * 64, seg * SEG:(seg + 1) * SEG], ot[i2][:], [B_ot[i2]], [B_scr],
                              B_ot[i2], accum=True)
                pending.append(tail)
        flush()
        S.barrier()
        es_ph.close()
        if l == 0 and stop == "B":
            es.close()
            return nc

        es_ph = ExitStack()
        st['ph'] = es_ph
        WK = SEG + 2 * KPAD
        qdt = sb("qdt", [128, 3, SEG], BF16)
        kdt = sb("kdt", [128, 3, WK], BF16)
        vdT = sb("vdT", [128, 3, WK], BF16)
        B_qd2 = [Buf(f"qd2_{i}") for i in range(2)]
        B_kd2 = [Buf(f"kd2_{i}") for i in range(2)]
        B_vd2 = [Buf(f"vd2_{i}") for i in range(2)]
        NCH = [d * (SEG // (128 * d) + 1) for d in DILS]
        CB = [0, NCH[0], NCH[0] + NCH[1]]
        NCHT = sum(NCH)
        vdt = [sb(f"vdt{i}", [128, NCHT, 65], BF16) for i in range(2)]; B_vdt = [Buf(f"vdt{i}") for i in range(2)]
        zdt = [sb(f"zdt{i}", [64, SEG], BF16) for i in range(2)]; B_zdt = [Buf(f"zdt{i}") for i in range(2)]
        odt = [sb(f"odt{i}", [64, SEG], BF16) for i in range(2)]; B_odt = [Buf(f"odt{i}") for i in range(2)]
        osum = sb("osum", [65, SEG], F32); B_osum = Buf("osum")
        rdd = sb("rdd", [65, SEG], F32); B_rdd = Buf("rdd")
        pdt = [sb(f"pdt{i}", [128, 1024], BF16) for i in range(2)]; B_pdt = [Buf(f"pdt{i}") for i in range(2)]
        pS = [ps(f"pSd{i}", [128, 1024]) for i in range(2)]; B_pS = [Buf(f"pSd{i}") for i in range(2)]
        pO = [ps(f"pOd{i}", [128, 512]) for i in range(2)]; B_pO = [Buf(f"pOd{i}") for i in range(2)]
        pB = ps("pBd", [128, 512]); B_pB = Buf("pBd")
        pT = ps("pTd", [128, 1024], BF16); B_pT = Buf("pTd")
        for i in range(2):
            S.run('dve', [], [B_vdt[i]], lambda e, i=i: e.memset(vdt[i][:, :, 64:65], 1.0))

        def slab(bf, g):
            s_ = bf * 3 + g
            return 64 * (s_ % 2), s_ // 2

        def c_loads(sh):
            seg, h = sh // H, sh % H
            bf = sh % 2
            s0 = seg * SEG
            S.dma('sp', zdt[bf][:], zd[h * 64:(h + 1) * 64, s0:s0 + SEG], [B_scr], [B_zdt[bf]], B_zdt[bf])
            for g in range(3):
                p0, sl = slab(bf, g)
                hw = HW[g]
                S.dma('sp', qdt[p0:p0 + 64, sl, :], qd[g, h * 64:(h + 1) * 64, s0:s0 + SEG], [B_scr], [B_qd2[bf]], B_qd2[bf],
                      accum=(g > 0))
                S.dma('sp', kdt[p0:p0 + 64, sl, KPAD:KPAD + SEG], kd[g, h * 64:(h + 1) * 64, s0:s0 + SEG], [B_scr],
                      [B_kd2[bf]], B_kd2[bf], accum=(g > 0))
                S.dma('sp', vdT[p0:p0 + 64, sl, KPAD:KPAD + SEG], vd[g, h * 64:(h + 1) * 64, s0:s0 + SEG], [B_scr],
                      [B_vd2[bf]], B_vd2[bf], accum=(g > 0))
                if seg == 1:
                    S.dma('sp', kdt[p0:p0 + 64, sl, KPAD - hw:KPAD], halo_k[g][0][h * 64:(h + 1) * 64, :], [B_halo],
                          [B_kd2[bf]], B_kd2[bf], accum=True)
                    S.dma('sp', kdt[p0:p0 + 64, sl, KPAD + SEG:KPAD + SEG + hw], halo_k[g][1][h * 64:(h + 1) * 64, :],
                          [B_halo], [B_kd2[bf]], B_kd2[bf], accum=True)
                    S.dma('sp', vdT[p0:p0 + 64, sl, KPAD - hw:KPAD], halo_v[g][0][h * 64:(h + 1) * 64, :], [B_halo],
                          [B_vd2[bf]], B_vd2[bf], accum=True)
                    S.dma('sp', vdT[p0:p0 + 64, sl, KPAD + SEG:KPAD + SEG + hw], halo_v[g][1][h * 64:(h + 1) * 64, :],
                          [B_halo], [B_vd2[bf]], B_vd2[bf], accum=True)

        def c_prep(sh):
            seg, h = sh // H, sh % H
            bf = sh % 2
            if seg == 0:
                def f(e):
                    for g in range(3):
                        p0, sl = slab(bf, g)
                        hw = HW[g]
                        e.memset(kdt[p0:p0 + 64, sl, KPAD - hw:KPAD], 0.0)
                        e.memset(kdt[p0:p0 + 64, sl, KPAD + SEG:KPAD + SEG + hw], 0.0)
                        e.memset(vdT[p0:p0 + 64, sl, KPAD - hw:KPAD], 0.0)
                        ins = e.memset(vdT[p0:p0 + 64, sl, KPAD + SEG:KPAD + SEG + hw], 0.0)
                    return ins
                S.run('pool', [], [B_kd2[bf], B_vd2[bf]], f)
            else:
                def f(e):
                    for g in range(3):
                        p0, sl = slab(bf, g)
                        hw = HW[g]
                        e.tensor_scalar(out=vdT[p0:p0 + 64, sl, KPAD - hw:KPAD], in0=vdT[p0:p0 + 64, sl, KPAD - hw:KPAD],
                                        scalar1=flags[p0:p0 + 64, 0:1], scalar2=None, op0=ALU.mult)
                        ins = e.tensor_scalar(out=vdT[p0:p0 + 64, sl, KPAD + SEG:KPAD + SEG + hw],
                                              in0=vdT[p0:p0 + 64, sl, KPAD + SEG:KPAD + SEG + hw],
                                              scalar1=flags[p0:p0 + 64, 1:2], scalar2=None, op0=ALU.mult)
                    return ins
                S.run('pool', [Bc], [B_vd2[bf]], f)
            chunks = []
            for g, d in enumerate(DILS):
                ntg = SEG // (128 * d)
                for r in range(d):
                    for t in range(ntg + 1):
                        start = KPAD + r + d * (128 * t - 64)
                        chunks.append((g, CB[g] + r * (ntg + 1) + t, start, d))
            groups8 = []
            for g in range(3):
                cg = [c for c in chunks if c[0] == g]
                for c8 in range(0, len(cg), 8):
                    groups8.append(cg[c8:c8 + 8])
            for grp in groups8:

                def f(e, grp=grp):
                    for c, (g, ci, start, d) in enumerate(grp):
                        p0, sl = slab(bf, g)
                        ins = e.transpose(pT[:, c * 128:c * 128 + 64], vdT[p0:p0 + 64, sl, start:start + 127 * d + 1:d],
                                          ident[p0:p0 + 64, p0:p0 + 64])
                    return ins
                S.run('pe', [B_vd2[bf], Bc], [B_pT], f)
                n = len(grp)
                ci0 = grp[0][1]
                S.run('dve', [B_pT], [B_vdt[bf]],
                      lambda e, n=n, ci0=ci0: e.tensor_copy(out=vdt[bf][:, ci0:ci0 + n, 0:64],
                                                            in_=pT[:, 0:n * 128].rearrange("p (c x) -> p c x", x=128)[:, :, 0:64]))

            def f(e):
                ins = None
                for g, d in enumerate(DILS):
                    ntg = SEG // (128 * d)
                    v = vdt[bf][:, CB[g]:CB[g] + NCH[g], :].rearrange("p (r t) x -> p r t x", t=ntg + 1)
                    if seg == 0:
                        e.memset(v[0:64, :, 0, 64:65], 0.0)
                        ins = e.memset(v[64:128, :, ntg, 64:65], 0.0)
                    else:
                        e.tensor_copy(out=v[0:64, :, 0, 64:65], in_=flags[0:64, 0:1].to_broadcast([64, d, 1]))
                        ins = e.tensor_copy(out=v[64:128, :, ntg, 64:65], in_=flags[64:128, 1:2].to_broadcast([64, d, 1]))
                return ins
            S.run('dve', [Bc], [B_vdt[bf]], f)

        bi = 0
        c_loads(0)
        chk("C0l")
        c_prep(0)
        chk("C0p")
        for sh in range(2 * H):
            if sh > 0:
                chk("C%de" % (sh - 1))
            seg, h = sh // H, sh % H
            bf = sh % 2
            s0 = seg * SEG
            if sh + 1 < 2 * H:
                c_loads(sh + 1)
            batches = []
            for g, d in enumerate(DILS):
                ntg = SEG // (128 * d)
                tiles = [(r, t) for r in range(d) for t in range(ntg)]
                for b4 in range(0, len(tiles), 4):
                    batches.append((g, d, ntg, tiles[b4:b4 + 4]))

            def QK(bidx, si):
                g, d, ntg, batch = batches[bidx]
                p0, sl = slab(bf, g)

                def f(e):
                    for bk in range(2):
                        e.matmul(pS[si][:, bk * 512:(bk + 1) * 512], lhsT=ident[:, :], rhs=maskb[:, :], start=True, stop=False,
                                 skip_group_check=True)
                    for n, (r, t) in enumerate(batch):
                        qstart = r + d * 128 * t
                        for ab in range(2):
                            kstart = KPAD + r + d * (128 * (t + ab) - 64)
                            o_ = pS[si][:, n * 256 + ab * 128:n * 256 + (ab + 1) * 128]
                            ins = e.matmul(o_, lhsT=kdt[p0:p0 + 64, sl, kstart:kstart + 127 * d + 1:d],
                                           rhs=qdt[p0:p0 + 64, sl, qstart:qstart + 127 * d + 1:d], start=False,
                                           stop=(n % 2 == 1 and ab == 1), skip_group_check=True)
                    return ins
                S.run('pe', [B_kd2[bf], B_qd2[bf], Bc], [B_pS[si]], f)

            def EXP(bidx, si):
                S.run('act', [B_pS[si]], [B_pdt[si]],
                      lambda e: e.activation(out=pdt[si][:], in_=pS[si][:, :], func=AF.Exp, scale=SC_DIL))

            def PV(bidx, si):
                g, d, ntg, batch = batches[bidx]

                def f(e):
                    for n, (r, t) in enumerate(batch):
                        for ab in range(2):
                            ci = CB[g] + r * (ntg + 1) + t + ab
                            ins = e.matmul(pO[si][0:65, n * 128:(n + 1) * 128], lhsT=vdt[bf][:, ci, :],
                                           rhs=pdt[si][:, n * 256 + ab * 128:n * 256 + (ab + 1) * 128],
                                           start=(ab == 0), stop=(ab == 1))
                    return ins
                S.run('pe', [B_vdt[bf], B_pdt[si]], [B_pO[si]], f)

            def OSUM(bidx, si):
                g, d, ntg, batch = batches[bidx]
                r0, t0 = batch[0]
                if g == 0:
                    dst = osum[0:65, 128 * t0:128 * t0 + 512]
                    src = pO[si][0:65, :]
                elif g == 1:
                    st_ = r0 + 512 * t0
                    dst = osum[0:65, st_:st_ + 4 * 511 + 1:4]
                    src = pO[si][0:65, :]
                else:
                    dst = osum[0:65, :].rearrange("p (n d) -> p d n", d=16)[:, r0:r0 + 2, :]
                    src = pO[si][0:65, :].rearrange("p (a b) -> p a b", a=2)
                if g == 0:
                    S.run('dve', [B_pO[si]], [B_osum], lambda e: e.tensor_copy(out=dst, in_=src))
                else:
                    S.run('dve', [B_pO[si], B_osum], [B_osum], lambda e: e.tensor_tensor(out=dst, in0=dst, in1=src, op=ALU.add))

            nb = len(batches)
            QK(0, bi % 2)
            for b_ in range(nb):
                si = (bi + b_) % 2
                EXP(b_, si)
                if b_ + 1 < nb:
                    QK(b_ + 1, (bi + b_ + 1) % 2)
                PV(b_, si)
                OSUM(b_, si)
            bi += nb
            chk("C%db" % sh)
            if sh + 1 < 2 * H:
                c_prep(sh + 1)
            S.run('act', [B_osum], [B_rdd], lambda e: e.activation(out=rdd[64:65, :], in_=osum[64:65, :], func=AF.Ln))
            S.run('act', [B_rdd], [B_rdd], lambda e: e.activation(out=rdd[64:65, :], in_=rdd[64:65, :], func=AF.Exp, scale=-1.0))
            for qi in range(8):
                qs = slice(qi * 512, (qi + 1) * 512)
                S.run('pe', [B_rdd, Bc2], [B_pB],
                      lambda e, qs=qs: e.matmul(pB[0:64, :], lhsT=onesf[64:65, 0:64], rhs=rdd[64:65, qs], start=True, stop=True))
                S.run('pool', [B_osum, B_zdt[bf]], [B_osum],
                      lambda e, qs=qs: e.tensor_tensor(out=osum[0:64, qs], in0=osum[0:64, qs], in1=zdt[bf][:, qs], op=ALU.mult))
                S.run('dve', [B_osum, B_pB], [B_odt[bf]],
                      lambda e, qs=qs: e.tensor_tensor(out=odt[bf][:, qs], in0=osum[0:64, qs], in1=pB[0:64, :], op=ALU.mult))
            S.dma('sp', od[h * 64:(h + 1) * 64, s0:s0 + SEG], odt[bf][:], [B_odt[bf]], [B_scr], B_odt[bf], accum=True)
        S.barrier()
        es_ph.close()
        if l == 0 and stop == "C":
            es.close()
            return nc

        es_ph = ExitStack()
        st['ph'] = es_ph
        wpa = sb("wpa", [128, 4, D], BF16); wpb = sb("wpb", [128, 4, D], BF16); wo = sb("wo", [128, 8, D], BF16)
        B_wd = Buf("wd")
        wstg = [sb(f"wstgd{i}", [128, 1024], F32) for i in range(2)]; B_wstg = [Buf(f"wstgd{i}") for i in range(2)]
        it = 0
        for (src, dstt, nk) in ((w_pa, wpa, 4), (w_pb, wpb, 4), (w_out, wo, 8)):
            for k in range(nk):
                w = it % 2
                it += 1
                S.dma('sp', wstg[w][:], src[l, k * 128:(k + 1) * 128, :], [], [B_wstg[w]], B_wstg[w])
                S.run('dve' if it % 2 else 'act', [B_wstg[w]], [B_wd],
                      (lambda e, w=w, dstt=dstt, k=k: e.tensor_copy(out=dstt[:, k, :], in_=wstg[w][:])) if it % 2 else
                      (lambda e, w=w, dstt=dstt, k=k: e.activation(out=dstt[:, k, :], in_=wstg[w][:], func=AF.Copy)))
        oin = [sb(f"oin{i}", [128, 8, TT], BF16) for i in range(2)]; B_oin = [Buf(f"oin{i}") for i in range(2)]
        gin = [sb(f"gin{i}", [128, 16, TT], BF16) for i in range(2)]; B_gin = [Buf(f"gin{i}") for i in range(2)]
        uT = [sb(f"uT{i}", [128, 8, TT], BF16) for i in range(2)]; B_uT = [Buf(f"uT{i}") for i in range(2)]
        t1 = [sb(f"t1{i}", [128, TT], F32) for i in range(2)]; B_t1 = [Buf(f"t1{i}") for i in range(2)]
        t2 = [sb(f"t2{i}", [128, TT], F32) for i in range(2)]; B_t2 = [Buf(f"t2{i}") for i in range(2)]
        xo = [sb(f"xo{i}", [128, D], F32) for i in range(2)]; B_xo = [Buf(f"xo{i}") for i in range(2)]
        yv = [sb(f"yv{i}", [128, D], F32) for i in range(2)]; B_yv = [Buf(f"yv{i}") for i in range(2)]
        gtb = sb("gtb", [128, D], F32); B_gtb = Buf("gtb")
        gfb = sb("gfb", [128, D], F32); B_gfb = Buf("gfb")
        gf1 = sb("gf1", [1, D], F32); B_gf1 = Buf("gf1")
        junk2 = sb("junk2", [128, D], F32); B_junk2 = Buf("junk2")
        ss2 = sb("ss2", [128, 2], F32); B_ss2 = Buf("ss2")
        pya = [ps(f"pya{i}", [128, 512]) for i in range(2)]; B_pya = [Buf(f"pya{i}") for i in range(2)]
        pyb = [ps(f"pyb{i}", [128, 512]) for i in range(2)]; B_pyb = [Buf(f"pyb{i}") for i in range(2)]
        py = [ps(f"py{i}", [128, 1024]) for i in range(2)]; B_py = [Buf(f"py{i}") for i in range(2)]
        if l == DEPTH - 1:
            S.dma('sp', gf1[:], g_final.rearrange("(a d) -> a d", a=1), [], [B_gf1], B_gf1)
            for hf in range(2):
                S.run('pe', [B_gf1, Bc2], [B_py[0]],
                      lambda e, hf=hf: e.matmul(py[0][:, hf * 512:(hf + 1) * 512], lhsT=onesf[0:1, :],
                                                rhs=gf1[0:1, hf * 512:(hf + 1) * 512], start=True, stop=True))
            S.run('act', [B_py[0]], [B_gfb], lambda e: e.activation(out=gfb[:], in_=py[0][:, :], func=AF.Copy))
        cur_seg = -1
        ci = 0
        si4 = 0
        for tile in range(NT):
            seg = tile // 8
            t0 = tile * TT
            i2 = tile % 2
            if seg != cur_seg:
                cur_seg = seg
                S.dma('sp', gtb[:], modbc[l, seg, 2], [B_modbc], [B_gtb], B_gtb)
            def d_loads(tl_):
                j2 = tl_ % 2
                tt0 = tl_ * TT
                S.dma('sp', oin[j2][:, 0:4, :], om[:, tt0:tt0 + TT].rearrange("(k p) t -> p k t", p=128), [B_scr], [B_oin[j2]],
                      B_oin[j2])
                S.dma('sp', oin[j2][:, 4:8, :], od[:, tt0:tt0 + TT].rearrange("(k p) t -> p k t", p=128), [B_scr], [B_oin[j2]],
                      B_oin[j2], accum=True)
                S.dma('sp', gin[j2][:], gab[:, tt0:tt0 + TT].rearrange("(k p) t -> p k t", p=128), [B_scr], [B_gin[j2]], B_gin[j2])
            if tile == 0:
                d_loads(0)
            if tile + 1 < NT:
                d_loads(tile + 1)
            for j in range(8):
                c2 = ci % 2
                ci += 1

                def f(e, j=j, c2=c2):
                    for k in range(4):
                        ins = e.matmul(pya[c2][:, :], lhsT=wpa[:, k, j * 128:(j + 1) * 128], rhs=oin[i2][:, k, :],
                                       start=(k == 0), stop=(k == 3))
                    return ins
                S.run('pe', [B_wd, B_oin[i2]], [B_pya[c2]], f)

                def f(e, j=j, c2=c2):
                    for k in range(4):
                        ins = e.matmul(pyb[c2][:, :], lhsT=wpb[:, k, j * 128:(j + 1) * 128], rhs=oin[i2][:, 4 + k, :],
                                       start=(k == 0), stop=(k == 3))
                    return ins
                S.run('pe', [B_wd, B_oin[i2]], [B_pyb[c2]], f)
                S.run('dve', [B_pya[c2], B_gin[i2]], [B_t1[c2]],
                      lambda e, j=j, c2=c2: e.tensor_tensor(out=t1[c2][:], in0=pya[c2][:, :], in1=gin[i2][:, j, :], op=ALU.mult))
                S.run('dve', [B_pyb[c2], B_gin[i2]], [B_t2[c2]],
                      lambda e, j=j, c2=c2: e.tensor_tensor(out=t2[c2][:], in0=pyb[c2][:, :], in1=gin[i2][:, 8 + j, :],
                                                            op=ALU.mult))
                S.run('pool', [B_t1[c2], B_t2[c2]], [B_uT[i2]],
                      lambda e, j=j, c2=c2: e.tensor_tensor(out=uT[i2][:, j, :], in0=t1[c2][:], in1=t2[c2][:], op=ALU.add))
            for s4 in range(4):
                x2 = si4 % 2
                si4 += 1
                r0 = t0 + s4 * 128
                S.dma('sp', xo[x2][:], x_src[r0:r0 + 128, :], [], [B_xo[x2]], B_xo[x2])

                def f(e, s4=s4, x2=x2):
                    for hf in range(2):
                        for k in range(8):
                            ins = e.matmul(py[x2][:, hf * 512:(hf + 1) * 512], lhsT=uT[i2][:, k, s4 * 128:(s4 + 1) * 128],
                                           rhs=wo[:, k, hf * 512:(hf + 1) * 512], start=(k == 0), stop=(k == 7))
                    return ins
                S.run('pe', [B_uT[i2], B_wd], [B_py[x2]], f)
                S.run('dve', [B_py[x2], B_gtb], [B_yv[x2]],
                      lambda e, x2=x2: e.tensor_tensor(out=yv[x2][:], in0=py[x2][:, :], in1=gtb[:], op=ALU.mult))
                S.run('pool', [B_yv[x2], B_xo[x2]], [B_xo[x2]],
                      lambda e, x2=x2: e.tensor_tensor(out=xo[x2][:], in0=xo[x2][:], in1=yv[x2][:], op=ALU.add))
                if l < DEPTH - 1:
                    S.dma('sp', xres[r0:r0 + 128, :], xo[x2][:], [B_xo[x2]], [B_scr], B_xo[x2], accum=True)
                else:
                    S.run('act', [B_xo[x2]], [B_junk2, B_ss2],
                          lambda e, x2=x2: e.activation(out=junk2[:], in_=xo[x2][:], func=AF.Square, accum_out=ss2[:, 0:1]))

                    S.run('dve', [B_ss2], [B_ss2], lambda e: e.tensor_scalar(
                        out=ss2[:, 1:2], in0=ss2[:, 0:1], scalar1=1.0 / D, scalar2=EPS, op0=ALU.mult, op1=ALU.add))
                    S.run('act', [B_ss2], [B_ss2], lambda e: e.activation(out=ss2[:, 1:2], in_=ss2[:, 1:2], func=AF.Sqrt))
                    S.run('dve', [B_ss2], [B_ss2], lambda e: e.reciprocal(out=ss2[:, 1:2], in_=ss2[:, 1:2]))
                    S.run('dve', [B_xo[x2], B_ss2, B_gfb], [B_yv[x2]],
                          lambda e, x2=x2: e.scalar_tensor_tensor(out=yv[x2][:], in0=xo[x2][:], scalar=ss2[:, 1:2], in1=gfb[:],
                                                                  op0=ALU.mult, op1=ALU.mult))
                    S.dma('sp', y_out[r0:r0 + 128, :], yv[x2][:], [B_yv[x2]], [B_scr], B_yv[x2], accum=True)
        S.barrier()
        es_ph.close()
        if l == 0 and stop == "D":
            es.close()
            return nc

    es.close()
    return nc


_CACHE = {}


def _rope_tables(core):
    s, c = core // 4, core % 4
    tabs = np.zeros((NT, 4, 128, TT), np.float32)
    p = np.arange(128)
    j2 = p % 32
    half2 = (p % 64) // 32
    inv2 = np.power(np.float32(10000.0), -(2.0 * np.arange(32, dtype=np.float32)) / np.float32(64)).astype(np.float32)
    inv1 = np.power(np.float32(10000.0), -(2.0 * np.arange(16, dtype=np.float32)) / np.float32(32)).astype(np.float32)
    for tile in range(NT):
        seg = tile // 8
        tl = (tile % 8) * TT
        pos = (np.arange(TT) + tl + (c * SEG if seg == 1 else 0)).astype(np.float32)
        ang2 = (pos[None, :] * inv2[j2][:, None]).astype(np.float32)
        tabs[tile, 0] = np.cos(ang2)
        tabs[tile, 1] = np.sin(ang2) * np.where(half2 == 0, 1.0, -1.0)[:, None]
        ang1 = (pos[None, :] * inv1[:, None]).astype(np.float32)
        cos1, sin1 = np.cos(ang1), np.sin(ang1)
        cc = np.concatenate([cos1, cos1], 0)
        ss = np.concatenate([-sin1, sin1], 0)
        tabs[tile, 2, 0:32] = cc
        tabs[tile, 2, 64:96] = cc
        tabs[tile, 3, 32:64] = ss
        tabs[tile, 3, 96:128] = ss
    return tabs


def kernel(x_prompt, x_sample, c_prompt, c_sample, w_ada, b_ada, g_norm, w_in, b_gate, g_cq, w_uq,
           g_ckv, w_ukv, w_pa, w_pb, w_out, g_final, _dbg=(), _stop=None):
    f32 = np.float32
    key = ("nc", tuple(_dbg), _stop)
    if key not in _CACHE:
        _CACHE[key] = build_program(dbg=tuple(_dbg), stop=_stop)
    nc = _CACHE[key]
    kk = np.arange(128)[:, None]
    qq = np.arange(128)[None, :]
    mA = np.where(kk >= qq, 0.0, -30000.0).astype(f32)
    mB = np.where(kk <= qq, 0.0, -30000.0).astype(f32)
    maskb = np.concatenate([mA, mB, mA, mB], 1).astype(ml_dtypes.bfloat16)
    ident = np.eye(128, dtype=f32).astype(ml_dtypes.bfloat16)
    sel = np.zeros((2, 2, 128), f32)
    sel[0, 0] = 1.0
    sel[1, 1] = 1.0
    shared = dict(w_ada=np.asarray(w_ada, f32), b_ada=np.asarray(b_ada, f32), g_norm=np.asarray(g_norm, f32),
                  w_in=np.asarray(w_in, f32),
                  b_gate=np.ascontiguousarray(np.asarray(b_gate, f32).reshape(DEPTH, 16, 128).transpose(0, 2, 1)),
                  g_cq=np.ascontiguousarray(np.asarray(g_cq, f32).reshape(DEPTH, 3, 128).transpose(0, 2, 1)),
                  w_uq=np.asarray(w_uq, f32),
                  g_ckv=np.ascontiguousarray(np.asarray(g_ckv, f32).reshape(DEPTH, 2, 128).transpose(0, 2, 1)), w_ukv=np.asarray(w_ukv, f32),
                  w_pa=np.asarray(w_pa, f32), w_pb=np.asarray(w_pb, f32), w_out=np.asarray(w_out, f32),
                  g_final=np.asarray(g_final, f32), maskb=maskb, ident=ident, sel=sel)
    x_prompt = np.asarray(x_prompt, f32)
    x_sample = np.asarray(x_sample, f32)
    c_prompt = np.asarray(c_prompt, f32)
    c_sample = np.asarray(c_sample, f32)
    in_maps = []
    for core in range(NCORES):
        s, c = core // 4, core % 4
        x_own = np.concatenate([x_prompt[core], x_sample[s, c * SEG:(c + 1) * SEG]], 0)
        cc = np.stack([c_prompt[core], c_sample[s]], 0)
        cT = np.ascontiguousarray(cc.reshape(2, 8, 128).transpose(2, 1, 0))
        flags = np.zeros((128, 2), f32)
        flags[:, 0] = 1.0 if c > 0 else 0.0
        flags[:, 1] = 1.0 if c < 3 else 0.0
        m = dict(shared)
        m.update(x_own=np.ascontiguousarray(x_own), cT=cT, tabs=_rope_tables(core), flags=flags)
        in_maps.append(m)
    res = run_bass_kernel_spmd(nc, in_maps, core_ids=list(range(NCORES)))
    if _dbg or _stop:
        return res
    y_prompt = np.stack([res.results[i]["y"][0:SEG] for i in range(NCORES)], 0)
    y_sample = np.stack([np.concatenate([res.results[4 * s + c]["y"][SEG:T] for c in range(4)], 0) for s in range(2)], 0)
    return (y_prompt.astype(f32), y_sample.astype(f32))
```

```python
import numpy as np
import ml_dtypes
from contextlib import ExitStack
import concourse.bass as bass
import concourse.mybir as mybir
from concourse.bass_utils import run_bass_kernel_spmd

F32 = mybir.dt.float32
BF16 = mybir.dt.bfloat16
ALU = mybir.AluOpType
AF = mybir.ActivationFunctionType

NCORES = 8
D = 1024
SEG = 4096
T = 8192
TT = 512
NT = T // TT
H = 8
DEPTH = 2
EPS = 1e-6
INW = 8352
C_CQ, C_CKV, C_KR, C_ZM, C_DIL, C_ZD, C_MG = 0, 384, 640, 672, 1184, 5792, 6304
DILS = (1, 4, 16)
HW = (64, 256, 1024)
KPAD = 1024
SC_MLA = 96 ** -0.5
SC_DIL = 64 ** -0.5


class _Stop(Exception):
    pass


class Buf:
    def __init__(self, name):
        self.name = name
        self.w = {}
        self.r = {}
        self.dsem = None
        self.dcnt = 0
        self.keep = False


class Sched:
    def __init__(self, nc, es):
        self.nc = nc
        self.es = es
        self.E = {'pe': nc.tensor, 'act': nc.scalar, 'dve': nc.vector, 'pool': nc.gpsimd, 'sp': nc.sync}
        self.sem = {k: es.enter_context(nc.semaphore("sem_" + k)) for k in self.E}
        self.cnt = {k: 0 for k in self.E}
        self.waited = {k: {} for k in self.E}
        self.dsems = []
        self.dpool = []
        self.bsem = es.enter_context(nc.semaphore("sem_bar"))
        self.bcnt = 0
        self.ccsem = es.enter_context(nc.semaphore("sem_cc"))
        self.cccnt = 0
        self.nsem = 8

    def _wait(self, eng, tok):
        sem, c, owner = tok
        if eng == 'pe' and owner == 'pe':
            return
        w = self.waited[eng]
        key = id(sem)
        if w.get(key, 0) >= c:
            return
        w[key] = c
        self.E[eng].wait_ge(sem, c)

    def pre(self, eng, reads, writes):
        for b in reads:
            for t in b.w.values():
                self._wait(eng, t)
        for b in writes:
            for t in b.w.values():
                self._wait(eng, t)
            for t in b.r.values():
                self._wait(eng, t)

    def _mark(self, tok, reads, writes, accum=False):
        key = id(tok[0])
        for b in reads:
            b.r[key] = tok
        for b in writes:
            if accum:
                b.w[key] = tok
            else:
                b.w = {key: tok}
                b.r = {}

    def post(self, eng, ins, reads, writes):
        self.cnt[eng] += 1
        ins.then_inc(self.sem[eng], 1)
        self._mark((self.sem[eng], self.cnt[eng], eng), reads, writes)

    def run(self, eng, reads, writes, fn, accum_w=False):
        self.pre(eng, reads, [] if accum_w else writes)
        ins = fn(self.E[eng])
        if accum_w:
            self.cnt[eng] += 1
            ins.then_inc(self.sem[eng], 1)
            self._mark((self.sem[eng], self.cnt[eng], eng), reads, writes, accum=True)
        else:
            self.post(eng, ins, reads, writes)

    def dma(self, q, out, in_, reads, writes, sb, accum=False):
        self.pre(q, reads, writes)
        if sb.dsem is None:
            if self.dpool and not sb.keep:
                sb.dsem, sb.dcnt = self.dpool.pop()
            else:
                sb.dsem = self.es.enter_context(self.nc.semaphore("d%d" % self.nsem))
                sb.dcnt = 0
                self.nsem += 1
            self.dsems.append(sb)
        ins = self.E[q].dma_start(out=out, in_=in_)
        sb.dcnt += 16
        ins.then_inc(sb.dsem, 16)
        self._mark((sb.dsem, sb.dcnt, 'dma'), reads, writes, accum=accum)

    def collective(self, in_ap, out_ap, reads, writes):
        self.pre('pool', reads, writes)
        ins = self.nc.gpsimd.collective_compute("AllGather", ALU.bypass,
                                                replica_groups=[[0, 1, 2, 3], [4, 5, 6, 7]],
                                                ins=[in_ap], outs=[out_ap], dma_qos="P2")
        self.cccnt += 1
        ins.then_inc(self.ccsem)
        self._mark((self.ccsem, self.cccnt, 'cc'), reads, writes)

    def barrier(self, end_phase=True):
        sp = self.E['sp']
        for k in self.E:
            if k != 'sp' and self.cnt[k] > 0:
                self._wait('sp', (self.sem[k], self.cnt[k], k))
        for sb in self.dsems:
            if sb.dcnt > 0:
                self._wait('sp', (sb.dsem, sb.dcnt, 'dma'))
        if self.cccnt > 0:
            self._wait('sp', (self.ccsem, self.cccnt, 'cc'))
        self.bcnt += 1
        sp.sem_inc(self.bsem, 1)
        for k in self.E:
            if k != 'sp':
                self.E[k].wait_ge(self.bsem, self.bcnt)
        if end_phase:
            keep = []
            for sb in self.dsems:
                if sb.keep:
                    keep.append(sb)
                else:
                    self.dpool.append((sb.dsem, sb.dcnt))
                    sb.dsem = None
            self.dsems = keep


def build_program(dbg=(), stop=None):
    nc = bass.Bass("TRN2", target_bir_lowering=False)
    es = ExitStack()
    st = {'ph': None}
    build_program.last = st
    try:
        return _build(nc, es, st, dbg, stop)
    except _Stop:
        build_program.last = st
        if st['ph'] is not None:
            st['ph'].close()
        es.close()
        return nc


def _build(nc, es, st, dbg, stop):

    def din(name, shape, dt=F32):
        return nc.dram_tensor(name, list(shape), dt, kind="ExternalInput").ap()

    def dscr(name, shape, dt=BF16, out=False):
        if name in dbg:
            return nc.dram_tensor(name, list(shape), dt, kind="ExternalOutput").ap()
        return nc.dram_tensor(name, list(shape), dt).ap()

    x_in = din("x_own", [T, D])
    cT_in = din("cT", [128, 8, 2])
    w_ada = din("w_ada", [DEPTH, D, 3 * D])
    b_ada = din("b_ada", [DEPTH, 3 * D])
    g_norm = din("g_norm", [DEPTH, D])
    w_in = din("w_in", [DEPTH, D, INW])
    b_gate = din("b_gate", [DEPTH, 128, 16])
    g_cq = din("g_cq", [DEPTH, 128, 3])
    w_uq = din("w_uq", [DEPTH, 384, 768])
    g_ckv = din("g_ckv", [DEPTH, 128, 2])
    w_ukv = din("w_ukv", [DEPTH, 256, 1024])
    w_pa = din("w_pa", [DEPTH, 512, D])
    w_pb = din("w_pb", [DEPTH, 512, D])
    w_out = din("w_out", [DEPTH, D, D])
    g_final = din("g_final", [D])
    tabs = din("tabs", [NT, 4, 128, TT])
    masks_in = din("maskb", [128, 512], BF16)
    ident_in = din("ident", [128, 128], BF16)
    sel_in = din("sel", [2, 2, 128])
    flags_in = din("flags", [128, 2])
    y_out = nc.dram_tensor("y", [T, D], F32, kind="ExternalOutput").ap()

    xres = dscr("xres", [T, D], F32)
    modbc = dscr("modbc", [DEPTH, 2, 3, 128, D], F32)
    hTs = dscr("hTs", [D, T])
    qm = dscr("qm", [H * 96, T])
    km = dscr("km", [H * 96, SEG])
    vm = dscr("vm", [512, SEG])
    km_sh = dscr("km_sh", [H * 96, SEG])
    vm_sh = dscr("vm_sh", [512, SEG])
    kmg = dscr("kmg", [H, 4 * 96, SEG])
    vmg = dscr("vmg", [4, 4 * 128, SEG])
    qd = dscr("qd", [3, 512, T])
    kd = dscr("kd", [3, 512, T])
    vd = dscr("vd", [3, 512, T])
    kd_sh = [[dscr(f"kd_sh{g}_{s}", [512, HW[g]]) for s in range(2)] for g in range(3)]
    vd_sh = [[dscr(f"vd_sh{g}_{s}", [512, HW[g]]) for s in range(2)] for g in range(3)]
    kdg = [[dscr(f"kdg{g}_{s}", [4 * 512, HW[g]]) for s in range(2)] for g in range(3)]
    vdg = [[dscr(f"vdg{g}_{s}", [4 * 512, HW[g]]) for s in range(2)] for g in range(3)]
    halo_k = [[dscr(f"halo_k{g}_{s}", [512, HW[g]]) for s in range(2)] for g in range(3)]
    halo_v = [[dscr(f"halo_v{g}_{s}", [512, HW[g]]) for s in range(2)] for g in range(3)]
    zm = dscr("zm", [512, T])
    zd = dscr("zd", [512, T])
    gab = dscr("gab", [2048, T])
    om = dscr("om", [512, T])
    od = dscr("od", [512, T])

    S = Sched(nc, es)
    st['S'] = S

    def chk(tag):
        if stop == tag:
            S.barrier()
            raise _Stop()
    pid = nc.sync.partition_id()
    rank = pid % 4
    ppid = nc.gpsimd.partition_id()
    rank_l = nc.gpsimd.snap((ppid + 3) % 4, min_val=0, max_val=3)
    rank_r = nc.gpsimd.snap((ppid + 1) % 4, min_val=0, max_val=3)

    uniq = [0]

    def sb(name, shape, dt):
        uniq[0] += 1
        return es_ph.enter_context(nc.sbuf_tensor("s%d_%s" % (uniq[0], name), list(shape), dt))

    def ps(name, shape, dt=F32):
        uniq[0] += 1
        return es_ph.enter_context(nc.psum_tensor("p%d_%s" % (uniq[0], name), list(shape), dt))

    B_modbc = Buf("modbc")
    B_scr = Buf("scr")
    B_share = Buf("share")
    B_gath = Buf("gath")
    B_halo = Buf("halo")
    dummy = Buf("dummy")
    dummy.keep = True
    dummy2 = Buf("dummy2")
    dummy2.keep = True

    es_ph = es
    ident = sb("ident", [128, 128], BF16); Bc = Buf("consts"); Bc.keep = True
    onesb = sb("onesb", [128, 128], BF16)
    onesf = sb("onesf", [128, 128], F32)
    maskb = sb("maskb", [128, 512], BF16)
    sel = sb("sel", [2, 2, 128], F32)
    flags = sb("flags", [128, 2], F32)
    S.dma('sp', ident[:], ident_in[:], [], [Bc], Bc)
    S.dma('sp', maskb[:], masks_in[:], [], [Bc], Bc, accum=True)
    S.dma('sp', sel[:], sel_in[:], [], [Bc], Bc, accum=True)
    S.dma('sp', flags[:], flags_in[:], [], [Bc], Bc, accum=True)
    Bc2 = Buf("consts2")
    S.run('dve', [], [Bc2], lambda e: (e.memset(onesb[:], 1.0), e.memset(onesf[:], 1.0))[1])

    es_ph = ExitStack()
    st['ph'] = es_ph
    cT = sb("cT", [128, 8, 2], F32); B_cT = Buf("cT")
    scT = sb("scT", [128, 8, 2], F32); B_scT = Buf("scT")
    wst = [sb(f"wst{i}", [128, 8, 512], F32) for i in range(2)]; B_wst = [Buf(f"wst{i}") for i in range(2)]
    bada = sb("bada", [1, 3 * D], F32); B_bada = Buf("bada")
    gn = sb("gn", [1, D], F32); B_gn = Buf("gn")
    modsb = sb("modsb", [2, 3 * D], F32); B_modsb = Buf("modsb")
    gnbc = sb("gnbc", [128, D], F32); B_gnbc = Buf("gnbc")
    bct = [sb(f"bct{i}", [128, 512], F32) for i in range(2)]; B_bct = [Buf(f"bct{i}") for i in range(2)]
    pmod = ps("pmod", [128, 512]); B_pmod = Buf("pmod")
    pbc = ps("pbc", [128, 512]); B_pbc = Buf("pbc")

    S.dma('sp', cT[:], cT_in[:], [], [B_cT], B_cT)
    S.run('act', [B_cT], [B_scT], lambda e: e.activation(out=scT[:], in_=cT[:], func=AF.Silu))
    it = 0
    for l in range(DEPTH):
        S.dma('sp', bada[:], b_ada[l:l + 1, :], [], [B_bada], B_bada)
        S.dma('sp', gn[:], g_norm[l:l + 1, :], [], [B_gn], B_gn)
        for j in range(6):
            w = it % 2
            it += 1
            S.dma('sp', wst[w][:], w_ada[l, :, j * 512:(j + 1) * 512].rearrange("(k p) c -> p k c", p=128),
                  [], [B_wst[w]], B_wst[w])

            def f(e, w=w, j=j):
                for k in range(8):
                    e.matmul(pmod[0:2, :], lhsT=scT[:, k, :], rhs=wst[w][:, k, :], start=(k == 0), stop=False)
                return e.matmul(pmod[0:2, :], lhsT=onesf[0:1, 0:2], rhs=bada[0:1, j * 512:(j + 1) * 512],
                                start=False, stop=True)
            S.run('pe', [B_scT, B_wst[w], B_bada, Bc2], [B_pmod], f)
            S.run('act', [B_pmod], [B_modsb],
                  lambda e, j=j: e.activation(out=modsb[0:2, j * 512:(j + 1) * 512], in_=pmod[0:2, :], func=AF.Copy))
        for hf in range(2):
            S.run('pe', [B_gn, Bc2], [B_pbc],
                  lambda e, hf=hf: e.matmul(pbc[:, :], lhsT=onesf[0:1, :], rhs=gn[0:1, hf * 512:(hf + 1) * 512],
                                            start=True, stop=True))
            S.run('act', [B_pbc], [B_gnbc],
                  lambda e, hf=hf: e.activation(out=gnbc[:, hf * 512:(hf + 1) * 512], in_=pbc[:, :], func=AF.Copy))
        for b in range(2):
            for kind in range(3):
                dst = {0: 1, 1: 0, 2: 2}[kind]
                for hf in range(2):
                    w = it % 2
                    it += 1
                    S.run('pe', [B_modsb, Bc], [B_pbc],
                          lambda e, b=b, kind=kind, hf=hf: e.matmul(
                              pbc[:, :], lhsT=sel[0:2, b, :],
                              rhs=modsb[0:2, kind * D + hf * 512: kind * D + (hf + 1) * 512], start=True, stop=True))
                    if kind == 1:
                        S.run('dve', [B_pbc, B_gnbc], [B_bct[w]],
                              lambda e, w=w, hf=hf: e.scalar_tensor_tensor(
                                  out=bct[w][:], in0=pbc[:, :], scalar=1.0, in1=gnbc[:, hf * 512:(hf + 1) * 512],
                                  op0=ALU.add, op1=ALU.mult))
                    else:
                        S.run('act', [B_pbc], [B_bct[w]],
                              lambda e, w=w: e.activation(out=bct[w][:], in_=pbc[:, :], func=AF.Copy))
                    S.dma('sp', modbc[l, b, dst, :, hf * 512:(hf + 1) * 512], bct[w][:], [B_bct[w]], [B_modbc],
                          B_bct[w], accum=True)
    S.barrier()
    es_ph.close()
    if stop == "P":
        es.close()
        return nc

    for l in range(DEPTH):
        x_src = x_in if l == 0 else xres
        tile_order = list(range(8, 16)) + list(range(0, 8))

        for pa in range(2):
            es_ph = ExitStack()
            st['ph'] = es_ph
            if pa == 0:
                segs = [(0, 672), (C_KR, 32), (C_ZM, 512), (C_DIL, 1536)]
                L_CQ, L_CKV, L_KR, L_ZM, L_G = 0, 384, 640, 704, 1216
                NWC = 2752
            else:
                segs = [(C_DIL + 1536, 3072), (C_ZD, 512), (C_MG, 2048)]
                L_G, L_ZD, L_MG = 0, 3072, 3584
                NWC = 5632
            wbf = sb("wbf", [128, 8, NWC], BF16); B_wbf = Buf("wbf")
            NW = 4
            wstg = [sb(f"wstg{i}", [128, 8, 256], F32) for i in range(NW)]; B_wstg = [Buf(f"wstg{i}") for i in range(NW)]
            it = 0
            lo = 0
            for (c0, ncol) in segs:
                for cc in range(0, ncol, 256):
                    n = min(256, ncol - cc)
                    w = it % NW
                    S.dma('sp' if it % 2 == 0 else 'act', wstg[w][:, :, 0:n],
                          w_in[l, :, c0 + cc:c0 + cc + n].rearrange("(k p) c -> p k c", p=128),
                          [], [B_wstg[w]], B_wstg[w])
                    dst_lo = lo + cc
                    if pa == 0 and c0 == C_KR and ncol == 32:
                        def f(e, w=w, dst_lo=dst_lo):
                            e.tensor_copy(out=wbf[:, :, dst_lo:dst_lo + 16], in_=wstg[w][:, :, 16:32])
                            return e.tensor_copy(out=wbf[:, :, dst_lo + 16:dst_lo + 32], in_=wstg[w][:, :, 0:16])
                        S.run('dve', [B_wstg[w]], [B_wbf], f, accum_w=True)
                    else:
                        eng = ('dve', 'act', 'pool')[it % 3]
                        if eng != 'act':
                            S.run(eng, [B_wstg[w]], [B_wbf], lambda e, w=w, dst_lo=dst_lo, n=n: e.tensor_copy(
                                out=wbf[:, :, dst_lo:dst_lo + n], in_=wstg[w][:, :, 0:n]), accum_w=True)
                        else:
                            S.run('act', [B_wstg[w]], [B_wbf], lambda e, w=w, dst_lo=dst_lo, n=n: e.activation(
                                out=wbf[:, :, dst_lo:dst_lo + n], in_=wstg[w][:, :, 0:n], func=AF.Copy), accum_w=True)
                    it += 1
                lo += ncol
            assert lo == NWC
            chk("A%dw" % pa)

            tbs = [sb(f"tb{i}", [128, 4, TT], F32) for i in range(2)]; B_tbs = [Buf(f"tb{i}") for i in range(2)]
            tb = tbs[0]; B_tb = B_tbs[0]
            hT = [sb(f"hT{i}", [128, 8, TT], BF16) for i in range(2)]; B_hT = [Buf(f"hT{i}") for i in range(2)]
            pmm = [ps(f"pmm{i}", [128, 512]) for i in range(4)]; B_pmm = [Buf(f"pmm{i}") for i in range(4)]
            stg = [sb(f"stg{i}", [128, TT], BF16) for i in range(6)]; B_stg = [Buf(f"stg{i}") for i in range(6)]
            ra = [sb(f"ra{i}", [128, TT], F32) for i in range(2)]; B_ra = [Buf(f"ra{i}") for i in range(2)]
            rt = [sb(f"rt{i}", [128, TT], F32) for i in range(2)]; B_rt = [Buf(f"rt{i}") for i in range(2)]
            cnt = {'pmm': 0, 'stg': 0, 'r': 0}

            if pa == 0:
                xt = [sb(f"xt{i}", [128, D], F32) for i in range(3)]; B_xt = [Buf(f"xt{i}") for i in range(3)]
                junk = sb("junk", [128, D], F32); B_junk = Buf("junk")
                hb = [sb(f"hb{i}", [128, D], BF16) for i in range(4)]; B_hb = [Buf(f"hb{i}") for i in range(4)]
                ssq = sb("ssq", [128, 8], F32); B_ssq = Buf("ssq")
                rstd = sb("rstd", [128, 8], F32); B_rstd = Buf("rstd")
                gsb = sb("gsb", [128, D], F32); shb = sb("shb", [128, D], F32); B_mod = Buf("modt")
                ptr = ps("ptr", [128, 1024], BF16); B_ptr = Buf("ptr")
                pss = ps("pss", [128, 512]); B_pss = Buf("pss")
                cqf = sb("cqf", [128, 5, TT], F32); B_cqf = Buf("cqf")
                sqb = sb("sqb", [128, 5, TT], BF16); B_sqb = Buf("sqb")
                rsq = sb("rsq", [128, 2, TT], F32); B_rsq = Buf("rsq")
                cn = sb("cn", [128, 5, TT], BF16); B_cn = Buf("cn")
                wuq = sb("wuq", [128, 3, H * 128], BF16); B_wuq = Buf("wuq")
                wuk = sb("wuk", [128, 2, 512], BF16); wuv = sb("wuv", [128, 2, 512], BF16); B_wukv = Buf("wukv")
                gq = sb("gq", [128, 3], F32); gkv = sb("gkv", [128, 2], F32); B_g = Buf("gqkv")
                kst = sb("kst", [32, TT], BF16); B_kst = Buf("kst")
                S.dma('sp', gq[:], g_cq[l], [], [B_g], B_g)
                S.dma('sp', gkv[:], g_ckv[l], [], [B_g], B_g, accum=True)
                for k in range(3):
                    w = it % NW
                    it += 1
                    for part in range(3):
                        S.dma('sp', wstg[w][:, part, 0:256], w_uq[l, k * 128:(k + 1) * 128, part * 256:(part + 1) * 256],
                              [], [B_wstg[w]], B_wstg[w], accum=(part > 0))

                    def f(e, w=w, k=k):
                        src = wstg[w][:, 0:3, 0:256].rearrange("p a c -> p (a c)").rearrange("p (h c) -> p h c", c=96)
                        dst = wuq[:, k, :].rearrange("p (h c) -> p h c", c=128)
                        e.tensor_scalar(out=dst[:, :, 0:96], in0=src, scalar1=gq[:, k:k + 1], scalar2=None, op0=ALU.mult)
                        e.tensor_scalar(out=dst[:, :, 96:112], in0=src[:, :, 80:96], scalar1=gq[:, k:k + 1], scalar2=None,
                                        op0=ALU.mult)
                        return e.tensor_scalar(out=dst[:, :, 112:128], in0=src[:, :, 64:80], scalar1=gq[:, k:k + 1],
                                               scalar2=None, op0=ALU.mult)
                    S.run('dve', [B_wstg[w], B_g], [B_wuq], f)
                for k in range(2):
                    w = it % NW
                    it += 1
                    for part in range(4):
                        S.dma('sp', wstg[w][:, part, 0:256], w_ukv[l, k * 128:(k + 1) * 128, part * 256:(part + 1) * 256],
                              [], [B_wstg[w]], B_wstg[w], accum=(part > 0))

                    def f(e, w=w, k=k):
                        src = wstg[w][:, 0:4, 0:256].rearrange("p a (h2 c) -> p (a h2) c", c=128)
                        e.tensor_scalar(out=wuk[:, k, :].rearrange("p (h c) -> p h c", c=64), in0=src[:, :, 0:64],
                                        scalar1=gkv[:, k:k + 1], scalar2=None, op0=ALU.mult)
                        return e.tensor_scalar(out=wuv[:, k, :].rearrange("p (h c) -> p h c", c=64), in0=src[:, :, 64:128],
                                               scalar1=gkv[:, k:k + 1], scalar2=None, op0=ALU.mult)
                    S.run('dve', [B_wstg[w], B_g], [B_wukv], f)
                chk("A0u")
            else:
                bg = sb("bg", [128, 16], F32); B_bg = Buf("bg")
                S.dma('sp', bg[:], b_gate[l], [], [B_bg], B_bg)

            def next_pmm():
                i = cnt['pmm'] % 4
                cnt['pmm'] += 1
                return i

            def next_stg():
                i = cnt['stg'] % 6
                cnt['stg'] += 1
                return i

            def inproj(pi, hi, lcol, m=128):
                def f(e):
                    for k in range(8):
                        ins = e.matmul(pmm[pi][0:m, :], lhsT=wbf[:, k, lcol:lcol + m], rhs=hT[hi][:, k, :],
                                       start=(k == 0), stop=(k == 7))
                    return ins
                S.run('pe', [B_wbf, B_hT[hi]], [B_pmm[pi]], f)

            def store(si, dst_ap, rows=128, q='sp'):
                S.dma(q, dst_ap, stg[si][0:rows, :], [B_stg[si]], [B_scr], B_stg[si], accum=True)

            def rope2(pi, dst_ap):
                ri = cnt['r'] % 2
                cnt['r'] += 1
                si = next_stg()
                S.run('dve', [B_pmm[pi], B_tb], [B_ra[ri]],
                      lambda e: e.tensor_tensor(out=ra[ri][:], in0=pmm[pi][:, :], in1=tb[:, 0, :], op=ALU.mult))

                def f(e):
                    for blk in range(4):
                        src = (blk ^ 1) * 32
                        ins = e.tensor_tensor(out=rt[ri][blk * 32:(blk + 1) * 32, :], in0=pmm[pi][src:src + 32, :],
                                              in1=tb[src:src + 32, 1, :], op=ALU.mult)
                    return ins
                S.run('dve', [B_pmm[pi], B_tb], [B_rt[ri]], f)
                S.run('pool', [B_ra[ri], B_rt[ri]], [B_stg[si]],
                      lambda e: e.tensor_tensor(out=stg[si][:], in0=ra[ri][:], in1=rt[ri][:], op=ALU.add))
                store(si, dst_ap)

            def actout(pi, dst_ap, func, bias=None, q='sp'):
                si = next_stg()
                if bias is None:
                    S.run('act', [B_pmm[pi]], [B_stg[si]],
                          lambda e: e.activation(out=stg[si][:], in_=pmm[pi][:, :], func=func))
                else:
                    S.run('act', [B_pmm[pi], B_bg], [B_stg[si]],
                          lambda e: e.activation(out=stg[si][:], in_=pmm[pi][:, :], func=func, bias=bias))
                store(si, dst_ap, q=q)

            segstate = {'cur': -1}

            def prepA(ti, s4):
                tile = tile_order[ti]
                seg = tile // 8
                t0 = tile * TT
                if s4 == 0 and seg != segstate['cur']:
                    segstate['cur'] = seg
                    S.dma('sp', gsb[:], modbc[l, seg, 0], [B_modbc], [B_mod], B_mod)
                    S.dma('sp', shb[:], modbc[l, seg, 1], [B_modbc], [B_mod], B_mod, accum=True)
                xi = (ti * 4 + s4) % 3
                hbi = s4
                S.dma('sp', xt[xi][:], x_src[t0 + s4 * 128:t0 + (s4 + 1) * 128, :], [], [B_xt[xi]], B_xt[xi])
                col = (ti % 2) * 4 + s4
                S.run('act', [B_xt[xi]], [B_junk, B_ssq],
                      lambda e: e.activation(out=junk[:], in_=xt[xi][:], func=AF.Square, accum_out=ssq[:, col:col + 1]))
                S.run('dve', [B_ssq], [B_rstd], lambda e: e.tensor_scalar(
                    out=rstd[:, col:col + 1], in0=ssq[:, col:col + 1], scalar1=1.0 / D, scalar2=EPS,
                    op0=ALU.mult, op1=ALU.add))
                S.run('act', [B_rstd], [B_rstd], lambda e: e.activation(
                    out=rstd[:, col:col + 1], in_=rstd[:, col:col + 1], func=AF.Sqrt))
                S.run('dve', [B_rstd], [B_rstd], lambda e: e.reciprocal(
                    out=rstd[:, col:col + 1], in_=rstd[:, col:col + 1]))
                S.run('dve', [B_xt[xi], B_rstd, B_mod], [B_junk],
                      lambda e: e.scalar_tensor_tensor(
                          out=junk[:], in0=xt[xi][:], scalar=rstd[:, col:col + 1], in1=gsb[:],
                          op0=ALU.mult, op1=ALU.mult))
                S.run('pool', [B_junk, B_mod], [B_hb[hbi]],
                      lambda e: e.tensor_tensor(out=hb[hbi][:], in0=junk[:], in1=shb[:], op=ALU.add))

            def prepB(ti, s4):
                hi = ti % 2
                hbi = s4

                def f(e):
                    for k in range(8):
                        ins = e.transpose(ptr[:, k * 128:(k + 1) * 128], hb[hbi][:, k * 128:(k + 1) * 128], ident[:])
                    return ins
                S.run('pe', [B_hb[hbi], Bc], [B_ptr], f)
                S.run('act', [B_ptr], [B_hT[hi]],
                      lambda e: e.activation(out=hT[hi][:, :, s4 * 128:(s4 + 1) * 128],
                                             in_=ptr[:, :].rearrange("p (k t) -> p k t", t=128), func=AF.Copy))
                if s4 == 3:
                    t0 = tile_order[ti] * TT
                    S.dma('sp', hTs[:, t0:t0 + TT].rearrange("(k p) t -> p k t", p=128), hT[hi][:], [B_hT[hi]], [B_scr],
                          B_hT[hi], accum=True)

            def hook(ti, where):
                if pa != 0 or ti + 1 >= NT:
                    return
                n = ti + 1
                if where == 'start':
                    prepA(n, 0); prepA(n, 1)
                elif where == 'lat':
                    prepA(n, 2); prepA(n, 3)
                elif where == 'zm':
                    prepB(n, 0)
                elif where == 'dq':
                    prepB(n, 1)
                elif where == 'dk':
                    prepB(n, 2)
                elif where == 'dv':
                    prepB(n, 3)

            if pa == 0:
                for s4 in range(4):
                    prepA(0, s4)
                for s4 in range(4):
                    prepB(0, s4)
            else:
                S.dma('sp', hT[0][:], hTs[:, tile_order[0] * TT:tile_order[0] * TT + TT].rearrange("(k p) t -> p k t", p=128), [B_scr], [B_hT[0]], B_hT[0])
            for ti, tile in enumerate(tile_order):
                if ti > 0:
                    chk("A%dt%d" % (pa, ti - 1))
                seg = tile // 8
                t0 = tile * TT
                hi = ti % 2
                tb = tbs[ti % 2]; B_tb = B_tbs[ti % 2]
                if ti == 0:
                    S.dma('sp', tb[:], tabs[tile].rearrange("a p t -> p a t"), [], [B_tb], B_tb)
                if ti + 1 < NT:
                    S.dma('sp', tbs[(ti + 1) % 2][:], tabs[tile_order[ti + 1]].rearrange("a p t -> p a t"), [],
                          [B_tbs[(ti + 1) % 2]], B_tbs[(ti + 1) % 2])
                if pa == 0:
                    hook(ti, 'start')
                    chk("A0h")
                    for j in range(5):
                        pi = next_pmm()
                        inproj(pi, hi, L_CQ + j * 128)
                        S.run('act', [B_pmm[pi]], [B_cqf],
                              lambda e, j=j, pi=pi: e.activation(out=cqf[:, j, :], in_=pmm[pi][:, :], func=AF.Copy))
                        S.run('act', [B_pmm[pi]], [B_sqb],
                              lambda e, j=j, pi=pi: e.activation(out=sqb[:, j, :], in_=pmm[pi][:, :], func=AF.Square))
                    for which, (j0, nj, nfeat) in enumerate(((0, 3, 384), (3, 2, 256))):
                        def f(e, j0=j0, nj=nj):
                            for j in range(nj):
                                ins = e.matmul(pss[:, :], lhsT=onesb[:], rhs=sqb[:, j0 + j, :], start=(j == 0),
                                               stop=(j == nj - 1))
                            return ins
                        S.run('pe', [B_sqb, Bc2], [B_pss], f)

                        S.run('dve', [B_pss], [B_rsq], lambda e, which=which, nfeat=nfeat: e.tensor_scalar(
                            out=rsq[:, which, :], in0=pss[:, :], scalar1=1.0 / nfeat, scalar2=EPS, op0=ALU.mult, op1=ALU.add))
                        S.run('act', [B_rsq], [B_rsq], lambda e, which=which: e.activation(
                            out=rsq[:, which, :], in_=rsq[:, which, :], func=AF.Sqrt))
                        S.run('dve', [B_rsq], [B_rsq], lambda e, which=which: e.reciprocal(
                            out=rsq[:, which, :], in_=rsq[:, which, :]))
                        for j in range(nj):
                            S.run('pool' if j % 2 else 'dve', [B_cqf, B_rsq], [B_cn],
                                  lambda e, j=j, j0=j0, which=which: e.tensor_tensor(
                                      out=cn[:, j0 + j, :], in0=cqf[:, j0 + j, :], in1=rsq[:, which, :], op=ALU.mult))
                    hook(ti, 'lat')
                    chk("A0c")
                    pi = next_pmm()
                    inproj(pi, hi, L_KR, m=64)
                    ri = cnt['r'] % 2
                    cnt['r'] += 1
                    S.run('dve', [B_pmm[pi], B_tb], [B_ra[ri]],
                          lambda e: e.tensor_tensor(out=ra[ri][0:32, :], in0=pmm[pi][0:32, :], in1=tb[0:32, 2, :], op=ALU.mult))
                    S.run('dve', [B_pmm[pi], B_tb], [B_rt[ri]],
                          lambda e: e.tensor_tensor(out=rt[ri][0:32, :], in0=pmm[pi][32:64, :], in1=tb[32:64, 3, :], op=ALU.mult))

                    S.run('pool', [B_ra[ri], B_rt[ri]], [B_kst],
                          lambda e: e.tensor_tensor(out=kst[:, :], in0=ra[ri][0:32, :], in1=rt[ri][0:32, :], op=ALU.add))
                    kdst = km if seg == 0 else km_sh
                    tl = t0 - seg * SEG
                    for h in range(H):
                        S.dma('sp', kdst[h * 96 + 64:h * 96 + 96, tl:tl + TT], kst[:, :], [B_kst],
                              [B_scr if seg == 0 else B_share], B_kst, accum=True)
                    chk("A0k")
                    for j in range(4):
                        pi = next_pmm()
                        inproj(pi, hi, L_ZM + j * 128)
                        actout(pi, zm[j * 128:(j + 1) * 128, t0:t0 + TT], AF.Silu, q='act')
                    glist = (0,)
                else:
                    if ti + 1 < NT:
                        tn = tile_order[ti + 1] * TT
                        S.dma('sp', hT[1 - hi][:], hTs[:, tn:tn + TT].rearrange("(k p) t -> p k t", p=128), [B_scr],
                              [B_hT[1 - hi]], B_hT[1 - hi])
                    glist = (1, 2)
                chk("A%dkv" % pa)
                hook(ti, 'zm')
                for gi, g in enumerate(glist):
                    base = L_G + gi * 1536
                    for j in range(4):
                        pi = next_pmm()
                        inproj(pi, hi, base + j * 128)
                        rope2(pi, qd[g, j * 128:(j + 1) * 128, t0:t0 + TT])
                    if gi == 0:
                        hook(ti, 'dq')
                    for j in range(4):
                        pi = next_pmm()
                        inproj(pi, hi, base + 512 + j * 128)
                        rope2(pi, kd[g, j * 128:(j + 1) * 128, t0:t0 + TT])
                    if gi == 0:
                        hook(ti, 'dk')
                    for j in range(4):
                        pi = next_pmm()
                        inproj(pi, hi, base + 1024 + j * 128)
                        actout(pi, vd[g, j * 128:(j + 1) * 128, t0:t0 + TT], AF.Copy, q='act')
                    if gi == 0:
                        hook(ti, 'dv')
                if pa == 0:
                    chk("A0z")
                    for h in range(H):
                        pi = next_pmm()

                        def f(e, h=h, pi=pi):
                            for k in range(3):
                                ins = e.matmul(pmm[pi][:, :], lhsT=wuq[:, k, h * 128:(h + 1) * 128], rhs=cn[:, k, :],
                                               start=(k == 0), stop=(k == 2))
                            return ins
                        S.run('pe', [B_wuq, B_cn], [B_pmm[pi]], f)
                        si = next_stg()
                        ri = cnt['r'] % 2
                        cnt['r'] += 1
                        S.run('act', [B_pmm[pi]], [B_stg[si]],
                              lambda e, si=si, pi=pi: e.activation(out=stg[si][0:64, :], in_=pmm[pi][0:64, :], func=AF.Copy))
                        S.run('dve', [B_pmm[pi], B_tb], [B_ra[ri]],
                              lambda e, ri=ri, pi=pi: e.tensor_tensor(out=ra[ri][64:96, :], in0=pmm[pi][64:96, :],
                                                                      in1=tb[64:96, 2, :], op=ALU.mult))
                        S.run('dve', [B_pmm[pi], B_tb], [B_rt[ri]],
                              lambda e, ri=ri, pi=pi: e.tensor_tensor(out=rt[ri][64:96, :], in0=pmm[pi][96:128, :],
                                                                      in1=tb[96:128, 3, :], op=ALU.mult))
                        S.run('pool', [B_ra[ri], B_rt[ri]], [B_stg[si]],
                              lambda e, ri=ri, si=si: e.tensor_tensor(out=stg[si][64:96, :], in0=ra[ri][64:96, :],
                                                                      in1=rt[ri][64:96, :], op=ALU.add))
                        store(si, qm[h * 96:(h + 1) * 96, t0:t0 + TT], rows=96)
                    chk("A0q")
                    for j in range(4):
                        pi = next_pmm()

                        def f(e, j=j, pi=pi):
                            for k in range(2):
                                ins = e.matmul(pmm[pi][:, :], lhsT=wuk[:, k, j * 128:(j + 1) * 128], rhs=cn[:, 3 + k, :],
                                               start=(k == 0), stop=(k == 1))
                            return ins
                        S.run('pe', [B_wukv, B_cn], [B_pmm[pi]], f)
                        si = next_stg()
                        S.run('act', [B_pmm[pi]], [B_stg[si]],
                              lambda e, si=si, pi=pi: e.activation(out=stg[si][:], in_=pmm[pi][:, :], func=AF.Copy))
                        for hh in range(2):
                            h = 2 * j + hh
                            S.dma('sp', kdst[h * 96:h * 96 + 64, tl:tl + TT], stg[si][hh * 64:(hh + 1) * 64, :], [B_stg[si]],
                                  [B_scr if seg == 0 else B_share], B_stg[si], accum=True)
                    vdst = vm if seg == 0 else vm_sh
                    for j in range(4):
                        pi = next_pmm()

                        def f(e, j=j, pi=pi):
                            for k in range(2):
                                ins = e.matmul(pmm[pi][:, :], lhsT=wuv[:, k, j * 128:(j + 1) * 128], rhs=cn[:, 3 + k, :],
                                               start=(k == 0), stop=(k == 1))
                            return ins
                        S.run('pe', [B_wukv, B_cn], [B_pmm[pi]], f)
                        si = next_stg()
                        S.run('act', [B_pmm[pi]], [B_stg[si]],
                              lambda e, si=si, pi=pi: e.activation(out=stg[si][:], in_=pmm[pi][:, :], func=AF.Copy))
                        S.dma('act', vdst[j * 128:(j + 1) * 128, tl:tl + TT], stg[si][:], [B_stg[si]],
                              [B_scr if seg == 0 else B_share], B_stg[si], accum=True)
                chk("A%dd" % pa)
                if pa == 0 and 7 <= ti <= 12:
                    cl = [(km_sh[h * 96:(h + 1) * 96, :], kmg[h]) for h in range(H)] + \
                         [(vm_sh[j * 128:(j + 1) * 128, :], vmg[j]) for j in range(4)]
                    for (a_, b_) in cl[2 * (ti - 7):2 * (ti - 7) + 2]:
                        S.collective(a_, b_, [B_share], [B_gath])
                if pa == 1:
                    for j in range(4):
                        pi = next_pmm()
                        inproj(pi, hi, L_ZD + j * 128)
                        actout(pi, zd[j * 128:(j + 1) * 128, t0:t0 + TT], AF.Silu, q='act')
                    for j in range(16):
                        pi = next_pmm()
                        inproj(pi, hi, L_MG + j * 128)
                        actout(pi, gab[j * 128:(j + 1) * 128, t0:t0 + TT], AF.Sigmoid, bias=bg[:, j:j + 1])
                    if ti == 7:
                        chk("A1pre")
                        for g in range(3):
                            for s in range(2):
                                c0 = SEG if s == 0 else T - HW[g]
                                S.dma('sp', kd_sh[g][s][:, :], kd[g, :, c0:c0 + HW[g]], [B_scr], [B_share], dummy2, accum=True)
                                S.dma('sp', vd_sh[g][s][:, :], vd[g, :, c0:c0 + HW[g]], [B_scr], [B_share], dummy2, accum=True)
                    if 7 <= ti <= 12:
                        cl = []
                        for g in (2, 1, 0):
                            for s in range(2):
                                cl.append((kd_sh[g][s], kdg[g][s]))
                                cl.append((vd_sh[g][s], vdg[g][s]))
                        for (a_, b_) in cl[2 * (ti - 7):2 * (ti - 7) + 2]:
                            S.collective(a_, b_, [B_share], [B_gath])
            if pa == 1:
                for g in range(3):
                    for (srcs, dsts) in ((kdg, halo_k), (vdg, halo_v)):
                        S.dma('pool', dsts[g][0][:, :], srcs[g][1].rearrange("(a p) f -> a p f", a=4)[
                            bass.ds(rank_l, 1), :, :].rearrange("a p f -> (a p) f"), [B_gath], [B_halo], dummy, accum=True)
                        S.dma('pool', dsts[g][1][:, :], srcs[g][0].rearrange("(a p) f -> a p f", a=4)[
                            bass.ds(rank_r, 1), :, :].rearrange("a p f -> (a p) f"), [B_gath], [B_halo], dummy, accum=True)
            S.barrier()
            es_ph.close()
            if l == 0 and stop == "A%d" % pa:
                es.close()
                return nc

        es_ph = ExitStack()
        st['ph'] = es_ph
        kt = [sb(f"kt{i}", [96, 4 * SEG], BF16) for i in range(2)]; B_kt = [Buf(f"kt{i}") for i in range(2)]
        vT1 = sb("vT0", [64, 4 * SEG], BF16); B_vT1 = Buf("vT0")
        vt = [sb(f"vt{i}", [128, 128, 65], BF16) for i in range(2)]; B_vt = [Buf(f"vt{i}") for i in range(2)]
        qt = [sb(f"qt{i}", [96, SEG], BF16) for i in range(2)]; B_qt = [Buf(f"qt{i}") for i in range(2)]
        zt = [sb(f"zt{i}", [64, SEG], BF16) for i in range(2)]; B_zt = [Buf(f"zt{i}") for i in range(2)]
        ot = [sb(f"ot{i}", [64, SEG], BF16) for i in range(2)]; B_ot = [Buf(f"ot{i}") for i in range(2)]
        pt = [sb(f"pt{i}", [128, 1024], BF16) for i in range(2)]; B_pt = [Buf(f"pt{i}") for i in range(2)]
        rd = sb("rd", [128, 512], F32); B_rd = Buf("rd")
        tmpo = sb("tmpo", [64, 512], F32); B_tmpo = Buf("tmpo")
        pS = [ps(f"pS{i}", [128, 1024]) for i in range(2)]; B_pS = [Buf(f"pS{i}") for i in range(2)]
        pOs = [ps(f"pO{i}", [128, 512]) for i in range(2)]; B_pOs = [Buf(f"pO{i}") for i in range(2)]
        pB = ps("pB", [128, 512]); B_pB = Buf("pB")
        pT = ps("pT", [128, 1024], BF16); B_pT = Buf("pT")
        for i in range(2):
            S.run('dve', [], [B_vt[i]], lambda e, i=i: e.memset(vt[i][:, :, 64:65], 1.0))

        def b_loads(sh):
            seg, h = sh // H, sh % H
            i2 = sh % 2
            S.dma('sp', qt[i2][:], qm[h * 96:(h + 1) * 96, seg * SEG:(seg + 1) * SEG], [B_scr], [B_qt[i2]], B_qt[i2])
            S.dma('sp', zt[i2][:], zm[h * 64:(h + 1) * 64, seg * SEG:(seg + 1) * SEG], [B_scr], [B_zt[i2]], B_zt[i2])
            if seg == 0:
                S.dma('sp', vT1[:, 0:SEG], vm[h * 64:(h + 1) * 64, :], [B_scr], [B_vT1], B_vT1)
                S.dma('sp', kt[i2][:, 0:SEG], km[h * 96:(h + 1) * 96, :], [B_scr], [B_kt[i2]], B_kt[i2])
            else:
                jj, hh = h // 2, h % 2
                for r in range(4):
                    S.dma('sp', vT1[:, r * SEG:(r + 1) * SEG], vmg[jj, r * 128 + hh * 64:r * 128 + (hh + 1) * 64, :],
                          [B_gath], [B_vT1], B_vT1, accum=(r > 0))
                for r in range(4):
                    S.dma('sp', kt[i2][:, r * SEG:(r + 1) * SEG], kmg[h, r * 96:(r + 1) * 96, :], [B_gath], [B_kt[i2]],
                          B_kt[i2], accum=(r > 0))

        def b_prep(sh, c8lo, c8hi):
            seg = sh // H
            i2 = sh % 2
            nch_ = (SEG if seg == 0 else 4 * SEG) // 128
            for c8 in range(c8lo, min(c8hi, nch_ // 8)):
                def f(e, c8=c8):
                    for c in range(8):
                        ch = c8 * 8 + c
                        ins = e.transpose(pT[:, c * 128:c * 128 + 64], vT1[:, ch * 128:(ch + 1) * 128], ident[0:64, 0:64])
                    return ins
                S.run('pe', [B_vT1, Bc], [B_pT], f)
                S.run('dve', [B_pT], [B_vt[i2]],
                      lambda e, c8=c8: e.tensor_copy(out=vt[i2][:, c8 * 8:(c8 + 1) * 8, 0:64],
                                                     in_=pT[:, :].rearrange("p (c x) -> p c x", x=128)[:, :, 0:64]))

        pending = []

        def flush():
            while pending:
                pending.pop(0)()

        gidx = 0
        b_loads(0)
        b_prep(0, 0, 16)
        for sh in range(2 * H):
            seg, h = sh // H, sh % H
            i2 = sh % 2
            NK = SEG if seg == 0 else 4 * SEG
            nch = NK // 128
            G = nch // 2
            for qi in range(8):
                qs = slice(qi * 512, (qi + 1) * 512)
                pO = pOs[qi % 2]
                B_pO = B_pOs[qi % 2]

                def QK(g, gi):
                    si = gi % 2

                    def f(e):
                        for j in range(2):
                            ch = 2 * g + j
                            ins = e.matmul(pS[si][:, j * 512:(j + 1) * 512], lhsT=kt[i2][:, ch * 128:(ch + 1) * 128],
                                           rhs=qt[i2][:, qs], start=True, stop=True)
                        return ins
                    S.run('pe', [B_kt[i2], B_qt[i2]], [B_pS[si]], f)

                def EXP(g, gi):
                    si, pi_ = gi % 2, gi % 2
                    S.run('act', [B_pS[si]], [B_pt[pi_]],
                          lambda e: e.activation(out=pt[pi_][:], in_=pS[si][:, :], func=AF.Exp, scale=SC_MLA))

                def PV(g, gi):
                    pi_ = gi % 2

                    def f(e):
                        for j in range(2):
                            ch = 2 * g + j
                            ins = e.matmul(pO[0:65, :], lhsT=vt[i2][:, ch, :], rhs=pt[pi_][:, j * 512:(j + 1) * 512],
                                           start=(ch == 0), stop=(ch == nch - 1))
                        return ins
                    S.run('pe', [B_vt[i2], B_pt[pi_]], [B_pO], f)
                QK(0, gidx)
                QK(1, gidx + 1)
                for g in range(G):
                    EXP(g, gidx + g)
                    PV(g, gidx + g)
                    if g + 2 < G:
                        QK(g + 2, gidx + g + 2)
                    if g == 3:
                        flush()
                        if qi == 0 and sh + 1 < 2 * H:
                            b_loads(sh + 1)
                    if sh + 1 < 2 * H and qi >= 2 and g == 5:
                        n8 = (SEG if (sh + 1) // H == 0 else 4 * SEG) // 128 // 8
                        per = (n8 + 5) // 6
                        b_prep(sh + 1, (qi - 2) * per, (qi - 1) * per)
                gidx += G
                S.run('dve', [B_pO], [B_rd], lambda e, pO=pO: e.reciprocal(out=rd[64:65, :], in_=pO[64:65, :]))

                def tail(pO=pO, B_pO=B_pO, qs=qs, i2=i2, last=(qi == 7), h=h, seg=seg):
                    S.run('pe', [B_rd, Bc2], [B_pB],
                          lambda e: e.matmul(pB[0:64, :], lhsT=onesf[64:65, 0:64], rhs=rd[64:65, :], start=True, stop=True))
                    S.run('dve', [B_pO, B_zt[i2]], [B_tmpo],
                          lambda e: e.tensor_tensor(out=tmpo[:, :], in0=pO[0:64, :], in1=zt[i2][:, qs], op=ALU.mult))
                    S.run('dve', [B_tmpo, B_pB], [B_ot[i2]],
                          lambda e: e.tensor_tensor(out=ot[i2][:, qs], in0=tmpo[:, :], in1=pB[0:64, :], op=ALU.mult))
                    if last:
                        S.dma('sp', om[h * 64:(h + 1) * 64, seg * SEG:(seg + 1) * SEG], ot[i2][:], [B_ot[i2]], [B_scr],
                              B_ot[i2], accum=True)
                pending.append(tail)
        flush()
        S.barrier()
        es_ph.close()
        if l == 0 and stop == "B":
            es.close()
            return nc

        es_ph = ExitStack()
        st['ph'] = es_ph
        WK = SEG + 2 * KPAD
        qdt = sb("qdt", [128, 3, SEG], BF16)
        kdt = sb("kdt", [128, 3, WK], BF16)
        vdT = sb("vdT", [128, 3, WK], BF16)
        B_qd2 = [Buf(f"qd2_{i}") for i in range(2)]
        B_kd2 = [Buf(f"kd2_{i}") for i in range(2)]
        B_vd2 = [Buf(f"vd2_{i}") for i in range(2)]
        NCH = [d * (SEG // (128 * d) + 1) for d in DILS]
        CB = [0, NCH[0], NCH[0] + NCH[1]]
        NCHT = sum(NCH)
        vdt = [sb(f"vdt{i}", [128, NCHT, 65], BF16) for i in range(2)]; B_vdt = [Buf(f"vdt{i}") for i in range(2)]
        zdt = [sb(f"zdt{i}", [64, SEG], BF16) for i in range(2)]; B_zdt = [Buf(f"zdt{i}") for i in range(2)]
        odt = [sb(f"odt{i}", [64, SEG], BF16) for i in range(2)]; B_odt = [Buf(f"odt{i}") for i in range(2)]
        osum = sb("osum", [65, SEG], F32); B_osum = Buf("osum")
        rdd = sb("rdd", [65, SEG], F32); B_rdd = Buf("rdd")
        pdt = [sb(f"pdt{i}", [128, 1024], BF16) for i in range(2)]; B_pdt = [Buf(f"pdt{i}") for i in range(2)]
        pS = [ps(f"pSd{i}", [128, 1024]) for i in range(2)]; B_pS = [Buf(f"pSd{i}") for i in range(2)]
        pO = [ps(f"pOd{i}", [128, 512]) for i in range(2)]; B_pO = [Buf(f"pOd{i}") for i in range(2)]
        pB = ps("pBd", [128, 512]); B_pB = Buf("pBd")
        pT = ps("pTd", [128, 1024], BF16); B_pT = Buf("pTd")
        for i in range(2):
            S.run('dve', [], [B_vdt[i]], lambda e, i=i: e.memset(vdt[i][:, :, 64:65], 1.0))

        def slab(bf, g):
            s_ = bf * 3 + g
            return 64 * (s_ % 2), s_ // 2

        def c_loads(sh):
            seg, h = sh // H, sh % H
            bf = sh % 2
            s0 = seg * SEG
            S.dma('sp', zdt[bf][:], zd[h * 64:(h + 1) * 64, s0:s0 + SEG], [B_scr], [B_zdt[bf]], B_zdt[bf])
            for g in range(3):
                p0, sl = slab(bf, g)
                hw = HW[g]
                S.dma('sp', qdt[p0:p0 + 64, sl, :], qd[g, h * 64:(h + 1) * 64, s0:s0 + SEG], [B_scr], [B_qd2[bf]], B_qd2[bf],
                      accum=(g > 0))
                S.dma('sp', kdt[p0:p0 + 64, sl, KPAD:KPAD + SEG], kd[g, h * 64:(h + 1) * 64, s0:s0 + SEG], [B_scr],
                      [B_kd2[bf]], B_kd2[bf], accum=(g > 0))
                S.dma('sp', vdT[p0:p0 + 64, sl, KPAD:KPAD + SEG], vd[g, h * 64:(h + 1) * 64, s0:s0 + SEG], [B_scr],
                      [B_vd2[bf]], B_vd2[bf], accum=(g > 0))
                if seg == 1:
                    S.dma('sp', kdt[p0:p0 + 64, sl, KPAD - hw:KPAD], halo_k[g][0][h * 64:(h + 1) * 64, :], [B_halo],
                          [B_kd2[bf]], B_kd2[bf], accum=True)
                    S.dma('sp', kdt[p0:p0 + 64, sl, KPAD + SEG:KPAD + SEG + hw], halo_k[g][1][h * 64:(h + 1) * 64, :],
                          [B_halo], [B_kd2[bf]], B_kd2[bf], accum=True)
                    S.dma('sp', vdT[p0:p0 + 64, sl, KPAD - hw:KPAD], halo_v[g][0][h * 64:(h + 1) * 64, :], [B_halo],
                          [B_vd2[bf]], B_vd2[bf], accum=True)
                    S.dma('sp', vdT[p0:p0 + 64, sl, KPAD + SEG:KPAD + SEG + hw], halo_v[g][1][h * 64:(h + 1) * 64, :],
                          [B_halo], [B_vd2[bf]], B_vd2[bf], accum=True)

        def c_prep(sh):
            seg, h = sh // H, sh % H
            bf = sh % 2
            if seg == 0:
                def f(e):
                    for g in range(3):
                        p0, sl = slab(bf, g)
                        hw = HW[g]
                        e.memset(kdt[p0:p0 + 64, sl, KPAD - hw:KPAD], 0.0)
                        e.memset(kdt[p0:p0 + 64, sl, KPAD + SEG:KPAD + SEG + hw], 0.0)
                        e.memset(vdT[p0:p0 + 64, sl, KPAD - hw:KPAD], 0.0)
                        ins = e.memset(vdT[p0:p0 + 64, sl, KPAD + SEG:KPAD + SEG + hw], 0.0)
                    return ins
                S.run('pool', [], [B_kd2[bf], B_vd2[bf]], f)
            else:
                def f(e):
                    for g in range(3):
                        p0, sl = slab(bf, g)
                        hw = HW[g]
                        e.tensor_scalar(out=vdT[p0:p0 + 64, sl, KPAD - hw:KPAD], in0=vdT[p0:p0 + 64, sl, KPAD - hw:KPAD],
                                        scalar1=flags[p0:p0 + 64, 0:1], scalar2=None, op0=ALU.mult)
                        ins = e.tensor_scalar(out=vdT[p0:p0 + 64, sl, KPAD + SEG:KPAD + SEG + hw],
                                              in0=vdT[p0:p0 + 64, sl, KPAD + SEG:KPAD + SEG + hw],
                                              scalar1=flags[p0:p0 + 64, 1:2], scalar2=None, op0=ALU.mult)
                    return ins
                S.run('pool', [Bc], [B_vd2[bf]], f)
            chunks = []
            for g, d in enumerate(DILS):
                ntg = SEG // (128 * d)
                for r in range(d):
                    for t in range(ntg + 1):
                        start = KPAD + r + d * (128 * t - 64)
                        chunks.append((g, CB[g] + r * (ntg + 1) + t, start, d))
            groups8 = []
            for g in range(3):
                cg = [c for c in chunks if c[0] == g]
                for c8 in range(0, len(cg), 8):
                    groups8.append(cg[c8:c8 + 8])
            for grp in groups8:

                def f(e, grp=grp):
                    for c, (g, ci, start, d) in enumerate(grp):
                        p0, sl = slab(bf, g)
                        ins = e.transpose(pT[:, c * 128:c * 128 + 64], vdT[p0:p0 + 64, sl, start:start + 127 * d + 1:d],
                                          ident[p0:p0 + 64, p0:p0 + 64])
                    return ins
                S.run('pe', [B_vd2[bf], Bc], [B_pT], f)
                n = len(grp)
                ci0 = grp[0][1]
                S.run('dve', [B_pT], [B_vdt[bf]],
                      lambda e, n=n, ci0=ci0: e.tensor_copy(out=vdt[bf][:, ci0:ci0 + n, 0:64],
                                                            in_=pT[:, 0:n * 128].rearrange("p (c x) -> p c x", x=128)[:, :, 0:64]))

            def f(e):
                ins = None
                for g, d in enumerate(DILS):
                    ntg = SEG // (128 * d)
                    v = vdt[bf][:, CB[g]:CB[g] + NCH[g], :].rearrange("p (r t) x -> p r t x", t=ntg + 1)
                    if seg == 0:
                        e.memset(v[0:64, :, 0, 64:65], 0.0)
                        ins = e.memset(v[64:128, :, ntg, 64:65], 0.0)
                    else:
                        e.tensor_copy(out=v[0:64, :, 0, 64:65], in_=flags[0:64, 0:1].to_broadcast([64, d, 1]))
                        ins = e.tensor_copy(out=v[64:128, :, ntg, 64:65], in_=flags[64:128, 1:2].to_broadcast([64, d, 1]))
                return ins
            S.run('dve', [Bc], [B_vdt[bf]], f)

        bi = 0
        c_loads(0)
        chk("C0l")
        c_prep(0)
        chk("C0p")
        for sh in range(2 * H):
            if sh > 0:
                chk("C%de" % (sh - 1))
            seg, h = sh // H, sh % H
            bf = sh % 2
            s0 = seg * SEG
            if sh + 1 < 2 * H:
                c_loads(sh + 1)
            batches = []
            for g, d in enumerate(DILS):
                ntg = SEG // (128 * d)
                tiles = [(r, t) for r in range(d) for t in range(ntg)]
                for b4 in range(0, len(tiles), 4):
                    batches.append((g, d, ntg, tiles[b4:b4 + 4]))

            def QK(bidx, si):
                g, d, ntg, batch = batches[bidx]
                p0, sl = slab(bf, g)

                def f(e):
                    for bk in range(2):
                        e.matmul(pS[si][:, bk * 512:(bk + 1) * 512], lhsT=ident[:, :], rhs=maskb[:, :], start=True, stop=False,
                                 skip_group_check=True)
                    for n, (r, t) in enumerate(batch):
                        qstart = r + d * 128 * t
                        for ab in range(2):
                            kstart = KPAD + r + d * (128 * (t + ab) - 64)
                            o_ = pS[si][:, n * 256 + ab * 128:n * 256 + (ab + 1) * 128]
                            ins = e.matmul(o_, lhsT=kdt[p0:p0 + 64, sl, kstart:kstart + 127 * d + 1:d],
                                           rhs=qdt[p0:p0 + 64, sl, qstart:qstart + 127 * d + 1:d], start=False,
                                           stop=(n % 2 == 1 and ab == 1), skip_group_check=True)
                    return ins
                S.run('pe', [B_kd2[bf], B_qd2[bf], Bc], [B_pS[si]], f)

            def EXP(bidx, si):
                S.run('act', [B_pS[si]], [B_pdt[si]],
                      lambda e: e.activation(out=pdt[si][:], in_=pS[si][:, :], func=AF.Exp, scale=SC_DIL))

            def PV(bidx, si):
                g, d, ntg, batch = batches[bidx]

                def f(e):
                    for n, (r, t) in enumerate(batch):
                        for ab in range(2):
                            ci = CB[g] + r * (ntg + 1) + t + ab
                            ins = e.matmul(pO[si][0:65, n * 128:(n + 1) * 128], lhsT=vdt[bf][:, ci, :],
                                           rhs=pdt[si][:, n * 256 + ab * 128:n * 256 + (ab + 1) * 128],
                                           start=(ab == 0), stop=(ab == 1))
                    return ins
                S.run('pe', [B_vdt[bf], B_pdt[si]], [B_pO[si]], f)

            def OSUM(bidx, si):
                g, d, ntg, batch = batches[bidx]
                r0, t0 = batch[0]
                if g == 0:
                    dst = osum[0:65, 128 * t0:128 * t0 + 512]
                    src = pO[si][0:65, :]
                elif g == 1:
                    st_ = r0 + 512 * t0
                    dst = osum[0:65, st_:st_ + 4 * 511 + 1:4]
                    src = pO[si][0:65, :]
                else:
                    dst = osum[0:65, :].rearrange("p (n d) -> p d n", d=16)[:, r0:r0 + 2, :]
                    src = pO[si][0:65, :].rearrange("p (a b) -> p a b", a=2)
                if g == 0:
                    S.run('dve', [B_pO[si]], [B_osum], lambda e: e.tensor_copy(out=dst, in_=src))
                else:
                    S.run('dve', [B_pO[si], B_osum], [B_osum], lambda e: e.tensor_tensor(out=dst, in0=dst, in1=src, op=ALU.add))

            nb = len(batches)
            QK(0, bi % 2)
            for b_ in range(nb):
                si = (bi + b_) % 2
                EXP(b_, si)
                if b_ + 1 < nb:
                    QK(b_ + 1, (bi + b_ + 1) % 2)
                PV(b_, si)
                OSUM(b_, si)
            bi += nb
            chk("C%db" % sh)
            if sh + 1 < 2 * H:
                c_prep(sh + 1)
            S.run('act', [B_osum], [B_rdd], lambda e: e.activation(out=rdd[64:65, :], in_=osum[64:65, :], func=AF.Ln))
            S.run('act', [B_rdd], [B_rdd], lambda e: e.activation(out=rdd[64:65, :], in_=rdd[64:65, :], func=AF.Exp, scale=-1.0))
            for qi in range(8):
                qs = slice(qi * 512, (qi + 1) * 512)
                S.run('pe', [B_rdd, Bc2], [B_pB],
                      lambda e, qs=qs: e.matmul(pB[0:64, :], lhsT=onesf[64:65, 0:64], rhs=rdd[64:65, qs], start=True, stop=True))
                S.run('pool', [B_osum, B_zdt[bf]], [B_osum],
                      lambda e, qs=qs: e.tensor_tensor(out=osum[0:64, qs], in0=osum[0:64, qs], in1=zdt[bf][:, qs], op=ALU.mult))
                S.run('dve', [B_osum, B_pB], [B_odt[bf]],
                      lambda e, qs=qs: e.tensor_tensor(out=odt[bf][:, qs], in0=osum[0:64, qs], in1=pB[0:64, :], op=ALU.mult))
            S.dma('sp', od[h * 64:(h + 1) * 64, s0:s0 + SEG], odt[bf][:], [B_odt[bf]], [B_scr], B_odt[bf], accum=True)
        S.barrier()
        es_ph.close()
        if l == 0 and stop == "C":
            es.close()
            return nc

        es_ph = ExitStack()
        st['ph'] = es_ph
        wpa = sb("wpa", [128, 4, D], BF16); wpb = sb("wpb", [128, 4, D], BF16); wo = sb("wo", [128, 8, D], BF16)
        B_wd = Buf("wd")
        wstg = [sb(f"wstgd{i}", [128, 1024], F32) for i in range(2)]; B_wstg = [Buf(f"wstgd{i}") for i in range(2)]
        it = 0
        for (src, dstt, nk) in ((w_pa, wpa, 4), (w_pb, wpb, 4), (w_out, wo, 8)):
            for k in range(nk):
                w = it % 2
                it += 1
                S.dma('sp', wstg[w][:], src[l, k * 128:(k + 1) * 128, :], [], [B_wstg[w]], B_wstg[w])
                S.run('dve' if it % 2 else 'act', [B_wstg[w]], [B_wd],
                      (lambda e, w=w, dstt=dstt, k=k: e.tensor_copy(out=dstt[:, k, :], in_=wstg[w][:])) if it % 2 else
                      (lambda e, w=w, dstt=dstt, k=k: e.activation(out=dstt[:, k, :], in_=wstg[w][:], func=AF.Copy)))
        oin = [sb(f"oin{i}", [128, 8, TT], BF16) for i in range(2)]; B_oin = [Buf(f"oin{i}") for i in range(2)]
        gin = [sb(f"gin{i}", [128, 16, TT], BF16) for i in range(2)]; B_gin = [Buf(f"gin{i}") for i in range(2)]
        uT = [sb(f"uT{i}", [128, 8, TT], BF16) for i in range(2)]; B_uT = [Buf(f"uT{i}") for i in range(2)]
        t1 = [sb(f"t1{i}", [128, TT], F32) for i in range(2)]; B_t1 = [Buf(f"t1{i}") for i in range(2)]
        t2 = [sb(f"t2{i}", [128, TT], F32) for i in range(2)]; B_t2 = [Buf(f"t2{i}") for i in range(2)]
        xo = [sb(f"xo{i}", [128, D], F32) for i in range(2)]; B_xo = [Buf(f"xo{i}") for i in range(2)]
        yv = [sb(f"yv{i}", [128, D], F32) for i in range(2)]; B_yv = [Buf(f"yv{i}") for i in range(2)]
        gtb = sb("gtb", [128, D], F32); B_gtb = Buf("gtb")
        gfb = sb("gfb", [128, D], F32); B_gfb = Buf("gfb")
        gf1 = sb("gf1", [1, D], F32); B_gf1 = Buf("gf1")
        junk2 = sb("junk2", [128, D], F32); B_junk2 = Buf("junk2")
        ss2 = sb("ss2", [128, 2], F32); B_ss2 = Buf("ss2")
        pya = [ps(f"pya{i}", [128, 512]) for i in range(2)]; B_pya = [Buf(f"pya{i}") for i in range(2)]
        pyb = [ps(f"pyb{i}", [128, 512]) for i in range(2)]; B_pyb = [Buf(f"pyb{i}") for i in range(2)]
        py = [ps(f"py{i}", [128, 1024]) for i in range(2)]; B_py = [Buf(f"py{i}") for i in range(2)]
        if l == DEPTH - 1:
            S.dma('sp', gf1[:], g_final.rearrange("(a d) -> a d", a=1), [], [B_gf1], B_gf1)
            for hf in range(2):
                S.run('pe', [B_gf1, Bc2], [B_py[0]],
                      lambda e, hf=hf: e.matmul(py[0][:, hf * 512:(hf + 1) * 512], lhsT=onesf[0:1, :],
                                                rhs=gf1[0:1, hf * 512:(hf + 1) * 512], start=True, stop=True))
            S.run('act', [B_py[0]], [B_gfb], lambda e: e.activation(out=gfb[:], in_=py[0][:, :], func=AF.Copy))
        cur_seg = -1
        ci = 0
        si4 = 0
        for tile in range(NT):
            seg = tile // 8
            t0 = tile * TT
            i2 = tile % 2
            if seg != cur_seg:
                cur_seg = seg
                S.dma('sp', gtb[:], modbc[l, seg, 2], [B_modbc], [B_gtb], B_gtb)
            def d_loads(tl_):
                j2 = tl_ % 2
                tt0 = tl_ * TT
                S.dma('sp', oin[j2][:, 0:4, :], om[:, tt0:tt0 + TT].rearrange("(k p) t -> p k t", p=128), [B_scr], [B_oin[j2]],
                      B_oin[j2])
                S.dma('sp', oin[j2][:, 4:8, :], od[:, tt0:tt0 + TT].rearrange("(k p) t -> p k t", p=128), [B_scr], [B_oin[j2]],
                      B_oin[j2], accum=True)
                S.dma('sp', gin[j2][:], gab[:, tt0:tt0 + TT].rearrange("(k p) t -> p k t", p=128), [B_scr], [B_gin[j2]], B_gin[j2])
            if tile == 0:
                d_loads(0)
            if tile + 1 < NT:
                d_loads(tile + 1)
            for j in range(8):
                c2 = ci % 2
                ci += 1

                def f(e, j=j, c2=c2):
                    for k in range(4):
                        ins = e.matmul(pya[c2][:, :], lhsT=wpa[:, k, j * 128:(j + 1) * 128], rhs=oin[i2][:, k, :],
                                       start=(k == 0), stop=(k == 3))
                    return ins
                S.run('pe', [B_wd, B_oin[i2]], [B_pya[c2]], f)

                def f(e, j=j, c2=c2):
                    for k in range(4):
                        ins = e.matmul(pyb[c2][:, :], lhsT=wpb[:, k, j * 128:(j + 1) * 128], rhs=oin[i2][:, 4 + k, :],
                                       start=(k == 0), stop=(k == 3))
                    return ins
                S.run('pe', [B_wd, B_oin[i2]], [B_pyb[c2]], f)
                S.run('dve', [B_pya[c2], B_gin[i2]], [B_t1[c2]],
                      lambda e, j=j, c2=c2: e.tensor_tensor(out=t1[c2][:], in0=pya[c2][:, :], in1=gin[i2][:, j, :], op=ALU.mult))
                S.run('dve', [B_pyb[c2], B_gin[i2]], [B_t2[c2]],
                      lambda e, j=j, c2=c2: e.tensor_tensor(out=t2[c2][:], in0=pyb[c2][:, :], in1=gin[i2][:, 8 + j, :],
                                                            op=ALU.mult))
                S.run('pool', [B_t1[c2], B_t2[c2]], [B_uT[i2]],
                      lambda e, j=j, c2=c2: e.tensor_tensor(out=uT[i2][:, j, :], in0=t1[c2][:], in1=t2[c2][:], op=ALU.add))
            for s4 in range(4):
                x2 = si4 % 2
                si4 += 1
                r0 = t0 + s4 * 128
                S.dma('sp', xo[x2][:], x_src[r0:r0 + 128, :], [], [B_xo[x2]], B_xo[x2])

                def f(e, s4=s4, x2=x2):
                    for hf in range(2):
                        for k in range(8):
                            ins = e.matmul(py[x2][:, hf * 512:(hf + 1) * 512], lhsT=uT[i2][:, k, s4 * 128:(s4 + 1) * 128],
                                           rhs=wo[:, k, hf * 512:(hf + 1) * 512], start=(k == 0), stop=(k == 7))
                    return ins
                S.run('pe', [B_uT[i2], B_wd], [B_py[x2]], f)
                S.run('dve', [B_py[x2], B_gtb], [B_yv[x2]],
                      lambda e, x2=x2: e.tensor_tensor(out=yv[x2][:], in0=py[x2][:, :], in1=gtb[:], op=ALU.mult))
                S.run('pool', [B_yv[x2], B_xo[x2]], [B_xo[x2]],
                      lambda e, x2=x2: e.tensor_tensor(out=xo[x2][:], in0=xo[x2][:], in1=yv[x2][:], op=ALU.add))
                if l < DEPTH - 1:
                    S.dma('sp', xres[r0:r0 + 128, :], xo[x2][:], [B_xo[x2]], [B_scr], B_xo[x2], accum=True)
                else:
                    S.run('act', [B_xo[x2]], [B_junk2, B_ss2],
                          lambda e, x2=x2: e.activation(out=junk2[:], in_=xo[x2][:], func=AF.Square, accum_out=ss2[:, 0:1]))

                    S.run('dve', [B_ss2], [B_ss2], lambda e: e.tensor_scalar(
                        out=ss2[:, 1:2], in0=ss2[:, 0:1], scalar1=1.0 / D, scalar2=EPS, op0=ALU.mult, op1=ALU.add))
                    S.run('act', [B_ss2], [B_ss2], lambda e: e.activation(out=ss2[:, 1:2], in_=ss2[:, 1:2], func=AF.Sqrt))
                    S.run('dve', [B_ss2], [B_ss2], lambda e: e.reciprocal(out=ss2[:, 1:2], in_=ss2[:, 1:2]))
                    S.run('dve', [B_xo[x2], B_ss2, B_gfb], [B_yv[x2]],
                          lambda e, x2=x2: e.scalar_tensor_tensor(out=yv[x2][:], in0=xo[x2][:], scalar=ss2[:, 1:2], in1=gfb[:],
                                                                  op0=ALU.mult, op1=ALU.mult))
                    S.dma('sp', y_out[r0:r0 + 128, :], yv[x2][:], [B_yv[x2]], [B_scr], B_yv[x2], accum=True)
        S.barrier()
        es_ph.close()
        if l == 0 and stop == "D":
            es.close()
            return nc

    es.close()
    return nc


_CACHE = {}


def _rope_tables(core):
    s, c = core // 4, core % 4
    tabs = np.zeros((NT, 4, 128, TT), np.float32)
    p = np.arange(128)
    j2 = p % 32
    half2 = (p % 64) // 32
    inv2 = np.power(np.float32(10000.0), -(2.0 * np.arange(32, dtype=np.float32)) / np.float32(64)).astype(np.float32)
    inv1 = np.power(np.float32(10000.0), -(2.0 * np.arange(16, dtype=np.float32)) / np.float32(32)).astype(np.float32)
    for tile in range(NT):
        seg = tile // 8
        tl = (tile % 8) * TT
        pos = (np.arange(TT) + tl + (c * SEG if seg == 1 else 0)).astype(np.float32)
        ang2 = (pos[None, :] * inv2[j2][:, None]).astype(np.float32)
        tabs[tile, 0] = np.cos(ang2)
        tabs[tile, 1] = np.sin(ang2) * np.where(half2 == 0, 1.0, -1.0)[:, None]
        ang1 = (pos[None, :] * inv1[:, None]).astype(np.float32)
        cos1, sin1 = np.cos(ang1), np.sin(ang1)
        cc = np.concatenate([cos1, cos1], 0)
        ss = np.concatenate([-sin1, sin1], 0)
        tabs[tile, 2, 0:32] = cc
        tabs[tile, 2, 64:96] = cc
        tabs[tile, 3, 32:64] = ss
        tabs[tile, 3, 96:128] = ss
    return tabs


def kernel(x_prompt, x_sample, c_prompt, c_sample, w_ada, b_ada, g_norm, w_in, b_gate, g_cq, w_uq,
           g_ckv, w_ukv, w_pa, w_pb, w_out, g_final, _dbg=(), _stop=None):
    f32 = np.float32
    key = ("nc", tuple(_dbg), _stop)
    if key not in _CACHE:
        _CACHE[key] = build_program(dbg=tuple(_dbg), stop=_stop)
    nc = _CACHE[key]
    kk = np.arange(128)[:, None]
    qq = np.arange(128)[None, :]
    mA = np.where(kk >= qq, 0.0, -30000.0).astype(f32)
    mB = np.where(kk <= qq, 0.0, -30000.0).astype(f32)
    maskb = np.concatenate([mA, mB, mA, mB], 1).astype(ml_dtypes.bfloat16)
    ident = np.eye(128, dtype=f32).astype(ml_dtypes.bfloat16)
    sel = np.zeros((2, 2, 128), f32)
    sel[0, 0] = 1.0
    sel[1, 1] = 1.0
    shared = dict(w_ada=np.asarray(w_ada, f32), b_ada=np.asarray(b_ada, f32), g_norm=np.asarray(g_norm, f32),
                  w_in=np.asarray(w_in, f32),
                  b_gate=np.ascontiguousarray(np.asarray(b_gate, f32).reshape(DEPTH, 16, 128).transpose(0, 2, 1)),
                  g_cq=np.ascontiguousarray(np.asarray(g_cq, f32).reshape(DEPTH, 3, 128).transpose(0, 2, 1)),
                  w_uq=np.asarray(w_uq, f32),
                  g_ckv=np.ascontiguousarray(np.asarray(g_ckv, f32).reshape(DEPTH, 2, 128).transpose(0, 2, 1)), w_ukv=np.asarray(w_ukv, f32),
                  w_pa=np.asarray(w_pa, f32), w_pb=np.asarray(w_pb, f32), w_out=np.asarray(w_out, f32),
                  g_final=np.asarray(g_final, f32), maskb=maskb, ident=ident, sel=sel)
    x_prompt = np.asarray(x_prompt, f32)
    x_sample = np.asarray(x_sample, f32)
    c_prompt = np.asarray(c_prompt, f32)
    c_sample = np.asarray(c_sample, f32)
    in_maps = []
    for core in range(NCORES):
        s, c = core // 4, core % 4
        x_own = np.concatenate([x_prompt[core], x_sample[s, c * SEG:(c + 1) * SEG]], 0)
        cc = np.stack([c_prompt[core], c_sample[s]], 0)
        cT = np.ascontiguousarray(cc.reshape(2, 8, 128).transpose(2, 1, 0))
        flags = np.zeros((128, 2), f32)
        flags[:, 0] = 1.0 if c > 0 else 0.0
        flags[:, 1] = 1.0 if c < 3 else 0.0
        m = dict(shared)
        m.update(x_own=np.ascontiguousarray(x_own), cT=cT, tabs=_rope_tables(core), flags=flags)
        in_maps.append(m)
    res = run_bass_kernel_spmd(nc, in_maps, core_ids=list(range(NCORES)))
    if _dbg or _stop:
        return res
    y_prompt = np.stack([res.results[i]["y"][0:SEG] for i in range(NCORES)], 0)
    y_sample = np.stack([np.concatenate([res.results[4 * s + c]["y"][SEG:T] for c in range(4)], 0) for s in range(2)], 0)
    return (y_prompt.astype(f32), y_sample.astype(f32))
```
